# Optimizing a Trainium2 kernel written in Bass

```python
import jax, jax.numpy as jnp
from jax import lax
import numpy as np

D_MODEL = 2048
BATCH = 8
SEQ = 2048
DEPTH = 1

MEM_LEN = 256
MLA_HEADS = 8
Q_LORA = 512
KV_LORA = 512
QK_NOPE = 128
QK_ROPE = 64
V_HEAD = 128
ROPE_THETA = 10000.0
MLA_OUT = MLA_HEADS * V_HEAD
CONV_WIDTH = 1024
CONV_K = 3
X_HEADS = 4
X_HEAD_DIM = 256
X_WIDTH = X_HEADS * X_HEAD_DIM
D_FF = 5632
FFN_CONV_K = 3
N_BRANCH = 3
Q_BLOCK = 128
EPS = 1e-6
SPLITS = (Q_LORA, KV_LORA, QK_ROPE, CONV_WIDTH, CONV_WIDTH, CONV_WIDTH, X_WIDTH, N_BRANCH * D_MODEL)
IN_COLS = Q_LORA + KV_LORA + QK_ROPE + 3 * CONV_WIDTH + X_WIDTH + N_BRANCH * D_MODEL

kernel_name = "hybrid_mla_shortconv_memxattn_convffn_encoder"


def rmsnorm(x, g):
    xf = x.astype(jnp.float32)
    y = xf * lax.rsqrt(jnp.mean(xf * xf, axis=-1, keepdims=True) + EPS)
    return (y * g.astype(jnp.float32)).astype(x.dtype)


def rope_angles(positions):
    inv_freq = jnp.power(ROPE_THETA, -jnp.arange(0, QK_ROPE, 2, dtype=jnp.float32) / QK_ROPE)
    ang = positions.astype(jnp.float32)[..., None] * inv_freq
    return jnp.cos(ang), jnp.sin(ang)


def apply_rope(t, cos, sin):
    tf = t.astype(jnp.float32)
    t1, t2 = jnp.split(tf, 2, axis=-1)
    out = jnp.concatenate([t1 * cos - t2 * sin, t1 * sin + t2 * cos], axis=-1)
    return out.astype(t.dtype)


def dwconv_centred(u, w):
    C = u.shape[-1]
    K = w.shape[0]
    return lax.conv_general_dilated(
        u, w[:, None, :].astype(u.dtype), window_strides=(1,), padding=[(K // 2, K // 2)],
        dimension_numbers=("NWC", "WIO", "NWC"), feature_group_count=C)


def mla_attention(q_nope, q_rope, k_nope, k_rope, v):
    B, S, H, _ = q_nope.shape
    nb = S // Q_BLOCK
    scale = (QK_NOPE + QK_ROPE) ** -0.5

    def blocks(t):
        return jnp.swapaxes(t.reshape((B, nb, Q_BLOCK) + t.shape[2:]), 0, 1)

    def one_block(qs):
        qn, qr = qs
        s = (jnp.einsum("bqhd,bkhd->bhqk", qn, k_nope)
             + jnp.einsum("bqhr,bkr->bhqk", qr, k_rope))
        p = jax.nn.softmax(s.astype(jnp.float32) * scale, axis=-1).astype(v.dtype)
        return jnp.einsum("bhqk,bkhd->bqhd", p, v)

    o = lax.map(one_block, (blocks(q_nope), blocks(q_rope)))
    return jnp.swapaxes(o, 0, 1).reshape(B, S, H * V_HEAD)


def memory_attention(q, mem_n, w_mem_kv):
    B, S, _ = q.shape
    M = mem_n.shape[1]
    kv = (mem_n @ w_mem_kv).reshape(B, M, 2, X_HEADS, X_HEAD_DIM)
    k, v = kv[:, :, 0], kv[:, :, 1]
    qh = q.reshape(B, S, X_HEADS, X_HEAD_DIM)
    s = jnp.einsum("bshd,bmhd->bhsm", qh, k).astype(jnp.float32) * (X_HEAD_DIM ** -0.5)
    p = jax.nn.softmax(s, axis=-1).astype(v.dtype)
    return jnp.einsum("bhsm,bmhd->bshd", p, v).reshape(B, S, X_WIDTH)


def hybrid_layer(x, mem, cos, sin, mix_norm, w_in, q_norm, w_uq, kv_norm, w_ukv, w_o_mla,
                 conv_w, w_out_conv, mem_norm, w_mem_kv, w_o_mem, gate_bias, w_o,
                 ffn_norm, w_up, ffn_conv_w, w_down):
    B, S, D = x.shape
    h = rmsnorm(x, mix_norm)
    z = h @ w_in
    idx = np.cumsum(np.array(SPLITS))[:-1].tolist()
    c_q, c_kv, k_r, cv, cb, cc, q_x, g = jnp.split(z, idx, axis=-1)

    q = (rmsnorm(c_q, q_norm) @ w_uq).reshape(B, S, MLA_HEADS, QK_NOPE + QK_ROPE)
    q_nope, q_rope = q[..., :QK_NOPE], q[..., QK_NOPE:]
    q_rope = apply_rope(q_rope, cos[:, :, None, :], sin[:, :, None, :])
    kv = (rmsnorm(c_kv, kv_norm) @ w_ukv).reshape(B, S, MLA_HEADS, QK_NOPE + V_HEAD)
    k_nope, v = kv[..., :QK_NOPE], kv[..., QK_NOPE:]
    k_rope = apply_rope(k_r, cos, sin)
    y_mla = mla_attention(q_nope, q_rope, k_nope, k_rope, v) @ w_o_mla

    y_conv = (cb * dwconv_centred(cc * cv, conv_w)) @ w_out_conv

    y_mem = memory_attention(q_x, rmsnorm(mem, mem_norm), w_mem_kv) @ w_o_mem

    gates = jax.nn.sigmoid((g + gate_bias).astype(jnp.float32)).astype(x.dtype)
    gates = gates.reshape(B, S, N_BRANCH, D)
    merged = gates[:, :, 0] * y_mla + gates[:, :, 1] * y_conv + gates[:, :, 2] * y_mem
    x = x + merged @ w_o

    u = dwconv_centred(rmsnorm(x, ffn_norm) @ w_up, ffn_conv_w)
    a, b = jnp.split(u, 2, axis=-1)
    x = x + (jax.nn.silu(a) * b) @ w_down
    return x


def setup_inputs(seed: int = 0) -> dict:
    key = jax.random.key(seed)
    ks = jax.random.split(key, 24)
    L = DEPTH
    f32 = jnp.float32

    def nrm(k, shape, scale):
        return jax.random.normal(k, shape, f32) * scale

    def gain(k, shape):
        return 1.0 + 0.01 * jax.random.normal(k, shape, f32)

    return {
        "x": nrm(ks[0], (BATCH, SEQ, D_MODEL), 1.0),
        "mem": nrm(ks[1], (BATCH, MEM_LEN, D_MODEL), 1.0),
        "positions": jnp.broadcast_to(jnp.arange(SEQ, dtype=jnp.int32), (BATCH, SEQ)),
        "mix_norm": gain(ks[2], (L, D_MODEL)),
        "w_in": nrm(ks[3], (L, D_MODEL, IN_COLS), D_MODEL ** -0.5),
        "q_norm": gain(ks[4], (L, Q_LORA)),
        "w_uq": nrm(ks[5], (L, Q_LORA, MLA_HEADS * (QK_NOPE + QK_ROPE)), Q_LORA ** -0.5),
        "kv_norm": gain(ks[6], (L, KV_LORA)),
        "w_ukv": nrm(ks[7], (L, KV_LORA, MLA_HEADS * (QK_NOPE + V_HEAD)), KV_LORA ** -0.5),
        "w_o_mla": nrm(ks[8], (L, MLA_OUT, D_MODEL), MLA_OUT ** -0.5),
        "conv_w": nrm(ks[9], (L, CONV_K, CONV_WIDTH), CONV_K ** -0.5),
        "w_out_conv": nrm(ks[10], (L, CONV_WIDTH, D_MODEL), CONV_WIDTH ** -0.5),
        "mem_norm": gain(ks[11], (L, D_MODEL)),
        "w_mem_kv": nrm(ks[12], (L, D_MODEL, 2 * X_WIDTH), D_MODEL ** -0.5),
        "w_o_mem": nrm(ks[13], (L, X_WIDTH, D_MODEL), X_WIDTH ** -0.5),
        "gate_bias": nrm(ks[14], (L, N_BRANCH * D_MODEL), 0.01),
        "w_o": nrm(ks[15], (L, D_MODEL, D_MODEL), D_MODEL ** -0.5),
        "ffn_norm": gain(ks[16], (L, D_MODEL)),
        "w_up": nrm(ks[17], (L, D_MODEL, 2 * D_FF), D_MODEL ** -0.5),
        "ffn_conv_w": nrm(ks[18], (L, FFN_CONV_K, 2 * D_FF), FFN_CONV_K ** -0.5),
        "w_down": nrm(ks[19], (L, D_FF, D_MODEL), D_FF ** -0.5),
        "final_norm": gain(ks[20], (D_MODEL,)),
    }


def reference(x, mem, positions, mix_norm, w_in, q_norm, w_uq, kv_norm, w_ukv, w_o_mla,
              conv_w, w_out_conv, mem_norm, w_mem_kv, w_o_mem, gate_bias, w_o,
              ffn_norm, w_up, ffn_conv_w, w_down, final_norm):
    cos, sin = rope_angles(positions)
    for l in range(DEPTH):
        x = hybrid_layer(x, mem, cos, sin, mix_norm[l], w_in[l], q_norm[l], w_uq[l], kv_norm[l],
                         w_ukv[l], w_o_mla[l], conv_w[l], w_out_conv[l], mem_norm[l],
                         w_mem_kv[l], w_o_mem[l], gate_bias[l], w_o[l], ffn_norm[l], w_up[l],
                         ffn_conv_w[l], w_down[l])
    return rmsnorm(x, final_norm)
```

```python
import math
from contextlib import ExitStack
import numpy as np
import concourse.bass as bass
import concourse.mybir as mybir
from concourse.bass_utils import run_bass_kernel_spmd

F32 = mybir.dt.float32
BF16 = mybir.dt.bfloat16
I32 = mybir.dt.int32
AF = mybir.ActivationFunctionType
ALU = mybir.AluOpType

S = 2048
D = 2048
TT = 512
NT = S // TT
MEM = 256
DFF = 5632
NFC = DFF // 128
IN_COLS = 11328
C_CQ, C_CKV, C_KR, C_CV, C_CB, C_CC, C_QX, C_G = 0, 512, 1024, 1088, 2112, 3136, 4160, 5184
EPS = 1e-6
P_MIX, P_FFN, P_MEMN, P_QN, P_KVN, P_GB, P_CW, P_FCW = 0, 16, 32, 48, 52, 56, 104, 128
NPRM = 392
NS = 4
SLOT = 4096


class Res:
    __slots__ = ("w", "r")

    def __init__(self):
        self.w = {}
        self.r = {}


def _mrg(d, tok):
    k, v = tok
    if d.get(k, 0) < v:
        d[k] = v


class Prog:
    ENG = ("pe", "act", "dve", "pool", "sp")

    def __init__(self, nc):
        self.nc = nc
        self.q = {e: [] for e in self.ENG}
        self.cnt = {e: 0 for e in self.ENG}
        self.dsem = []
        self.seen = {e: {} for e in self.ENG}

    def _wait(self, eng, key, val):
        if key == eng and eng == "pe":
            return
        if self.seen[eng].get(key, 0) >= val:
            return
        self.seen[eng][key] = val
        self.q[eng].append(("wait", key, val))

    def _deps(self, eng, reads, writes):
        for r in reads:
            for k, v in r.w.items():
                self._wait(eng, k, v)
        for w in writes:
            for k, v in w.w.items():
                self._wait(eng, k, v)
            for k, v in w.r.items():
                self._wait(eng, k, v)

    def _note(self, tok, reads, writes):
        for r in reads:
            _mrg(r.r, tok)
        for w in writes:
            w.w = {tok[0]: tok[1]}
            w.r = {}

    def op(self, eng, fn, reads=(), writes=()):
        self._deps(eng, reads, writes)
        self.cnt[eng] += 1
        tok = (eng, self.cnt[eng])
        self.q[eng].append(("op", fn, True))
        self._note(tok, reads, writes)
        return tok

    def group(self, eng, fns, reads=(), writes=()):
        self._deps(eng, reads, writes)
        n = len(fns)
        for i, fn in enumerate(fns):
            self.q[eng].append(("op", fn, i == n - 1))
        self.cnt[eng] += 1
        tok = (eng, self.cnt[eng])
        self._note(tok, reads, writes)
        return tok

    def new_dsem(self):
        self.dsem.append(0)
        return len(self.dsem) - 1

    def dma(self, eng, fn, sem, reads=(), writes=()):
        self._deps(eng, reads, writes)
        self.dsem[sem] += 16
        tok = (("dma", sem), self.dsem[sem])
        self.q[eng].append(("dma", fn, sem))
        self._note(tok, reads, writes)
        return tok

    def wait_all(self, eng, toks):
        for t in toks:
            self._wait(eng, t[0], t[1])

    def run(self, stack):
        nc = self.nc
        semh = {}
        for e in self.ENG:
            semh[e] = stack.enter_context(nc.semaphore("s_" + e))
        for i in range(len(self.dsem)):
            semh[("dma", i)] = stack.enter_context(nc.semaphore("d_%d" % i))
        block = stack.enter_context(nc.Block())

        def replay(name):
            def f(engine):
                for item in self.q[name]:
                    if item[0] == "wait":
                        engine.wait_ge(semh[item[1]], item[2])
                    elif item[0] == "op":
                        ins = item[1](engine)
                        if item[2]:
                            ins.then_inc(semh[name], 1)
                    else:
                        ins = item[1](engine)
                        ins.then_inc(semh[("dma", item[2])], 16)
            return f

        block.tensor(replay("pe"))
        block.scalar(replay("act"))
        block.vector(replay("dve"))
        block.gpsimd(replay("pool"))
        block.sync(replay("sp"))


def build_nc(dbg=None):
    import os
    dbg = dbg or {}
    NT_A = dbg.get('A', NT); NT_1 = dbg.get('S1', NT); NT_2 = dbg.get('S2', NT); DO_M = dbg.get('M', 1)
    nc = bass.Bass("TRN2", target_bir_lowering=False)

    def din(name, shape, dt=F32):
        return nc.dram_tensor(name, shape, dt, kind="ExternalInput").ap()

    x = din("x", [S, D])
    mem = din("mem", [MEM, D])
    pos = din("positions", [1, S], I32)
    w_in = din("w_in", [D, IN_COLS])
    w_uq = din("w_uq", [512, 1536])
    w_ukv = din("w_ukv", [512, 2048])
    w_o_mla = din("w_o_mla", [1024, D])
    w_out_conv = din("w_out_conv", [1024, D])
    w_mem_kv = din("w_mem_kv", [D, 2048])
    w_o_mem = din("w_o_mem", [1024, D])
    w_o = din("w_o", [D, D])
    w_up = din("w_up", [D, 2 * DFF])
    w_down = din("w_down", [DFF, D])
    prm = din("prm", [NPRM, 128])
    fin = din("final_norm", [1, D])
    cst = din("cst", [128, 130])
    y = nc.dram_tensor("y", [S, D], F32, kind="ExternalOutput").ap()
    x1s = nc.dram_tensor("x1s", [S, D], F32, kind="ExternalOutput").ap()

    with ExitStack() as st:
        def sb(name, shape, dt):
            return st.enter_context(nc.sbuf_tensor(name, shape, dt))

        P = Prog(nc)
        slots = [sb("slot%d" % i, [128, SLOT], BF16) for i in range(NS)]
        R_slot = [Res() for _ in range(NS)]
        slot_sem = [P.new_dsem() for _ in range(NS)]
        hT = sb("hT", [128, 16, TT], BF16); R_hT = Res()
        hTh = sb("hTh", [128, 16, 4], BF16); R_hTh = Res()
        big = sb("big", [128, 34816], BF16)
        KT = big[:, 0:16384].rearrange("p (h s) -> p h s", h=8)
        Vt = big[:, 16384:32768].rearrange("p (c n) -> p c n", c=16)
        krT = big[:, 32768:34816]
        actT = big[:, 0:NFC * TT].rearrange("p (c n) -> p c n", c=NFC)
        gfin = big[:, 24576:24576 + 4096].bitcast(F32)
        R_KT = Res(); R_V = Res(); R_kr = Res(); R_gfin = Res()
        R_act = [Res() for _ in range(NFC)]
        KmT = sb("KmT", [128, 8, MEM], BF16); R_Km = Res()
        Vm = sb("Vm", [128, 2, 1024], BF16); R_Vm = Res()
        U = sb("U", [128, 12288], F32)
        xt = U[:, 0:8192].rearrange("p (j d) -> p j d", j=4)
        attnT = U[:, 0:2048].bitcast(BF16).rearrange("p (c n) -> p c n", c=8)
        convT = U[:, 2048:4096].bitcast(BF16).rearrange("p (c n) -> p c n", c=8)
        memT = U[:, 4096:6144].bitcast(BF16).rearrange("p (c n) -> p c n", c=8)
        cqraw = U[:, 6144:8192].rearrange("p (c n) -> p c n", c=4)
        R_attn = Res(); R_conv = Res(); R_memo = Res(); R_cqraw = Res()
        R_xt = [R_attn, R_conv, R_memo, R_cqraw]
        mg = U[:, 8192:12288]
        mergedT = mg.bitcast(BF16).rearrange("p (c n) -> p c n", c=16)
        xn = mg.bitcast(BF16).rearrange("p (j d) -> p j d", j=4)
        cqn = mg[:, 0:1024].bitcast(BF16).rearrange("p (c n) -> p c n", c=4)
        cosT = mg[0:64, 1024:1536]
        sinT = mg[0:64, 1536:2048]
        rtA = mg[0:64, 2048:2560]
        rtB = mg[0:64, 2560:3072]
        posi = mg[0:64, 3072:3584].bitcast(I32)
        R_cqn = Res(); R_cs = Res(); R_rt = Res()
        R_mg = [R_cqn, R_cs, R_rt]
        qq = sb("qq", [128, 8, TT], BF16)
        qn = qq[:, 0:4, :]
        qr = qq[:, 4:8, :]
        R_qn = Res(); R_qr = Res()
        xh = qq[:, :, :].rearrange("p a b -> p (a b)").bitcast(F32).rearrange("p (j d) -> p j d", j=1)
        R_xh = [R_qn, R_qr]
        junk = sb("junk", [128, 2048], BF16); R_junk = Res()
        T = [sb("T%d" % i, [128, 514], F32) for i in range(6)]
        R_T = [Res() for _ in range(6)]
        prow = T[0][:, 0:512].rearrange("p (b c) -> p b c", b=4)
        R_prow = R_T[0]
        PT = [sb("PT%d" % i, [128, TT], BF16) for i in range(3)]
        R_PT = [Res() for _ in range(3)]
        wrot = sb("wrot", [128, 16, 64], BF16); R_wrot = Res()
        pT = sb("pT", [128, NPRM], F32); R_pT = Res()
        idf = sb("idf", [128, 130], F32); R_idf = Res()
        idb = sb("idb", [128, 128], BF16); R_idb = Res()
        ones = sb("ones", [128, 128], BF16); R_ones = Res()
        small = sb("small", [128, 16], F32); R_small = Res()
        ucar = sb("ucar", [128, 8], F32); R_ucar = Res()
        uh = sb("uh", [128, 8, 4], F32); R_uh = Res()
        fcar = sb("fcar", [128, 2 * NFC], F32); R_fcar = Res()
        fh = sb("fh", [128, 2 * NFC, 4], F32); R_fh = Res()
        zero1 = sb("zero1", [128, 4], F32)

        NB = 6
        ps = [st.enter_context(nc.psum_tensor("ps%d" % i, [128, 512], F32)) for i in range(NB)]
        psT = [st.enter_context(nc.psum_tensor("psT%d" % i, [128, 1024], BF16)) for i in range(2)]
        R_ps = [Res() for _ in range(NB)]
        R_psT = [Res(), Res()]
        held = [False] * NB
        rot = [0]

        def palloc():
            for _ in range(NB):
                i = rot[0] % NB
                rot[0] += 1
                if not held[i]:
                    held[i] = True
                    return i
            raise RuntimeError("psum exhausted")

        def pfree(i):
            held[i] = False

        sem_x = P.new_dsem(); sem_misc = P.new_dsem(); sem_st = P.new_dsem(); sem_pos = sem_misc
        sem_xh = P.new_dsem(); sem_xr = P.new_dsem(); sem_idf = P.new_dsem(); sem_prm = P.new_dsem()
        slot_ctr = [0]

        def wload(w, r0, kcn, c0, ncols):
            i = slot_ctr[0] % NS
            slot_ctr[0] += 1
            view = slots[i][:, 0:kcn * ncols].rearrange("p (k n) -> p k n", k=kcn)
            src = w[r0:r0 + kcn * 128, c0:c0 + ncols].rearrange("(k p) n -> p k n", p=128)
            P.dma("pool", lambda e, view=view, src=src: e.dma_start(out=view, in_=src), slot_sem[i],
                  writes=[R_slot[i]])
            return view, R_slot[i]

        def MM(o, l, r, s, t):
            return lambda e: e.matmul(o, l, r, start=s, stop=t)

        P.dma("sp", lambda e: e.dma_start(out=idf[:], in_=cst), sem_idf, writes=[R_idf])
        for b in range(4):
            r0 = b * 128
            n = min(128, NPRM - r0)
            P.dma("sp", lambda e, b=b, r0=r0, n=n: e.dma_start(out=prow[0:n, b, :], in_=prm[r0:r0 + n, :]),
                  sem_prm, writes=[R_prow])
        P.op("dve", lambda e: e.memset(ones[:], 1.0), writes=[R_ones])
        P.op("dve", lambda e: e.memset(small[:], 0.0), writes=[R_small])
        P.op("dve", lambda e: e.memset(small[:, 10:11], EPS), writes=[R_small])
        P.op("dve", lambda e: e.memset(small[:, 11:12], math.pi / 2), writes=[R_small])
        P.op("dve", lambda e: e.memset(zero1[:], 0.0))
        P.op("dve", lambda e: e.memset(ucar[:], 0.0), writes=[R_ucar])
        P.op("dve", lambda e: e.memset(fcar[:], 0.0), writes=[R_fcar])
        P.op("dve", lambda e: e.memset(uh[:], 0.0), writes=[R_uh])
        P.op("dve", lambda e: e.memset(fh[:], 0.0), writes=[R_fh])
        P.op("dve", lambda e: e.tensor_copy(out=idb[:], in_=idf[:, 0:128]), reads=[R_idf], writes=[R_idb])
        for b in range(4):
            n = min(128, NPRM - b * 128)
            bk = palloc()
            P.group("pe", [lambda e, b=b, n=n, bk=bk: e.transpose(ps[bk][:, 0:n], prow[0:n, b, :], idf[0:n, 0:n])],
                    reads=[R_prow, R_idf], writes=[R_ps[bk]])
            P.op("dve", lambda e, b=b, n=n, bk=bk: e.tensor_copy(out=pT[:, b * 128:b * 128 + n], in_=ps[bk][:, 0:n]),
                 reads=[R_ps[bk]], writes=[R_pT])
            pfree(bk)
        eps_ap = small[:, 10:11]
        halfpi = small[:, 11:12]
        invf = idf[0:64, 128:129]

        def norm_T(src, R_src, nj, gbase, dst, R_dst, ncopy):
            for j in range(nj):
                P.op("act", lambda e, j=j: e.activation(out=junk[:], in_=src[:, j, :], func=AF.Square,
                                                         accum_out=small[:, j:j + 1]),
                     reads=R_src, writes=[R_junk, R_small])
            P.op("act", lambda e: e.activation(out=small[:, 5:5 + nj], in_=small[:, 0:nj], func=AF.Ln,
                                               scale=1.0 / D, bias=eps_ap), reads=[], writes=[R_small])
            P.op("act", lambda e: e.activation(out=small[:, 5:5 + nj], in_=small[:, 5:5 + nj], func=AF.Exp,
                                               scale=-0.5), writes=[R_small])
            for j in range(nj):
                P.op("dve", lambda e, j=j: e.tensor_scalar(out=xn[:, j, :], in0=src[:, j, :],
                                                            scalar1=small[:, 5 + j:6 + j], scalar2=None,
                                                            op0=ALU.mult),
                     reads=list(R_src) + [R_small], writes=R_mg)
            for kc in range(16):
                hb = kc % 2
                fns = [(lambda e, j=j, kc=kc, hb=hb: e.transpose(psT[hb][:, j * 128:(j + 1) * 128],
                                                                 xn[:, j, kc * 128:(kc + 1) * 128], idb[:]))
                       for j in range(nj)]
                P.group("pe", fns, reads=R_mg + [R_idb], writes=[R_psT[hb]])
                P.op("dve", lambda e, kc=kc, hb=hb: e.tensor_scalar(out=dst[:, kc, 0:ncopy],
                                                                    in0=psT[hb][:, 0:ncopy],
                                                                    scalar1=pT[:, gbase + kc: gbase + kc + 1],
                                                                    scalar2=None, op0=ALU.mult),
                     reads=[R_psT[hb], R_pT], writes=[R_dst])

        def load_xt(srcd, t, sem):
            src = srcd[t * TT:(t + 1) * TT, :].rearrange("(j p) d -> p j d", p=128)
            P.dma("sp", lambda e, src=src: e.dma_start(out=xt, in_=src), sem, writes=R_xt)

        def load_halo(srcd):
            P.op("dve", lambda e: e.memset(xh, 0.0), writes=R_xh)
            for i in range(3):
                P.dma("sp", lambda e, i=i: e.dma_start(out=xh[i:i + 1, 0, :], in_=srcd[(i + 1) * TT:(i + 1) * TT + 1, :]),
                      sem_xh, writes=R_xh)

        def rope_tables(t):
            RL = dbg.get('RL', 9)
            P.dma("pool", lambda e: e.dma_start(out=posi, in_=pos[0:1, t * TT:(t + 1) * TT].partition_broadcast(64)),
                  sem_pos, writes=[R_rt])
            if RL < 1:
                return
            P.op("dve", lambda e: e.tensor_copy(out=rtA, in_=posi), reads=[R_rt], writes=[R_rt])
            P.op("dve", lambda e: e.tensor_scalar(out=rtA, in0=rtA, scalar1=invf, scalar2=None, op0=ALU.mult),
                 reads=[R_rt, R_idf], writes=[R_rt])
            P.op("dve", lambda e: e.tensor_scalar(out=rtB, in0=rtA, scalar1=1.0 / (2 * math.pi), scalar2=None,
                                                  op0=ALU.mult), reads=[R_rt], writes=[R_rt])
            if RL < 2:
                return
            P.op("dve", lambda e: e.tensor_copy(out=posi, in_=rtB), reads=[R_rt], writes=[R_rt])
            P.op("dve", lambda e: e.tensor_copy(out=rtB, in_=posi), reads=[R_rt], writes=[R_rt])
            P.op("dve", lambda e: e.scalar_tensor_tensor(out=rtA, in0=rtB, scalar=-2 * math.pi, in1=rtA,
                                                         op0=ALU.mult, op1=ALU.add), reads=[R_rt], writes=[R_rt])
            P.op("dve", lambda e: e.tensor_scalar(out=rtA, in0=rtA, scalar1=-3.141592, scalar2=3.141592,
                                                  op0=ALU.max, op1=ALU.min), reads=[R_rt], writes=[R_rt])
            if RL < 3:
                return
            P.op("act", lambda e: e.activation(out=sinT, in_=rtA, func=AF.Sin), reads=[R_rt], writes=[R_cs])
            if RL < 4:
                return
            P.op("act", lambda e: e.activation(out=rtB, in_=rtA, func=AF.Abs), reads=[R_rt], writes=[R_rt])
            if RL < 5:
                return
            P.op("act", lambda e: e.activation(out=cosT, in_=rtB, func=AF.Sin, scale=-1.0, bias=halfpi[0:64, :]),
                 reads=[R_rt, R_small], writes=[R_cs])

        def rms_bcast(bk_ss, n, dstT, R_dstT):
            P.op("act", lambda e: e.activation(out=dstT[:, 0:TT], in_=ps[bk_ss][:], func=AF.Ln, scale=1.0 / n,
                                               bias=eps_ap), reads=[R_ps[bk_ss], R_small], writes=[R_dstT])
            P.op("act", lambda e: e.activation(out=dstT[:, 0:TT], in_=dstT[:, 0:TT], func=AF.Exp, scale=-0.5),
                 writes=[R_dstT])

        def latent_norm(wcols, gbase, dst, R_dst):
            LL = dbg.get('LL', 9)
            bss = palloc()
            for g in range(2):
                wv, Rw = wload(w_in, 0, 16, wcols + g * 256, 256)
                if LL < 1:
                    continue
                for cc in range(2):
                    c = g * 2 + cc
                    bk = palloc()
                    P.group("pe", [MM(ps[bk][:], wv[:, kc, cc * 128:(cc + 1) * 128], hT[:, kc, :], kc == 0, kc == 15)
                                   for kc in range(16)], reads=[Rw, R_hT], writes=[R_ps[bk]])
                    LV = dbg.get('LV', 3)
                    if LL >= 2 and (LV & 2):
                        P.op("dve", lambda e, bk=bk, c=c: e.tensor_copy(out=cqraw[:, c, :], in_=ps[bk][:]),
                             reads=[R_ps[bk]], writes=[R_cqraw])
                    if LL >= 2 and (LV & 1):
                        P.op("act", lambda e, c=c: e.activation(out=PT[0][:], in_=cqraw[:, c, :], func=AF.Square),
                             reads=[R_cqraw], writes=[R_PT[0]])
                    pfree(bk)
                    if LL >= 3:
                        P.group("pe", [MM(ps[bss][:], ones[:], PT[0][:], c == 0, c == 3)], reads=[R_PT[0], R_ones],
                                writes=[R_ps[bss]])
            if LL >= 4:
                rms_bcast(bss, 512, T[0], R_T[0])
            pfree(bss)
            if LL < 5:
                return
            for c in range(4):
                P.op("dve", lambda e, c=c: e.scalar_tensor_tensor(out=dst[:, c, :], in0=cqraw[:, c, :],
                                                                   scalar=pT[:, gbase + c:gbase + c + 1],
                                                                   in1=T[0][:, 0:TT], op0=ALU.mult, op1=ALU.mult),
                     reads=[R_cqraw, R_pT, R_T[0]], writes=[R_dst])

        def rope_evac(bk_raw, bk_rot, dst_ap, R_dst):
            P.op("dve", lambda e: e.tensor_tensor(out=T[1][0:64, 0:TT], in0=ps[bk_raw][0:64, :], in1=cosT, op=ALU.mult),
                 reads=[R_ps[bk_raw], R_cs], writes=[R_T[1]])
            P.op("dve", lambda e: e.tensor_tensor(out=T[2][0:64, 0:TT], in0=ps[bk_rot][0:64, :], in1=sinT, op=ALU.mult),
                 reads=[R_ps[bk_rot], R_cs], writes=[R_T[2]])
            P.op("dve", lambda e: e.tensor_tensor(out=dst_ap, in0=T[1][0:64, 0:TT], in1=T[2][0:64, 0:TT], op=ALU.add),
                 reads=[R_T[1], R_T[2]], writes=[R_dst])

        P.dma("sp", lambda e: e.dma_start(out=xt[:, 0:2, :], in_=mem.rearrange("(j p) d -> p j d", p=128)),
              sem_x, writes=R_xt)
        norm_T(xt, R_xt, 2, P_MEMN, hT, R_hT, MEM)
        for g in range(8):
            wv, Rw = wload(w_mem_kv, 0, 16, g * 256, 256)
            if g < 4:
                for cc in range(2):
                    c = g * 2 + cc
                    bk = palloc()
                    P.group("pe", [MM(ps[bk][:, 0:MEM], wv[:, kc, cc * 128:(cc + 1) * 128], hT[:, kc, 0:MEM],
                                      kc == 0, kc == 15) for kc in range(16)], reads=[Rw, R_hT], writes=[R_ps[bk]])
                    P.op("dve", lambda e, bk=bk, c=c: e.tensor_copy(out=KmT[:, c, :], in_=ps[bk][:, 0:MEM]),
                         reads=[R_ps[bk]], writes=[R_Km])
                    pfree(bk)
            else:
                c0 = (g - 4) * 256
                for mc in range(2):
                    bk = palloc()
                    P.group("pe", [MM(ps[bk][:, 0:256], hT[:, kc, mc * 128:(mc + 1) * 128], wv[:, kc, :],
                                      kc == 0, kc == 15) for kc in range(16)], reads=[Rw, R_hT], writes=[R_ps[bk]])
                    P.op("dve", lambda e, bk=bk, mc=mc, c0=c0: e.tensor_copy(out=Vm[:, mc, c0:c0 + 256],
                                                                             in_=ps[bk][:, 0:256]),
                         reads=[R_ps[bk]], writes=[R_Vm])
                    pfree(bk)

        for t in range(NT_A):
            AL = dbg.get('AL', 9)
            load_xt(x, t, sem_x)
            norm_T(xt, R_xt, 4, P_MIX, hT, R_hT, TT)
            if AL < 1:
                continue
            rope_tables(t)
            if AL < 2:
                continue
            latent_norm(C_CKV, P_KVN, cqn, R_cqn)
            if AL < 3:
                continue
            wv, Rw = wload(w_in, 0, 16, C_KR, 64)
            P.op("act", lambda e, wv=wv: e.activation(out=wrot[:, :, 0:32], in_=wv[:, :, 32:64], func=AF.Copy, scale=-1.0),
                 reads=[Rw], writes=[R_wrot])
            P.op("act", lambda e, wv=wv: e.activation(out=wrot[:, :, 32:64], in_=wv[:, :, 0:32], func=AF.Copy),
                 reads=[Rw], writes=[R_wrot])
            b1 = palloc(); b2 = palloc()
            P.group("pe", [MM(ps[b1][0:64, :], wv[:, kc, :], hT[:, kc, :], kc == 0, kc == 15) for kc in range(16)],
                    reads=[Rw, R_hT], writes=[R_ps[b1]])
            P.group("pe", [MM(ps[b2][0:64, :], wrot[:, kc, :], hT[:, kc, :], kc == 0, kc == 15) for kc in range(16)],
                    reads=[R_wrot, R_hT], writes=[R_ps[b2]])
            rope_evac(b1, b2, krT[0:64, t * TT:(t + 1) * TT], R_kr)
            pfree(b1); pfree(b2)
            if AL < 4:
                continue
            for half in range(2):
                wv, Rw = wload(w_ukv, 0, 4, half * 1024, 1024)
                for hh in range(4):
                    h = half * 4 + hh
                    bk = palloc()
                    P.group("pe", [MM(ps[bk][:], wv[:, kc, hh * 256: hh * 256 + 128], cqn[:, kc, :], kc == 0, kc == 3)
                                   for kc in range(4)], reads=[Rw, R_cqn], writes=[R_ps[bk]])
                    P.op("act", lambda e, bk=bk, h=h, t=t: e.activation(out=KT[:, h, t * TT:(t + 1) * TT], in_=ps[bk][:],
                                                                        func=AF.Copy),
                         reads=[R_ps[bk]], writes=[R_KT])
                    pfree(bk)
                wv4 = wv.rearrange("p k (h c) -> p k h c", h=4)
                for j in range(4):
                    bk = palloc()
                    P.group("pe", [MM(ps[bk][:].rearrange("p (h c) -> p h c", h=4), cqn[:, kc, j * 128:(j + 1) * 128],
                                      wv4[:, kc, :, 128:256], kc == 0, kc == 3) for kc in range(4)],
                            reads=[Rw, R_cqn], writes=[R_ps[bk]])
                    P.op("dve", lambda e, bk=bk, j=j, t=t, half=half: e.tensor_copy(
                        out=Vt[:, t * 4 + j, half * 512:(half + 1) * 512], in_=ps[bk][:]),
                         reads=[R_ps[bk]], writes=[R_V])
                    pfree(bk)

        SC_MLA = 192.0 ** -0.5
        SC_MEM = 256.0 ** -0.5
        for t in range(NT_1):
            load_xt(x, t, sem_x)
            if t == 0:
                load_halo(x)
            norm_T(xt, R_xt, 4, P_MIX, hT, R_hT, TT)
            if t == 0:
                norm_T(xh, R_xh, 1, P_MIX, hTh, R_hTh, 4)
            rope_tables(t)
            latent_norm(C_CQ, P_QN, cqn, R_cqn)
            SL = dbg.get('SL', 9)
            if SL < 1:
                continue
            for hg in range(2):
                wv, Rw = wload(w_uq, 0, 4, hg * 768, 768)
                wv4 = wv.rearrange("p k (h c) -> p k h c", h=4)
                wrv = wrot[:, :, :].rearrange("p k c -> p (k c)")[:, 0:1024].rearrange("p (k h c) -> p k h c", k=4, h=4)
                P.op("act", lambda e, wv4=wv4, wrv=wrv: e.activation(out=wrv[:, :, :, 0:32], in_=wv4[:, :, :, 160:192],
                                                                     func=AF.Copy, scale=-1.0),
                     reads=[Rw], writes=[R_wrot])
                P.op("act", lambda e, wv4=wv4, wrv=wrv: e.activation(out=wrv[:, :, :, 32:64], in_=wv4[:, :, :, 128:160],
                                                                     func=AF.Copy),
                     reads=[Rw], writes=[R_wrot])
                for hh in range(4):
                    bk = palloc()
                    P.group("pe", [MM(ps[bk][:], wv4[:, kc, hh, 0:128], cqn[:, kc, :], kc == 0, kc == 3)
                                   for kc in range(4)], reads=[Rw, R_cqn], writes=[R_ps[bk]])
                    P.op("act", lambda e, bk=bk, hh=hh: e.activation(out=qn[:, hh, :], in_=ps[bk][:], func=AF.Copy),
                         reads=[R_ps[bk]], writes=[R_qn])
                    pfree(bk)
                    b1 = palloc(); b2 = palloc()
                    P.group("pe", [MM(ps[b1][0:64, :], wv4[:, kc, hh, 128:192], cqn[:, kc, :], kc == 0, kc == 3)
                                   for kc in range(4)], reads=[Rw, R_cqn], writes=[R_ps[b1]])
                    P.group("pe", [MM(ps[b2][0:64, :], wrv[:, kc, hh, :], cqn[:, kc, :], kc == 0, kc == 3)
                                   for kc in range(4)], reads=[R_wrot, R_cqn], writes=[R_ps[b2]])
                    rope_evac(b1, b2, qr[0:64, hh, :], R_qr)
                    pfree(b1); pfree(b2)
                if SL < 2:
                    continue
                for hh in range(4):
                    h = hg * 4 + hh
                    bo = palloc(); bs = palloc()
                    LAG = 2
                    for kk in range(16 + LAG):
                        if kk < 16:
                            kc = kk
                            bk = palloc()
                            pb = kc % 3
                            P.group("pe", [MM(ps[bk][:], KT[:, h, kc * 128:(kc + 1) * 128], qn[:, hh, :], True, False),
                                           MM(ps[bk][:], krT[0:64, kc * 128:(kc + 1) * 128], qr[0:64, hh, :], False, True)],
                                    reads=[R_KT, R_kr, R_qn, R_qr], writes=[R_ps[bk]])
                            P.op("act", lambda e, bk=bk, pb=pb: e.activation(out=PT[pb][:], in_=ps[bk][:], func=AF.Exp,
                                                                             scale=SC_MLA),
                                 reads=[R_ps[bk]], writes=[R_PT[pb]])
                            pfree(bk)
                        if kk >= LAG:
                            kc = kk - LAG
                            pb = kc % 3
                            P.group("pe", [MM(ps[bo][:], Vt[:, kc, h * 128:(h + 1) * 128], PT[pb][:], kc == 0, kc == 15),
                                           MM(ps[bs][:], ones[:], PT[pb][:], kc == 0, kc == 15)],
                                    reads=[R_V, R_PT[pb], R_ones], writes=[R_ps[bo], R_ps[bs]])
                    P.op("dve", lambda e, bs=bs: e.reciprocal(out=T[3][:, 0:TT], in_=ps[bs][:]),
                         reads=[R_ps[bs]], writes=[R_T[3]])
                    P.op("dve", lambda e, bo=bo, h=h: e.tensor_tensor(out=attnT[:, h, :], in0=ps[bo][:], in1=T[3][:, 0:TT],
                                                                      op=ALU.mult),
                         reads=[R_ps[bo], R_T[3]], writes=[R_attn])
                    pfree(bo); pfree(bs)
            if SL < 3:
                continue
            for g in range(4):
                wcv, Rcv = wload(w_in, 0, 16, C_CV + g * 256, 256)
                wcc, Rcc = wload(w_in, 0, 16, C_CC + g * 256, 256)
                wcb, Rcb = wload(w_in, 0, 16, C_CB + g * 256, 256)
                for cc in range(2):
                    i = g * 2 + cc
                    ub = T[4 + (i % 2)]; R_ub = R_T[4 + (i % 2)]
                    bv = palloc(); bc = palloc()
                    P.group("pe", [MM(ps[bv][:], wcv[:, kc, cc * 128:(cc + 1) * 128], hT[:, kc, :], kc == 0, kc == 15)
                                   for kc in range(16)], reads=[Rcv, R_hT], writes=[R_ps[bv]])
                    P.group("pe", [MM(ps[bc][:], wcc[:, kc, cc * 128:(cc + 1) * 128], hT[:, kc, :], kc == 0, kc == 15)
                                   for kc in range(16)], reads=[Rcc, R_hT], writes=[R_ps[bc]])
                    P.op("act", lambda e, bv=bv: e.activation(out=T[0][:, 0:TT], in_=ps[bv][:], func=AF.Copy),
                         reads=[R_ps[bv]], writes=[R_T[0]])
                    P.op("dve", lambda e, bc=bc, ub=ub: e.tensor_tensor(out=ub[:, 1:TT + 1], in0=ps[bc][:], in1=T[0][:, 0:TT],
                                                                        op=ALU.mult),
                         reads=[R_ps[bc], R_T[0]], writes=[R_ub])
                    pfree(bv); pfree(bc)
                    if t == 0:
                        hv = palloc(); hc = palloc()
                        P.group("pe", [MM(ps[hv][:, 0:4], wcv[:, kc, cc * 128:(cc + 1) * 128], hTh[:, kc, :], kc == 0, kc == 15)
                                       for kc in range(16)], reads=[Rcv, R_hTh], writes=[R_ps[hv]])
                        P.group("pe", [MM(ps[hc][:, 0:4], wcc[:, kc, cc * 128:(cc + 1) * 128], hTh[:, kc, :], kc == 0, kc == 15)
                                       for kc in range(16)], reads=[Rcc, R_hTh], writes=[R_ps[hc]])
                        P.op("act", lambda e, hv=hv: e.activation(out=T[1][:, 0:4], in_=ps[hv][:, 0:4], func=AF.Copy),
                             reads=[R_ps[hv]], writes=[R_T[1]])
                        P.op("dve", lambda e, hc=hc, i=i: e.tensor_tensor(out=uh[:, i, :], in0=ps[hc][:, 0:4], in1=T[1][:, 0:4],
                                                                          op=ALU.mult),
                             reads=[R_ps[hc], R_T[1]], writes=[R_uh])
                        pfree(hv); pfree(hc)
                    P.op("act", lambda e, ub=ub, i=i: e.activation(out=ub[:, 0:1], in_=ucar[:, i:i + 1], func=AF.Copy),
                         reads=[R_ucar], writes=[R_ub])
                    rsrc = uh[:, i, t:t + 1] if t < NT - 1 else zero1[:, 0:1]
                    P.op("act", lambda e, ub=ub, rsrc=rsrc: e.activation(out=ub[:, TT + 1:TT + 2], in_=rsrc, func=AF.Copy),
                         reads=[R_uh], writes=[R_ub])
                    P.op("act", lambda e, ub=ub, i=i: e.activation(out=ucar[:, i:i + 1], in_=ub[:, TT:TT + 1], func=AF.Copy),
                         reads=[R_ub], writes=[R_ucar])
                    P.op("dve", lambda e, ub=ub, i=i: e.tensor_scalar(out=T[1][:, 0:TT], in0=ub[:, 0:TT],
                                                                      scalar1=pT[:, P_CW + i:P_CW + i + 1], scalar2=None,
                                                                      op0=ALU.mult), reads=[R_ub, R_pT], writes=[R_T[1]])
                    P.op("dve", lambda e, ub=ub, i=i: e.scalar_tensor_tensor(out=T[1][:, 0:TT], in0=ub[:, 1:TT + 1],
                                                                             scalar=pT[:, P_CW + 8 + i:P_CW + 9 + i],
                                                                             in1=T[1][:, 0:TT], op0=ALU.mult, op1=ALU.add),
                         reads=[R_ub, R_pT], writes=[R_T[1]])
                    P.op("dve", lambda e, ub=ub, i=i: e.scalar_tensor_tensor(out=T[1][:, 0:TT], in0=ub[:, 2:TT + 2],
                                                                             scalar=pT[:, P_CW + 16 + i:P_CW + 17 + i],
                                                                             in1=T[1][:, 0:TT], op0=ALU.mult, op1=ALU.add),
                         reads=[R_ub, R_pT], writes=[R_T[1]])
                    bb = palloc()
                    P.group("pe", [MM(ps[bb][:], wcb[:, kc, cc * 128:(cc + 1) * 128], hT[:, kc, :], kc == 0, kc == 15)
                                   for kc in range(16)], reads=[Rcb, R_hT], writes=[R_ps[bb]])
                    P.op("dve", lambda e, bb=bb, i=i: e.tensor_tensor(out=convT[:, i, :], in0=ps[bb][:], in1=T[1][:, 0:TT],
                                                                      op=ALU.mult),
                         reads=[R_ps[bb], R_T[1]], writes=[R_conv])
                    pfree(bb)
            if SL < 4:
                continue
            qx = [qn, qr]
            R_qx = [R_qn, R_qr]
            for hx in range(4):
                wv, Rw = wload(w_in, 0, 16, C_QX + hx * 256, 256)
                for dc in range(2):
                    bk = palloc()
                    P.group("pe", [MM(ps[bk][:], wv[:, kc, dc * 128:(dc + 1) * 128], hT[:, kc, :], kc == 0, kc == 15)
                                   for kc in range(16)], reads=[Rw, R_hT], writes=[R_ps[bk]])
                    P.op("act", lambda e, bk=bk, hx=hx, dc=dc: e.activation(out=qx[dc][:, hx, :], in_=ps[bk][:], func=AF.Copy),
                         reads=[R_ps[bk]], writes=[R_qx[dc]])
                    pfree(bk)
            for hx in range(4):
                for mc in range(2):
                    bk = palloc()
                    P.group("pe", [MM(ps[bk][:], KmT[:, hx * 2 + dc, mc * 128:(mc + 1) * 128], qx[dc][:, hx, :], dc == 0, dc == 1)
                                   for dc in range(2)], reads=[R_Km, R_qn, R_qr], writes=[R_ps[bk]])
                    P.op("act", lambda e, bk=bk, mc=mc: e.activation(out=PT[mc][:], in_=ps[bk][:], func=AF.Exp, scale=SC_MEM),
                         reads=[R_ps[bk]], writes=[R_PT[mc]])
                    pfree(bk)
                bs = palloc()
                P.group("pe", [MM(ps[bs][:], ones[:], PT[mc][:], mc == 0, mc == 1) for mc in range(2)],
                        reads=[R_PT[0], R_PT[1], R_ones], writes=[R_ps[bs]])
                P.op("dve", lambda e, bs=bs: e.reciprocal(out=T[3][:, 0:TT], in_=ps[bs][:]), reads=[R_ps[bs]], writes=[R_T[3]])
                pfree(bs)
                for dv in range(2):
                    bo = palloc()
                    P.group("pe", [MM(ps[bo][:], Vm[:, mc, hx * 256 + dv * 128: hx * 256 + (dv + 1) * 128], PT[mc][:],
                                      mc == 0, mc == 1) for mc in range(2)], reads=[R_Vm, R_PT[0], R_PT[1]],
                            writes=[R_ps[bo]])
                    P.op("dve", lambda e, bo=bo, hx=hx, dv=dv: e.tensor_tensor(out=memT[:, hx * 2 + dv, :], in0=ps[bo][:],
                                                                               in1=T[3][:, 0:TT], op=ALU.mult),
                         reads=[R_ps[bo], R_T[3]], writes=[R_memo])
                    pfree(bo)
            if SL < 5:
                continue
            branches = [(w_o_mla, attnT, R_attn), (w_out_conv, convT, R_conv), (w_o_mem, memT, R_memo)]
            macc = [T[4], T[5]]
            R_macc = [R_T[4], R_T[5]]
            for jp in range(8):
                for br in range(3):
                    wsrc, actv, R_actv = branches[br]
                    wp, Rp = wload(wsrc, 0, 8, jp * 256, 256)
                    wg, Rg = wload(w_in, 0, 16, C_G + br * 2048 + jp * 256, 256)
                    for jj in range(2):
                        j = jp * 2 + jj
                        by = palloc(); bg = palloc()
                        P.group("pe", [MM(ps[by][:], wp[:, kc, jj * 128:(jj + 1) * 128], actv[:, kc, :], kc == 0, kc == 7)
                                       for kc in range(8)], reads=[Rp, R_actv], writes=[R_ps[by]])
                        P.group("pe", [MM(ps[bg][:], wg[:, kc, jj * 128:(jj + 1) * 128], hT[:, kc, :], kc == 0, kc == 15)
                                       for kc in range(16)], reads=[Rg, R_hT], writes=[R_ps[bg]])
                        gcol = P_GB + br * 16 + j
                        P.op("act", lambda e, bg=bg, gcol=gcol: e.activation(out=T[0][:, 0:TT], in_=ps[bg][:], func=AF.Sigmoid,
                                                                             bias=pT[:, gcol:gcol + 1]),
                             reads=[R_ps[bg], R_pT], writes=[R_T[0]])
                        if br == 0:
                            P.op("dve", lambda e, by=by, jj=jj: e.tensor_tensor(out=macc[jj][:, 0:TT], in0=ps[by][:],
                                                                                in1=T[0][:, 0:TT], op=ALU.mult),
                                 reads=[R_ps[by], R_T[0]], writes=[R_macc[jj]])
                        else:
                            P.op("dve", lambda e, by=by: e.tensor_tensor(out=T[1][:, 0:TT], in0=ps[by][:], in1=T[0][:, 0:TT],
                                                                         op=ALU.mult),
                                 reads=[R_ps[by], R_T[0]], writes=[R_T[1]])
                            if br == 1:
                                P.op("dve", lambda e, jj=jj: e.tensor_tensor(out=macc[jj][:, 0:TT], in0=macc[jj][:, 0:TT],
                                                                             in1=T[1][:, 0:TT], op=ALU.add),
                                     reads=[R_T[1]], writes=[R_macc[jj]])
                            else:
                                P.op("dve", lambda e, jj=jj, j=j: e.tensor_tensor(out=mergedT[:, j, :], in0=macc[jj][:, 0:TT],
                                                                                  in1=T[1][:, 0:TT], op=ALU.add),
                                     reads=[R_T[1], R_macc[jj]], writes=R_mg)
                        pfree(by); pfree(bg)
            if SL < 6:
                continue
            load_xt(x, t, sem_xr)
            for cg in range(4):
                wA, RA = wload(w_o, 0, 8, cg * 512, 512)
                wB, RB = wload(w_o, 1024, 8, cg * 512, 512)
                for jt in range(4):
                    bk = palloc()
                    P.group("pe", [MM(ps[bk][:], mergedT[:, kc, jt * 128:(jt + 1) * 128],
                                      (wA[:, kc, :] if kc < 8 else wB[:, kc - 8, :]), kc == 0, kc == 15)
                                   for kc in range(16)], reads=[RA, RB] + R_mg, writes=[R_ps[bk]])
                    P.op("dve", lambda e, bk=bk, jt=jt, cg=cg: e.tensor_tensor(out=xt[:, jt, cg * 512:(cg + 1) * 512],
                                                                               in0=ps[bk][:],
                                                                               in1=xt[:, jt, cg * 512:(cg + 1) * 512], op=ALU.add),
                         reads=[R_ps[bk]], writes=R_xt)
                    pfree(bk)
            dst = x1s[t * TT:(t + 1) * TT, :].rearrange("(j p) d -> p j d", p=128)
            P.dma("sp", lambda e, dst=dst: e.dma_start(out=dst, in_=xt), sem_st, reads=R_xt)

        P.dma("sp", lambda e: e.dma_start(out=gfin, in_=fin.partition_broadcast(128)), sem_misc,
              writes=[R_gfin, R_KT, R_V, R_kr])
        R_x1s = Res()
        for t in range(NT_2):
            P.wait_all("sp", [(("dma", sem_st), P.dsem[sem_st])])
            load_xt(x1s, t, sem_x)
            if t == 0:
                load_halo(x1s)
            norm_T(xt, R_xt, 4, P_FFN, hT, R_hT, TT)
            if t == 0:
                norm_T(xh, R_xh, 1, P_FFN, hTh, R_hTh, 4)
            for g in range(NFC // 2):
                wa, Ra = wload(w_up, 0, 16, g * 256, 256)
                wb, Rb = wload(w_up, 0, 16, DFF + g * 256, 256)
                for cc in range(2):
                    i = g * 2 + cc
                    outs = []
                    for ab, (wv, Rw) in enumerate(((wa, Ra), (wb, Rb))):
                        idx = ab * NFC + i
                        ub = T[ab]; R_ub = R_T[ab]
                        cv = T[2 + ab]; R_cv = R_T[2 + ab]
                        bk = palloc()
                        P.group("pe", [MM(ps[bk][:], wv[:, kc, cc * 128:(cc + 1) * 128], hT[:, kc, :], kc == 0, kc == 15)
                                       for kc in range(16)], reads=[Rw, R_hT], writes=[R_ps[bk]])
                        P.op("act", lambda e, bk=bk, ub=ub: e.activation(out=ub[:, 1:TT + 1], in_=ps[bk][:], func=AF.Copy),
                             reads=[R_ps[bk]], writes=[R_ub])
                        pfree(bk)
                        if t == 0:
                            hb_ = palloc()
                            P.group("pe", [MM(ps[hb_][:, 0:4], wv[:, kc, cc * 128:(cc + 1) * 128], hTh[:, kc, :], kc == 0, kc == 15)
                                           for kc in range(16)], reads=[Rw, R_hTh], writes=[R_ps[hb_]])
                            P.op("act", lambda e, hb_=hb_, idx=idx: e.activation(out=fh[:, idx, :], in_=ps[hb_][:, 0:4], func=AF.Copy),
                                 reads=[R_ps[hb_]], writes=[R_fh])
                            pfree(hb_)
                        P.op("act", lambda e, ub=ub, idx=idx: e.activation(out=ub[:, 0:1], in_=fcar[:, idx:idx + 1], func=AF.Copy),
                             reads=[R_fcar], writes=[R_ub])
                        rsrc = fh[:, idx, t:t + 1] if t < NT - 1 else zero1[:, 0:1]
                        P.op("act", lambda e, ub=ub, rsrc=rsrc: e.activation(out=ub[:, TT + 1:TT + 2], in_=rsrc, func=AF.Copy),
                             reads=[R_fh], writes=[R_ub])
                        P.op("act", lambda e, ub=ub, idx=idx: e.activation(out=fcar[:, idx:idx + 1], in_=ub[:, TT:TT + 1], func=AF.Copy),
                             reads=[R_ub], writes=[R_fcar])
                        c0 = P_FCW + ab * NFC + i
                        P.op("dve", lambda e, ub=ub, cv=cv, c0=c0: e.tensor_scalar(out=cv[:, 0:TT], in0=ub[:, 0:TT],
                                                                                   scalar1=pT[:, c0:c0 + 1], scalar2=None,
                                                                                   op0=ALU.mult), reads=[R_ub, R_pT], writes=[R_cv])
                        P.op("dve", lambda e, ub=ub, cv=cv, c0=c0: e.scalar_tensor_tensor(out=cv[:, 0:TT], in0=ub[:, 1:TT + 1],
                                                                                          scalar=pT[:, c0 + 88:c0 + 89],
                                                                                          in1=cv[:, 0:TT], op0=ALU.mult, op1=ALU.add),
                             reads=[R_ub, R_pT], writes=[R_cv])
                        P.op("dve", lambda e, ub=ub, cv=cv, c0=c0: e.scalar_tensor_tensor(out=cv[:, 0:TT], in0=ub[:, 2:TT + 2],
                                                                                          scalar=pT[:, c0 + 176:c0 + 177],
                                                                                          in1=cv[:, 0:TT], op0=ALU.mult, op1=ALU.add),
                             reads=[R_ub, R_pT], writes=[R_cv])
                    P.op("act", lambda e: e.activation(out=T[4][:, 0:TT], in_=T[2][:, 0:TT], func=AF.Silu),
                         reads=[R_T[2]], writes=[R_T[4]])
                    P.op("dve", lambda e, i=i: e.tensor_tensor(out=actT[:, i, :], in0=T[4][:, 0:TT], in1=T[3][:, 0:TT], op=ALU.mult),
                         reads=[R_T[4], R_T[3]], writes=[R_act[i], R_KT, R_V])
            KG = [(0, 8), (8, 8), (16, 8), (24, 8), (32, 8), (40, 4)]
            for cgp in range(4):
                bks = [palloc() for _ in range(4)]
                for (k0, kn) in KG:
                    wv, Rw = wload(w_down, k0 * 128, kn, cgp * 512, 512)
                    for jt in range(4):
                        P.group("pe", [MM(ps[bks[jt]][:], actT[:, k0 + kk, jt * 128:(jt + 1) * 128], wv[:, kk, :],
                                          (k0 + kk) == 0, (k0 + kk) == NFC - 1) for kk in range(kn)],
                                reads=[Rw] + [R_act[k0 + kk] for kk in range(kn)], writes=[R_ps[bks[jt]]])
                for jt in range(4):
                    P.op("dve", lambda e, jt=jt, cgp=cgp, bk=bks[jt]: e.tensor_tensor(
                        out=xt[:, jt, cgp * 512:(cgp + 1) * 512], in0=ps[bk][:], in1=xt[:, jt, cgp * 512:(cgp + 1) * 512],
                        op=ALU.add), reads=[R_ps[bks[jt]]], writes=R_xt)
                    pfree(bks[jt])
            for j in range(4):
                P.op("act", lambda e, j=j: e.activation(out=junk[:], in_=xt[:, j, :], func=AF.Square,
                                                         accum_out=small[:, j:j + 1]), reads=R_xt, writes=[R_junk, R_small])
            P.op("act", lambda e: e.activation(out=small[:, 5:9], in_=small[:, 0:4], func=AF.Ln, scale=1.0 / D, bias=eps_ap),
                 writes=[R_small])
            P.op("act", lambda e: e.activation(out=small[:, 5:9], in_=small[:, 5:9], func=AF.Exp, scale=-0.5), writes=[R_small])
            for j in range(4):
                P.op("dve", lambda e, j=j: e.scalar_tensor_tensor(out=xt[:, j, :], in0=xt[:, j, :], scalar=small[:, 5 + j:6 + j],
                                                                   in1=gfin, op0=ALU.mult, op1=ALU.mult),
                     reads=[R_small, R_gfin], writes=R_xt)
            dst = y[t * TT:(t + 1) * TT, :].rearrange("(j p) d -> p j d", p=128)
            P.dma("sp", lambda e, dst=dst: e.dma_start(out=dst, in_=xt), sem_st, reads=R_xt)
        P.wait_all("sp", [(("dma", i), P.dsem[i]) for i in range(len(P.dsem))])
        P.q["sp"].append(("op", lambda e: e.nop(), False))
        P.run(st)
    return nc


_NC_CACHE = {}


def kernel(**inputs):
    f32 = np.float32
    if "nc" not in _NC_CACHE:
        _NC_CACHE["nc"] = build_nc()
    nc = _NC_CACHE["nc"]

    def a(name):
        return np.ascontiguousarray(np.asarray(inputs[name]))

    x = a("x").astype(f32, copy=False)
    mem = a("mem").astype(f32, copy=False)
    pos = a("positions").astype(np.int32, copy=False)
    B = x.shape[0]
    prm = np.concatenate([a(n).astype(f32, copy=False).reshape(-1) for n in
                          ("mix_norm", "ffn_norm", "mem_norm", "q_norm", "kv_norm", "gate_bias", "conv_w", "ffn_conv_w")]
                         ).reshape(NPRM, 128)
    cst = np.zeros((128, 130), f32)
    cst[:, :128] = np.eye(128, dtype=f32)
    invf = np.power(f32(10000.0), -np.arange(0, 64, 2, dtype=f32) / f32(64)).astype(f32)
    cst[0:32, 128] = invf
    cst[32:64, 128] = invf
    shared = {
        "w_in": a("w_in")[0], "w_uq": a("w_uq")[0], "w_ukv": a("w_ukv")[0], "w_o_mla": a("w_o_mla")[0],
        "w_out_conv": a("w_out_conv")[0], "w_mem_kv": a("w_mem_kv")[0], "w_o_mem": a("w_o_mem")[0],
        "w_o": a("w_o")[0], "w_up": a("w_up")[0], "w_down": a("w_down")[0],
        "prm": prm, "final_norm": a("final_norm").reshape(1, D), "cst": cst,
    }
    in_maps = []
    for c in range(B):
        m = dict(shared)
        m["x"] = x[c]
        m["mem"] = mem[c]
        m["positions"] = pos[c].reshape(1, S)
        in_maps.append(m)
    res = run_bass_kernel_spmd(nc, in_maps, core_ids=list(range(B)))
    return np.stack([r["y"] for r in res.results], axis=0).astype(f32, copy=False)
```

```python
import math
from contextlib import ExitStack
import numpy as np
import concourse.bass as bass
import concourse.mybir as mybir
from concourse.bass_utils import run_bass_kernel_spmd

F32 = mybir.dt.float32
BF16 = mybir.dt.bfloat16
I32 = mybir.dt.int32
AF = mybir.ActivationFunctionType
ALU = mybir.AluOpType

S = 2048
D = 2048
TT = 512
NT = S // TT
MEM = 256
DFF = 5632
NFC = DFF // 128
IN_COLS = 11328
C_CQ, C_CKV, C_KR, C_CV, C_CB, C_CC, C_QX, C_G = 0, 512, 1024, 1088, 2112, 3136, 4160, 5184
EPS = 1e-6
P_MIX, P_FFN, P_MEMN, P_QN, P_KVN, P_GB, P_CW, P_FCW = 0, 16, 32, 48, 52, 56, 104, 128
NPRM = 392
NS = 4
SLOT = 4096


class Res:
    __slots__ = ("w", "r")

    def __init__(self):
        self.w = {}
        self.r = {}


def _mrg(d, tok):
    k, v = tok
    if d.get(k, 0) < v:
        d[k] = v


class Prog:
    ENG = ("pe", "act", "dve", "pool", "sp")

    def __init__(self, nc):
        self.nc = nc
        self.q = {e: [] for e in self.ENG}
        self.cnt = {e: 0 for e in self.ENG}
        self.dsem = []
        self.seen = {e: {} for e in self.ENG}

    def _wait(self, eng, key, val):
        if key == eng and eng == "pe":
            return
        if self.seen[eng].get(key, 0) >= val:
            return
        self.seen[eng][key] = val
        self.q[eng].append(("wait", key, val))

    def _deps(self, eng, reads, writes):
        for r in reads:
            for k, v in r.w.items():
                self._wait(eng, k, v)
        for w in writes:
            for k, v in w.w.items():
                self._wait(eng, k, v)
            for k, v in w.r.items():
                self._wait(eng, k, v)

    def _note(self, tok, reads, writes):
        for r in reads:
            _mrg(r.r, tok)
        for w in writes:
            w.w = {tok[0]: tok[1]}
            w.r = {}

    def op(self, eng, fn, reads=(), writes=()):
        self._deps(eng, reads, writes)
        self.cnt[eng] += 1
        tok = (eng, self.cnt[eng])
        self.q[eng].append(("op", fn, True))
        self._note(tok, reads, writes)
        return tok

    def group(self, eng, fns, reads=(), writes=()):
        self._deps(eng, reads, writes)
        n = len(fns)
        for i, fn in enumerate(fns):
            self.q[eng].append(("op", fn, i == n - 1))
        self.cnt[eng] += 1
        tok = (eng, self.cnt[eng])
        self._note(tok, reads, writes)
        return tok

    def new_dsem(self):
        self.dsem.append(0)
        return len(self.dsem) - 1

    def dma(self, eng, fn, sem, reads=(), writes=()):
        self._deps(eng, reads, writes)
        self.dsem[sem] += 16
        tok = (("dma", sem), self.dsem[sem])
        self.q[eng].append(("dma", fn, sem))
        self._note(tok, reads, writes)
        return tok

    def wait_all(self, eng, toks):
        for t in toks:
            self._wait(eng, t[0], t[1])

    def run(self, stack):
        nc = self.nc
        semh = {}
        for e in self.ENG:
            semh[e] = stack.enter_context(nc.semaphore("s_" + e))
        for i in range(len(self.dsem)):
            semh[("dma", i)] = stack.enter_context(nc.semaphore("d_%d" % i))
        block = stack.enter_context(nc.Block())

        def replay(name):
            def f(engine):
                for item in self.q[name]:
                    if item[0] == "wait":
                        engine.wait_ge(semh[item[1]], item[2])
                    elif item[0] == "op":
                        ins = item[1](engine)
                        if item[2]:
                            ins.then_inc(semh[name], 1)
                    else:
                        ins = item[1](engine)
                        ins.then_inc(semh[("dma", item[2])], 16)
            return f

        block.tensor(replay("pe"))
        block.scalar(replay("act"))
        block.vector(replay("dve"))
        block.gpsimd(replay("pool"))
        block.sync(replay("sp"))


def build_nc(dbg=None):
    import os
    dbg = dbg or {}
    NT_A = dbg.get('A', NT); NT_1 = dbg.get('S1', NT); NT_2 = dbg.get('S2', NT); DO_M = dbg.get('M', 1)
    nc = bass.Bass("TRN2", target_bir_lowering=False)

    def din(name, shape, dt=F32):
        return nc.dram_tensor(name, shape, dt, kind="ExternalInput").ap()

    x = din("x", [S, D])
    mem = din("mem", [MEM, D])
    pos = din("positions", [1, S], I32)
    w_in = din("w_in", [D, IN_COLS])
    w_uq = din("w_uq", [512, 1536])
    w_ukv = din("w_ukv", [512, 2048])
    w_o_mla = din("w_o_mla", [1024, D])
    w_out_conv = din("w_out_conv", [1024, D])
    w_mem_kv = din("w_mem_kv", [D, 2048])
    w_o_mem = din("w_o_mem", [1024, D])
    w_o = din("w_o", [D, D])
    w_up = din("w_up", [D, 2 * DFF])
    w_down = din("w_down", [DFF, D])
    prm = din("prm", [NPRM, 128])
    fin = din("final_norm", [1, D])
    cst = din("cst", [128, 130])
    y = nc.dram_tensor("y", [S, D], F32, kind="ExternalOutput").ap()
    x1s = nc.dram_tensor("x1s", [S, D], F32, kind="ExternalOutput").ap()

    with ExitStack() as st:
        def sb(name, shape, dt):
            return st.enter_context(nc.sbuf_tensor(name, shape, dt))

        P = Prog(nc)
        slots = [sb("slot%d" % i, [128, SLOT], BF16) for i in range(NS)]
        R_slot = [Res() for _ in range(NS)]
        slot_sem = [P.new_dsem() for _ in range(NS)]
        hT = sb("hT", [128, 16, TT], BF16); R_hT = Res()
        hTh = sb("hTh", [128, 16, 4], BF16); R_hTh = Res()
        big = sb("big", [128, 34816], BF16)
        KT = big[:, 0:16384].rearrange("p (h s) -> p h s", h=8)
        Vt = big[:, 16384:32768].rearrange("p (c n) -> p c n", c=16)
        krT = big[:, 32768:34816]
        actT = big[:, 0:NFC * TT].rearrange("p (c n) -> p c n", c=NFC)
        gfin = big[:, 22528:26624].bitcast(F32)
        R_b0 = Res(); R_b1 = Res()
        R_KT = Res(); R_V = Res(); R_kr = Res(); R_gfin = Res()
        R_act = [Res() for _ in range(NFC)]
        KV8 = sb("KV8", [128, 4096], BF16)
        KmT = KV8[:, 0:2048].rearrange("p (c n) -> p c n", c=8); R_Km = Res()
        Vm = KV8[:, 2048:4096].rearrange("p (c n) -> p c n", c=2); R_Vm = Res()
        U = sb("U", [128, 12288], F32)
        xt = U[:, 0:8192].rearrange("p (j d) -> p j d", j=4)
        attnT = U[:, 0:2048].bitcast(BF16).rearrange("p (c n) -> p c n", c=8)
        convT = U[:, 2048:4096].bitcast(BF16).rearrange("p (c n) -> p c n", c=8)
        memT = U[:, 4096:6144].bitcast(BF16).rearrange("p (c n) -> p c n", c=8)
        cqraw = U[:, 6144:8192].rearrange("p (c n) -> p c n", c=4)
        R_attn = Res(); R_conv = Res(); R_memo = Res(); R_cqraw = Res()
        R_xt = [R_attn, R_conv, R_memo, R_cqraw]
        mg = U[:, 8192:12288]
        mergedT = mg.bitcast(BF16).rearrange("p (c n) -> p c n", c=16)
        xn = mg.bitcast(BF16).rearrange("p (j d) -> p j d", j=4)
        cqn = mg[:, 0:1024].bitcast(BF16).rearrange("p (c n) -> p c n", c=4)
        cosT = mg[0:64, 1024:1536]
        sinT = mg[0:64, 1536:2048]
        rtA = mg[0:64, 2048:2560]
        rtB = mg[0:64, 2560:3072]
        posi = mg[0:64, 3072:3584].bitcast(I32)
        R_cqn = Res(); R_cs = Res(); R_rt = Res()
        R_mg = [R_cqn, R_cs, R_rt]
        qq = sb("qq", [128, 8, TT], BF16)
        qn = qq[:, 0:4, :]
        qr = qq[:, 4:8, :]
        R_qn = Res(); R_qr = Res()
        xh = qq[:, :, :].rearrange("p a b -> p (a b)").bitcast(F32).rearrange("p (j d) -> p j d", j=1)
        R_xh = [R_qn, R_qr]
        junk = sb("junk", [128, 2048], BF16); R_junk = Res()
        T = [sb("T%d" % i, [128, 514], F32) for i in range(6)]
        R_T = [Res() for _ in range(6)]
        prow = T[0][:, 0:512].rearrange("p (b c) -> p b c", b=4)
        R_prow = R_T[0]
        PT = [sb("PT%d" % i, [128, TT], BF16) for i in range(3)]
        R_PT = [Res() for _ in range(3)]
        wrot = sb("wrot", [128, 16, 64], BF16); R_wrot = Res()
        pT = sb("pT", [128, NPRM], F32); R_pT = Res()
        idf = sb("idf", [128, 130], F32); R_idf = Res()
        idb = sb("idb", [128, 128], BF16); R_idb = Res()
        ones = sb("ones", [128, 128], BF16); R_ones = Res()
        small = sb("small", [128, 16], F32); R_small = Res()
        ucar = sb("ucar", [128, 8], F32); R_ucar = Res()
        uh = sb("uh", [128, 8, 4], F32); R_uh = Res()
        fcar = sb("fcar", [128, 2 * NFC], F32); R_fcar = Res()
        fh = sb("fh", [128, 2 * NFC, 4], F32); R_fh = Res()
        zero1 = sb("zero1", [128, 4], F32)

        NB = 6
        ps = [st.enter_context(nc.psum_tensor("ps%d" % i, [128, 512], F32)) for i in range(NB)]
        psT = [st.enter_context(nc.psum_tensor("psT%d" % i, [128, 1024], BF16)) for i in range(2)]
        R_ps = [Res() for _ in range(NB)]
        R_psT = [Res(), Res()]
        held = [False] * NB
        rot = [0]

        def palloc():
            for _ in range(NB):
                i = rot[0] % NB
                rot[0] += 1
                if not held[i]:
                    held[i] = True
                    return i
            raise RuntimeError("psum exhausted")

        def pfree(i):
            held[i] = False

        sem_x = P.new_dsem(); sem_misc = P.new_dsem(); sem_st = P.new_dsem(); sem_pos = P.new_dsem()
        sem_xh = P.new_dsem(); sem_xr = P.new_dsem(); sem_idf = P.new_dsem(); sem_prm = P.new_dsem()
        slot_ctr = [0]

        def wload(w, r0, kcn, c0, ncols):
            i = slot_ctr[0] % NS
            slot_ctr[0] += 1
            view = slots[i][:, 0:kcn * ncols].rearrange("p (k n) -> p k n", k=kcn)
            src = w[r0:r0 + kcn * 128, c0:c0 + ncols].rearrange("(k p) n -> p k n", p=128)
            P.dma("pool", lambda e, view=view, src=src: e.dma_start(out=view, in_=src), slot_sem[i],
                  writes=[R_slot[i]])
            return view, R_slot[i]

        def MM(o, l, r, s, t):
            return lambda e: e.matmul(o, l, r, start=s, stop=t)

        P.dma("sp", lambda e: e.dma_start(out=idf[:], in_=cst), sem_idf, writes=[R_idf])
        for b in range(4):
            r0 = b * 128
            n = min(128, NPRM - r0)
            P.dma("sp", lambda e, b=b, r0=r0, n=n: e.dma_start(out=prow[0:n, b, :], in_=prm[r0:r0 + n, :]),
                  sem_prm, writes=[R_prow])
        P.op("dve", lambda e: e.memset(ones[:], 1.0), writes=[R_ones])
        P.op("dve", lambda e: e.memset(small[:], 0.0), writes=[R_small])
        P.op("dve", lambda e: e.memset(small[:, 10:11], EPS), writes=[R_small])
        P.op("dve", lambda e: e.memset(small[:, 11:12], math.pi / 2), writes=[R_small])
        P.op("dve", lambda e: e.memset(zero1[:], 0.0))
        P.op("dve", lambda e: e.memset(ucar[:], 0.0), writes=[R_ucar])
        P.op("dve", lambda e: e.memset(fcar[:], 0.0), writes=[R_fcar])
        P.op("dve", lambda e: e.memset(uh[:], 0.0), writes=[R_uh])
        P.op("dve", lambda e: e.memset(fh[:], 0.0), writes=[R_fh])
        P.op("dve", lambda e: e.tensor_copy(out=idb[:], in_=idf[:, 0:128]), reads=[R_idf], writes=[R_idb])
        for b in range(4):
            n = min(128, NPRM - b * 128)
            bk = palloc()
            P.group("pe", [lambda e, b=b, n=n, bk=bk: e.transpose(ps[bk][:, 0:n], prow[0:n, b, :], idf[0:n, 0:n])],
                    reads=[R_prow, R_idf], writes=[R_ps[bk]])
            P.op("dve", lambda e, b=b, n=n, bk=bk: e.tensor_copy(out=pT[:, b * 128:b * 128 + n], in_=ps[bk][:, 0:n]),
                 reads=[R_ps[bk]], writes=[R_pT])
            pfree(bk)
        eps_ap = small[:, 10:11]
        halfpi = small[:, 11:12]
        invf = idf[0:64, 128:129]

        def norm_T(src, R_src, nj, gbase, dst, R_dst, ncopy):
            for j in range(nj):
                P.op("act", lambda e, j=j: e.activation(out=junk[:], in_=src[j], func=AF.Square,
                                                         accum_out=small[:, j:j + 1]),
                     reads=R_src, writes=[R_junk, R_small])
            P.op("act", lambda e: e.activation(out=small[:, 5:5 + nj], in_=small[:, 0:nj], func=AF.Ln,
                                               scale=1.0 / D, bias=eps_ap), reads=[], writes=[R_small])
            P.op("act", lambda e: e.activation(out=small[:, 5:5 + nj], in_=small[:, 5:5 + nj], func=AF.Exp,
                                               scale=-0.5), writes=[R_small])
            for j in range(nj):
                P.op("dve", lambda e, j=j: e.tensor_scalar(out=xn[:, j, :], in0=src[j],
                                                            scalar1=small[:, 5 + j:6 + j], scalar2=None,
                                                            op0=ALU.mult),
                     reads=list(R_src) + [R_small], writes=R_mg)
            for kc in range(16):
                hb = kc % 2
                fns = [(lambda e, j=j, kc=kc, hb=hb: e.transpose(psT[hb][:, j * 128:(j + 1) * 128],
                                                                 xn[:, j, kc * 128:(kc + 1) * 128], idb[:]))
                       for j in range(nj)]
                P.group("pe", fns, reads=R_mg + [R_idb], writes=[R_psT[hb]])
                P.op("dve", lambda e, kc=kc, hb=hb: e.tensor_scalar(out=dst[:, kc, 0:ncopy],
                                                                    in0=psT[hb][:, 0:ncopy],
                                                                    scalar1=pT[:, gbase + kc: gbase + kc + 1],
                                                                    scalar2=None, op0=ALU.mult),
                     reads=[R_psT[hb], R_pT], writes=[R_dst])

        def load_xt(srcd, t, sem):
            src = srcd[t * TT:(t + 1) * TT, :].rearrange("(j p) d -> p j d", p=128)
            P.dma("sp", lambda e, src=src: e.dma_start(out=xt, in_=src), sem, writes=R_xt)

        def load_halo(srcd):
            P.op("dve", lambda e: e.memset(xh, 0.0), writes=R_xh)
            for i in range(3):
                P.dma("sp", lambda e, i=i: e.dma_start(out=xh[i:i + 1, 0, :], in_=srcd[(i + 1) * TT:(i + 1) * TT + 1, :]),
                      sem_xh, writes=R_xh)

        def rope_tables(t):
            RL = dbg.get('RL', 9)
            P.dma("sp", lambda e: e.dma_start(out=posi, in_=pos[0:1, t * TT:(t + 1) * TT].partition_broadcast(64)),
                  sem_pos, writes=[R_rt])
            if RL < 1:
                return
            P.op("dve", lambda e: e.tensor_copy(out=rtA, in_=posi), reads=[R_rt], writes=[R_rt])
            P.op("dve", lambda e: e.tensor_scalar(out=rtA, in0=rtA, scalar1=invf, scalar2=None, op0=ALU.mult),
                 reads=[R_rt, R_idf], writes=[R_rt])
            P.op("dve", lambda e: e.tensor_scalar(out=rtB, in0=rtA, scalar1=1.0 / (2 * math.pi), scalar2=None,
                                                  op0=ALU.mult), reads=[R_rt], writes=[R_rt])
            if RL < 2:
                return
            P.op("dve", lambda e: e.tensor_copy(out=posi, in_=rtB), reads=[R_rt], writes=[R_rt])
            P.op("dve", lambda e: e.tensor_copy(out=rtB, in_=posi), reads=[R_rt], writes=[R_rt])
            P.op("dve", lambda e: e.scalar_tensor_tensor(out=rtA, in0=rtB, scalar=-2 * math.pi, in1=rtA,
                                                         op0=ALU.mult, op1=ALU.add), reads=[R_rt], writes=[R_rt])
            P.op("dve", lambda e: e.tensor_scalar(out=rtA, in0=rtA, scalar1=-3.141592, scalar2=3.141592,
                                                  op0=ALU.max, op1=ALU.min), reads=[R_rt], writes=[R_rt])
            if RL < 3:
                return
            P.op("act", lambda e: e.activation(out=sinT, in_=rtA, func=AF.Sin), reads=[R_rt], writes=[R_cs])
            if RL < 4:
                return
            P.op("act", lambda e: e.activation(out=rtB, in_=rtA, func=AF.Abs), reads=[R_rt], writes=[R_rt])
            if RL < 5:
                return
            P.op("act", lambda e: e.activation(out=cosT, in_=rtB, func=AF.Sin, scale=-1.0, bias=halfpi[0:64, :]),
                 reads=[R_rt, R_small], writes=[R_cs])

        def rms_bcast(bk_ss, n, dstT, R_dstT):
            P.op("act", lambda e: e.activation(out=dstT[:, 0:TT], in_=ps[bk_ss][:], func=AF.Ln, scale=1.0 / n,
                                               bias=eps_ap), reads=[R_ps[bk_ss], R_small], writes=[R_dstT])
            P.op("act", lambda e: e.activation(out=dstT[:, 0:TT], in_=dstT[:, 0:TT], func=AF.Exp, scale=-0.5),
                 writes=[R_dstT])

        def latent_norm(wcols, gbase, dst, R_dst):
            LL = dbg.get('LL', 9)
            bss = palloc()
            for g in range(2):
                wv, Rw = wload(w_in, 0, 16, wcols + g * 256, 256)
                if LL < 1:
                    continue
                for cc in range(2):
                    c = g * 2 + cc
                    bk = palloc()
                    P.group("pe", [MM(ps[bk][:], wv[:, kc, cc * 128:(cc + 1) * 128], hT[:, kc, :], kc == 0, kc == 15)
                                   for kc in range(16)], reads=[Rw, R_hT], writes=[R_ps[bk]])
                    LV = dbg.get('LV', 3)
                    if LL >= 2 and (LV & 2):
                        P.op("dve", lambda e, bk=bk, c=c: e.tensor_copy(out=cqraw[:, c, :], in_=ps[bk][:]),
                             reads=[R_ps[bk]], writes=[R_cqraw])
                    if LL >= 2 and (LV & 1):
                        P.op("act", lambda e, c=c: e.activation(out=PT[0][:], in_=cqraw[:, c, :], func=AF.Square),
                             reads=[R_cqraw], writes=[R_PT[0]])
                    pfree(bk)
                    if LL >= 3:
                        P.group("pe", [MM(ps[bss][:], ones[:], PT[0][:], c == 0, c == 3)], reads=[R_PT[0], R_ones],
                                writes=[R_ps[bss]])
            if LL >= 4:
                rms_bcast(bss, 512, T[0], R_T[0])
            pfree(bss)
            if LL < 5:
                return
            for c in range(4):
                P.op("dve", lambda e, c=c: e.scalar_tensor_tensor(out=dst[:, c, :], in0=cqraw[:, c, :],
                                                                   scalar=pT[:, gbase + c:gbase + c + 1],
                                                                   in1=T[0][:, 0:TT], op0=ALU.mult, op1=ALU.mult),
                     reads=[R_cqraw, R_pT, R_T[0]], writes=[R_dst])

        def rope_evac(bk_raw, bk_rot, dst_ap, R_dst):
            P.op("dve", lambda e: e.tensor_tensor(out=T[1][0:64, 0:TT], in0=ps[bk_raw][0:64, :], in1=cosT, op=ALU.mult),
                 reads=[R_ps[bk_raw], R_cs], writes=[R_T[1]])
            P.op("dve", lambda e: e.tensor_tensor(out=T[2][0:64, 0:TT], in0=ps[bk_rot][0:64, :], in1=sinT, op=ALU.mult),
                 reads=[R_ps[bk_rot], R_cs], writes=[R_T[2]])
            P.op("dve", lambda e: e.tensor_tensor(out=dst_ap, in0=T[1][0:64, 0:TT], in1=T[2][0:64, 0:TT], op=ALU.add),
                 reads=[R_T[1], R_T[2]], writes=[R_dst])

        P.dma("sp", lambda e: e.dma_start(out=xt[:, 0:2, :], in_=mem.rearrange("(j p) d -> p j d", p=128)),
              sem_x, writes=R_xt)
        norm_T([xt[:, j, :] for j in range(2)], R_xt, 2, P_MEMN, hT, R_hT, MEM)
        for g in range(8):
            wv, Rw = wload(w_mem_kv, 0, 16, g * 256, 256)
            if g < 4:
                for cc in range(2):
                    c = g * 2 + cc
                    bk = palloc()
                    P.group("pe", [MM(ps[bk][:, 0:MEM], wv[:, kc, cc * 128:(cc + 1) * 128], hT[:, kc, 0:MEM],
                                      kc == 0, kc == 15) for kc in range(16)], reads=[Rw, R_hT], writes=[R_ps[bk]])
                    P.op("dve", lambda e, bk=bk, c=c: e.tensor_copy(out=KmT[:, c, :], in_=ps[bk][:, 0:MEM]),
                         reads=[R_ps[bk]], writes=[R_Km])
                    pfree(bk)
            else:
                c0 = (g - 4) * 256
                for mc in range(2):
                    bk = palloc()
                    P.group("pe", [MM(ps[bk][:, 0:256], hT[:, kc, mc * 128:(mc + 1) * 128], wv[:, kc, :],
                                      kc == 0, kc == 15) for kc in range(16)], reads=[Rw, R_hT], writes=[R_ps[bk]])
                    P.op("dve", lambda e, bk=bk, mc=mc, c0=c0: e.tensor_copy(out=Vm[:, mc, c0:c0 + 256],
                                                                             in_=ps[bk][:, 0:256]),
                         reads=[R_ps[bk]], writes=[R_Vm])
                    pfree(bk)

        for t in range(NT_A):
            AL = dbg.get('AL', 9)
            load_xt(x, t, sem_x)
            norm_T([xt[:, j, :] for j in range(4)], R_xt, 4, P_MIX, hT, R_hT, TT)
            if AL < 1:
                continue
            rope_tables(t)
            if AL < 2:
                continue
            latent_norm(C_CKV, P_KVN, cqn, R_cqn)
            if AL < 3:
                continue
            wv, Rw = wload(w_in, 0, 16, C_KR, 64)
            P.op("act", lambda e, wv=wv: e.activation(out=wrot[:, :, 0:32], in_=wv[:, :, 32:64], func=AF.Copy, scale=-1.0),
                 reads=[Rw], writes=[R_wrot])
            P.op("act", lambda e, wv=wv: e.activation(out=wrot[:, :, 32:64], in_=wv[:, :, 0:32], func=AF.Copy),
                 reads=[Rw], writes=[R_wrot])
            b1 = palloc(); b2 = palloc()
            P.group("pe", [MM(ps[b1][0:64, :], wv[:, kc, :], hT[:, kc, :], kc == 0, kc == 15) for kc in range(16)],
                    reads=[Rw, R_hT], writes=[R_ps[b1]])
            P.group("pe", [MM(ps[b2][0:64, :], wrot[:, kc, :], hT[:, kc, :], kc == 0, kc == 15) for kc in range(16)],
                    reads=[R_wrot, R_hT], writes=[R_ps[b2]])
            rope_evac(b1, b2, krT[0:64, t * TT:(t + 1) * TT], R_kr)
            pfree(b1); pfree(b2)
            if AL < 4:
                continue
            for half in range(2):
                wv, Rw = wload(w_ukv, 0, 4, half * 1024, 1024)
                for hh in range(4):
                    h = half * 4 + hh
                    bk = palloc()
                    P.group("pe", [MM(ps[bk][:], wv[:, kc, hh * 256: hh * 256 + 128], cqn[:, kc, :], kc == 0, kc == 3)
                                   for kc in range(4)], reads=[Rw, R_cqn], writes=[R_ps[bk]])
                    P.op("act", lambda e, bk=bk, h=h, t=t: e.activation(out=KT[:, h, t * TT:(t + 1) * TT], in_=ps[bk][:],
                                                                        func=AF.Copy),
                         reads=[R_ps[bk]], writes=[R_KT])
                    pfree(bk)
                wv4 = wv.rearrange("p k (h c) -> p k h c", h=4)
                for j in range(4):
                    bk = palloc()
                    P.group("pe", [MM(ps[bk][:].rearrange("p (h c) -> p h c", h=4), cqn[:, kc, j * 128:(j + 1) * 128],
                                      wv4[:, kc, :, 128:256], kc == 0, kc == 3) for kc in range(4)],
                            reads=[Rw, R_cqn], writes=[R_ps[bk]])
                    P.op("dve", lambda e, bk=bk, j=j, t=t, half=half: e.tensor_copy(
                        out=Vt[:, t * 4 + j, half * 512:(half + 1) * 512], in_=ps[bk][:]),
                         reads=[R_ps[bk]], writes=[R_V])
                    pfree(bk)

        SC_MLA = 192.0 ** -0.5
        SC_MEM = 256.0 ** -0.5
        for t in range(NT_1):
            load_xt(x, t, sem_x)
            if t == 0:
                load_halo(x)
            norm_T([xt[:, j, :] for j in range(4)], R_xt, 4, P_MIX, hT, R_hT, TT)
            if t == 0:
                norm_T([xh[:, 0, :]], R_xh, 1, P_MIX, hTh, R_hTh, 4)
            rope_tables(t)
            latent_norm(C_CQ, P_QN, cqn, R_cqn)
            SL = dbg.get('SL', 9)
            if SL < 1:
                continue
            for hg in range(2):
                wv, Rw = wload(w_uq, 0, 4, hg * 768, 768)
                wv4 = wv.rearrange("p k (h c) -> p k h c", h=4)
                wrv = wrot[:, :, :].rearrange("p k c -> p (k c)")[:, 0:1024].rearrange("p (k h c) -> p k h c", k=4, h=4)
                P.op("act", lambda e, wv4=wv4, wrv=wrv: e.activation(out=wrv[:, :, :, 0:32], in_=wv4[:, :, :, 160:192],
                                                                     func=AF.Copy, scale=-1.0),
                     reads=[Rw], writes=[R_wrot])
                P.op("act", lambda e, wv4=wv4, wrv=wrv: e.activation(out=wrv[:, :, :, 32:64], in_=wv4[:, :, :, 128:160],
                                                                     func=AF.Copy),
                     reads=[Rw], writes=[R_wrot])
                for hh in range(4):
                    bk = palloc()
                    P.group("pe", [MM(ps[bk][:], wv4[:, kc, hh, 0:128], cqn[:, kc, :], kc == 0, kc == 3)
                                   for kc in range(4)], reads=[Rw, R_cqn], writes=[R_ps[bk]])
                    P.op("act", lambda e, bk=bk, hh=hh: e.activation(out=qn[:, hh, :], in_=ps[bk][:], func=AF.Copy),
                         reads=[R_ps[bk]], writes=[R_qn])
                    pfree(bk)
                    b1 = palloc(); b2 = palloc()
                    P.group("pe", [MM(ps[b1][0:64, :], wv4[:, kc, hh, 128:192], cqn[:, kc, :], kc == 0, kc == 3)
                                   for kc in range(4)], reads=[Rw, R_cqn], writes=[R_ps[b1]])
                    P.group("pe", [MM(ps[b2][0:64, :], wrv[:, kc, hh, :], cqn[:, kc, :], kc == 0, kc == 3)
                                   for kc in range(4)], reads=[R_wrot, R_cqn], writes=[R_ps[b2]])
                    rope_evac(b1, b2, qr[0:64, hh, :], R_qr)
                    pfree(b1); pfree(b2)
                if SL < 2:
                    continue
                for hh in range(4):
                    h = hg * 4 + hh
                    bo = palloc(); bs = palloc()
                    LAG = 2
                    for kk in range(16 + LAG):
                        if kk < 16:
                            kc = kk
                            bk = palloc()
                            pb = kc % 3
                            P.group("pe", [MM(ps[bk][:], KT[:, h, kc * 128:(kc + 1) * 128], qn[:, hh, :], True, False),
                                           MM(ps[bk][:], krT[0:64, kc * 128:(kc + 1) * 128], qr[0:64, hh, :], False, True)],
                                    reads=[R_KT, R_kr, R_qn, R_qr], writes=[R_ps[bk]])
                            P.op("act", lambda e, bk=bk, pb=pb: e.activation(out=PT[pb][:], in_=ps[bk][:], func=AF.Exp,
                                                                             scale=SC_MLA),
                                 reads=[R_ps[bk]], writes=[R_PT[pb]])
                            pfree(bk)
                        if kk >= LAG:
                            kc = kk - LAG
                            pb = kc % 3
                            P.group("pe", [MM(ps[bo][:], Vt[:, kc, h * 128:(h + 1) * 128], PT[pb][:], kc == 0, kc == 15),
                                           MM(ps[bs][:], ones[:], PT[pb][:], kc == 0, kc == 15)],
                                    reads=[R_V, R_PT[pb], R_ones], writes=[R_ps[bo], R_ps[bs]])
                    P.op("dve", lambda e, bs=bs: e.reciprocal(out=T[3][:, 0:TT], in_=ps[bs][:]),
                         reads=[R_ps[bs]], writes=[R_T[3]])
                    P.op("dve", lambda e, bo=bo, h=h: e.tensor_tensor(out=attnT[:, h, :], in0=ps[bo][:], in1=T[3][:, 0:TT],
                                                                      op=ALU.mult),
                         reads=[R_ps[bo], R_T[3]], writes=[R_attn])
                    pfree(bo); pfree(bs)
            if SL < 3:
                continue
            for g in range(4):
                wcv, Rcv = wload(w_in, 0, 16, C_CV + g * 256, 256)
                wcc, Rcc = wload(w_in, 0, 16, C_CC + g * 256, 256)
                wcb, Rcb = wload(w_in, 0, 16, C_CB + g * 256, 256)
                for cc in range(2):
                    i = g * 2 + cc
                    ub = T[4 + (i % 2)]; R_ub = R_T[4 + (i % 2)]
                    bv = palloc(); bc = palloc()
                    P.group("pe", [MM(ps[bv][:], wcv[:, kc, cc * 128:(cc + 1) * 128], hT[:, kc, :], kc == 0, kc == 15)
                                   for kc in range(16)], reads=[Rcv, R_hT], writes=[R_ps[bv]])
                    P.group("pe", [MM(ps[bc][:], wcc[:, kc, cc * 128:(cc + 1) * 128], hT[:, kc, :], kc == 0, kc == 15)
                                   for kc in range(16)], reads=[Rcc, R_hT], writes=[R_ps[bc]])
                    P.op("act", lambda e, bv=bv: e.activation(out=T[0][:, 0:TT], in_=ps[bv][:], func=AF.Copy),
                         reads=[R_ps[bv]], writes=[R_T[0]])
                    P.op("dve", lambda e, bc=bc, ub=ub: e.tensor_tensor(out=ub[:, 1:TT + 1], in0=ps[bc][:], in1=T[0][:, 0:TT],
                                                                        op=ALU.mult),
                         reads=[R_ps[bc], R_T[0]], writes=[R_ub])
                    pfree(bv); pfree(bc)
                    if t == 0:
                        hv = palloc(); hc = palloc()
                        P.group("pe", [MM(ps[hv][:, 0:4], wcv[:, kc, cc * 128:(cc + 1) * 128], hTh[:, kc, :], kc == 0, kc == 15)
                                       for kc in range(16)], reads=[Rcv, R_hTh], writes=[R_ps[hv]])
                        P.group("pe", [MM(ps[hc][:, 0:4], wcc[:, kc, cc * 128:(cc + 1) * 128], hTh[:, kc, :], kc == 0, kc == 15)
                                       for kc in range(16)], reads=[Rcc, R_hTh], writes=[R_ps[hc]])
                        P.op("act", lambda e, hv=hv: e.activation(out=T[1][:, 0:4], in_=ps[hv][:, 0:4], func=AF.Copy),
                             reads=[R_ps[hv]], writes=[R_T[1]])
                        P.op("dve", lambda e, hc=hc, i=i: e.tensor_tensor(out=uh[:, i, :], in0=ps[hc][:, 0:4], in1=T[1][:, 0:4],
                                                                          op=ALU.mult),
                             reads=[R_ps[hc], R_T[1]], writes=[R_uh])
                        pfree(hv); pfree(hc)
                    P.op("act", lambda e, ub=ub, i=i: e.activation(out=ub[:, 0:1], in_=ucar[:, i:i + 1], func=AF.Copy),
                         reads=[R_ucar], writes=[R_ub])
                    rsrc = uh[:, i, t:t + 1] if t < NT - 1 else zero1[:, 0:1]
                    P.op("act", lambda e, ub=ub, rsrc=rsrc: e.activation(out=ub[:, TT + 1:TT + 2], in_=rsrc, func=AF.Copy),
                         reads=[R_uh], writes=[R_ub])
                    P.op("act", lambda e, ub=ub, i=i: e.activation(out=ucar[:, i:i + 1], in_=ub[:, TT:TT + 1], func=AF.Copy),
                         reads=[R_ub], writes=[R_ucar])
                    P.op("dve", lambda e, ub=ub, i=i: e.tensor_scalar(out=T[1][:, 0:TT], in0=ub[:, 0:TT],
                                                                      scalar1=pT[:, P_CW + i:P_CW + i + 1], scalar2=None,
                                                                      op0=ALU.mult), reads=[R_ub, R_pT], writes=[R_T[1]])
                    P.op("dve", lambda e, ub=ub, i=i: e.scalar_tensor_tensor(out=T[1][:, 0:TT], in0=ub[:, 1:TT + 1],
                                                                             scalar=pT[:, P_CW + 8 + i:P_CW + 9 + i],
                                                                             in1=T[1][:, 0:TT], op0=ALU.mult, op1=ALU.add),
                         reads=[R_ub, R_pT], writes=[R_T[1]])
                    P.op("dve", lambda e, ub=ub, i=i: e.scalar_tensor_tensor(out=T[1][:, 0:TT], in0=ub[:, 2:TT + 2],
                                                                             scalar=pT[:, P_CW + 16 + i:P_CW + 17 + i],
                                                                             in1=T[1][:, 0:TT], op0=ALU.mult, op1=ALU.add),
                         reads=[R_ub, R_pT], writes=[R_T[1]])
                    bb = palloc()
                    P.group("pe", [MM(ps[bb][:], wcb[:, kc, cc * 128:(cc + 1) * 128], hT[:, kc, :], kc == 0, kc == 15)
                                   for kc in range(16)], reads=[Rcb, R_hT], writes=[R_ps[bb]])
                    P.op("dve", lambda e, bb=bb, i=i: e.tensor_tensor(out=convT[:, i, :], in0=ps[bb][:], in1=T[1][:, 0:TT],
                                                                      op=ALU.mult),
                         reads=[R_ps[bb], R_T[1]], writes=[R_conv])
                    pfree(bb)
            if SL < 4:
                continue
            qx = [qn, qr]
            R_qx = [R_qn, R_qr]
            for hx in range(4):
                wv, Rw = wload(w_in, 0, 16, C_QX + hx * 256, 256)
                for dc in range(2):
                    bk = palloc()
                    P.group("pe", [MM(ps[bk][:], wv[:, kc, dc * 128:(dc + 1) * 128], hT[:, kc, :], kc == 0, kc == 15)
                                   for kc in range(16)], reads=[Rw, R_hT], writes=[R_ps[bk]])
                    P.op("act", lambda e, bk=bk, hx=hx, dc=dc: e.activation(out=qx[dc][:, hx, :], in_=ps[bk][:], func=AF.Copy),
                         reads=[R_ps[bk]], writes=[R_qx[dc]])
                    pfree(bk)
            for hx in range(4):
                for mc in range(2):
                    bk = palloc()
                    P.group("pe", [MM(ps[bk][:], KmT[:, hx * 2 + dc, mc * 128:(mc + 1) * 128], qx[dc][:, hx, :], dc == 0, dc == 1)
                                   for dc in range(2)], reads=[R_Km, R_qn, R_qr], writes=[R_ps[bk]])
                    P.op("act", lambda e, bk=bk, mc=mc: e.activation(out=PT[mc][:], in_=ps[bk][:], func=AF.Exp, scale=SC_MEM),
                         reads=[R_ps[bk]], writes=[R_PT[mc]])
                    pfree(bk)
                bs = palloc()
                P.group("pe", [MM(ps[bs][:], ones[:], PT[mc][:], mc == 0, mc == 1) for mc in range(2)],
                        reads=[R_PT[0], R_PT[1], R_ones], writes=[R_ps[bs]])
                P.op("dve", lambda e, bs=bs: e.reciprocal(out=T[3][:, 0:TT], in_=ps[bs][:]), reads=[R_ps[bs]], writes=[R_T[3]])
                pfree(bs)
                for dv in range(2):
                    bo = palloc()
                    P.group("pe", [MM(ps[bo][:], Vm[:, mc, hx * 256 + dv * 128: hx * 256 + (dv + 1) * 128], PT[mc][:],
                                      mc == 0, mc == 1) for mc in range(2)], reads=[R_Vm, R_PT[0], R_PT[1]],
                            writes=[R_ps[bo]])
                    P.op("dve", lambda e, bo=bo, hx=hx, dv=dv: e.tensor_tensor(out=memT[:, hx * 2 + dv, :], in0=ps[bo][:],
                                                                               in1=T[3][:, 0:TT], op=ALU.mult),
                         reads=[R_ps[bo], R_T[3]], writes=[R_memo])
                    pfree(bo)
            if SL < 5:
                continue
            branches = [(w_o_mla, attnT, R_attn), (w_out_conv, convT, R_conv), (w_o_mem, memT, R_memo)]
            macc = [T[4], T[5]]
            R_macc = [R_T[4], R_T[5]]
            for jp in range(8):
                for br in range(3):
                    wsrc, actv, R_actv = branches[br]
                    wp, Rp = wload(wsrc, 0, 8, jp * 256, 256)
                    wg, Rg = wload(w_in, 0, 16, C_G + br * 2048 + jp * 256, 256)
                    for jj in range(2):
                        j = jp * 2 + jj
                        by = palloc(); bg = palloc()
                        P.group("pe", [MM(ps[by][:], wp[:, kc, jj * 128:(jj + 1) * 128], actv[:, kc, :], kc == 0, kc == 7)
                                       for kc in range(8)], reads=[Rp, R_actv], writes=[R_ps[by]])
                        P.group("pe", [MM(ps[bg][:], wg[:, kc, jj * 128:(jj + 1) * 128], hT[:, kc, :], kc == 0, kc == 15)
                                       for kc in range(16)], reads=[Rg, R_hT], writes=[R_ps[bg]])
                        gcol = P_GB + br * 16 + j
                        P.op("act", lambda e, bg=bg, gcol=gcol: e.activation(out=T[0][:, 0:TT], in_=ps[bg][:], func=AF.Sigmoid,
                                                                             bias=pT[:, gcol:gcol + 1]),
                             reads=[R_ps[bg], R_pT], writes=[R_T[0]])
                        if br == 0:
                            P.op("dve", lambda e, by=by, jj=jj: e.tensor_tensor(out=macc[jj][:, 0:TT], in0=ps[by][:],
                                                                                in1=T[0][:, 0:TT], op=ALU.mult),
                                 reads=[R_ps[by], R_T[0]], writes=[R_macc[jj]])
                        else:
                            P.op("dve", lambda e, by=by: e.tensor_tensor(out=T[1][:, 0:TT], in0=ps[by][:], in1=T[0][:, 0:TT],
                                                                         op=ALU.mult),
                                 reads=[R_ps[by], R_T[0]], writes=[R_T[1]])
                            if br == 1:
                                P.op("dve", lambda e, jj=jj: e.tensor_tensor(out=macc[jj][:, 0:TT], in0=macc[jj][:, 0:TT],
                                                                             in1=T[1][:, 0:TT], op=ALU.add),
                                     reads=[R_T[1]], writes=[R_macc[jj]])
                            else:
                                P.op("dve", lambda e, jj=jj, j=j: e.tensor_tensor(out=mergedT[:, j, :], in0=macc[jj][:, 0:TT],
                                                                                  in1=T[1][:, 0:TT], op=ALU.add),
                                     reads=[R_T[1], R_macc[jj]], writes=R_mg)
                        pfree(by); pfree(bg)
            if SL < 6:
                continue
            load_xt(x, t, sem_xr)
            for cg in range(4):
                wA, RA = wload(w_o, 0, 8, cg * 512, 512)
                wB, RB = wload(w_o, 1024, 8, cg * 512, 512)
                for jt in range(4):
                    bk = palloc()
                    P.group("pe", [MM(ps[bk][:], mergedT[:, kc, jt * 128:(jt + 1) * 128],
                                      (wA[:, kc, :] if kc < 8 else wB[:, kc - 8, :]), kc == 0, kc == 15)
                                   for kc in range(16)], reads=[RA, RB] + R_mg, writes=[R_ps[bk]])
                    P.op("dve", lambda e, bk=bk, jt=jt, cg=cg: e.tensor_tensor(out=xt[:, jt, cg * 512:(cg + 1) * 512],
                                                                               in0=ps[bk][:],
                                                                               in1=xt[:, jt, cg * 512:(cg + 1) * 512], op=ALU.add),
                         reads=[R_ps[bk]], writes=R_xt)
                    pfree(bk)
            dst = x1s[t * TT:(t + 1) * TT, :].rearrange("(j p) d -> p j d", p=128)
            P.dma("sp", lambda e, dst=dst: e.dma_start(out=dst, in_=xt), sem_st, reads=R_xt)

        P.dma("sp", lambda e: e.dma_start(out=gfin, in_=fin.partition_broadcast(128)), sem_misc,
              writes=[R_gfin, R_KT, R_V, R_kr])
        XB = [big[:, 26624:30720].bitcast(F32), big[:, 30720:34816].bitcast(F32),
              qq[:, :, :].rearrange("p a b -> p (a b)").bitcast(F32), KV8[:, :].bitcast(F32)]
        R_XB = [R_b0, R_b1, R_qn, R_qr, R_Km, R_Vm]
        XBUF = [[xt[:, j, :] for j in range(4)], XB]
        R_XBUF = [R_xt, R_XB]
        sem_xb = P.new_dsem()
        SEM_L = [sem_x, sem_xb]

        def load_x1(t, first_extra=()):
            bsel = t % 2
            for j in range(4):
                src = x1s[t * TT + j * 128: t * TT + (j + 1) * 128, :]
                P.dma("sp", lambda e, src=src, dstp=XBUF[bsel][j]: e.dma_start(out=dstp, in_=src), SEM_L[bsel],
                      writes=list(R_XBUF[bsel]) + list(first_extra))

        P.wait_all("sp", [(("dma", sem_st), P.dsem[sem_st])])
        if NT_2 > 0:
            load_x1(0)
            load_halo(x1s)
            norm_T(XBUF[0], R_XBUF[0], 4, P_FFN, hT, R_hT, TT)
            norm_T([xh[:, 0, :]], R_xh, 1, P_FFN, hTh, R_hTh, 4)
        for t in range(NT_2):
            cur = XBUF[t % 2]
            R_cur = R_XBUF[t % 2]
            for g in range(NFC // 2):
                wa, Ra = wload(w_up, 0, 16, g * 256, 256)
                wb, Rb = wload(w_up, 0, 16, DFF + g * 256, 256)
                for cc in range(2):
                    i = g * 2 + cc
                    outs = []
                    for ab, (wv, Rw) in enumerate(((wa, Ra), (wb, Rb))):
                        idx = ab * NFC + i
                        ub = T[ab]; R_ub = R_T[ab]
                        cv = T[2 + ab]; R_cv = R_T[2 + ab]
                        bk = palloc()
                        P.group("pe", [MM(ps[bk][:], wv[:, kc, cc * 128:(cc + 1) * 128], hT[:, kc, :], kc == 0, kc == 15)
                                       for kc in range(16)], reads=[Rw, R_hT], writes=[R_ps[bk]])
                        P.op("act", lambda e, bk=bk, ub=ub: e.activation(out=ub[:, 1:TT + 1], in_=ps[bk][:], func=AF.Copy),
                             reads=[R_ps[bk]], writes=[R_ub])
                        pfree(bk)
                        if t == 0:
                            hb_ = palloc()
                            P.group("pe", [MM(ps[hb_][:, 0:4], wv[:, kc, cc * 128:(cc + 1) * 128], hTh[:, kc, :], kc == 0, kc == 15)
                                           for kc in range(16)], reads=[Rw, R_hTh], writes=[R_ps[hb_]])
                            P.op("act", lambda e, hb_=hb_, idx=idx: e.activation(out=fh[:, idx, :], in_=ps[hb_][:, 0:4], func=AF.Copy),
                                 reads=[R_ps[hb_]], writes=[R_fh])
                            pfree(hb_)
                        P.op("act", lambda e, ub=ub, idx=idx: e.activation(out=ub[:, 0:1], in_=fcar[:, idx:idx + 1], func=AF.Copy),
                             reads=[R_fcar], writes=[R_ub])
                        rsrc = fh[:, idx, t:t + 1] if t < NT - 1 else zero1[:, 0:1]
                        P.op("act", lambda e, ub=ub, rsrc=rsrc: e.activation(out=ub[:, TT + 1:TT + 2], in_=rsrc, func=AF.Copy),
                             reads=[R_fh], writes=[R_ub])
                        P.op("act", lambda e, ub=ub, idx=idx: e.activation(out=fcar[:, idx:idx + 1], in_=ub[:, TT:TT + 1], func=AF.Copy),
                             reads=[R_ub], writes=[R_fcar])
                        c0 = P_FCW + ab * NFC + i
                        P.op("dve", lambda e, ub=ub, cv=cv, c0=c0: e.tensor_scalar(out=cv[:, 0:TT], in0=ub[:, 0:TT],
                                                                                   scalar1=pT[:, c0:c0 + 1], scalar2=None,
                                                                                   op0=ALU.mult), reads=[R_ub, R_pT], writes=[R_cv])
                        P.op("dve", lambda e, ub=ub, cv=cv, c0=c0: e.scalar_tensor_tensor(out=cv[:, 0:TT], in0=ub[:, 1:TT + 1],
                                                                                          scalar=pT[:, c0 + 88:c0 + 89],
                                                                                          in1=cv[:, 0:TT], op0=ALU.mult, op1=ALU.add),
                             reads=[R_ub, R_pT], writes=[R_cv])
                        P.op("dve", lambda e, ub=ub, cv=cv, c0=c0: e.scalar_tensor_tensor(out=cv[:, 0:TT], in0=ub[:, 2:TT + 2],
                                                                                          scalar=pT[:, c0 + 176:c0 + 177],
                                                                                          in1=cv[:, 0:TT], op0=ALU.mult, op1=ALU.add),
                             reads=[R_ub, R_pT], writes=[R_cv])
                    P.op("act", lambda e: e.activation(out=T[4][:, 0:TT], in_=T[2][:, 0:TT], func=AF.Silu),
                         reads=[R_T[2]], writes=[R_T[4]])
                    P.op("dve", lambda e, i=i: e.tensor_tensor(out=actT[:, i, :], in0=T[4][:, 0:TT], in1=T[3][:, 0:TT], op=ALU.mult),
                         reads=[R_T[4], R_T[3]], writes=[R_act[i], R_KT, R_V])
            KG = [(0, 8), (8, 8), (16, 8), (24, 8), (32, 8), (40, 4)]
            for cgp in range(4):
                bks = [palloc() for _ in range(4)]
                for (k0, kn) in KG:
                    wv, Rw = wload(w_down, k0 * 128, kn, cgp * 512, 512)
                    for jt in range(4):
                        P.group("pe", [MM(ps[bks[jt]][:], actT[:, k0 + kk, jt * 128:(jt + 1) * 128], wv[:, kk, :],
                                          (k0 + kk) == 0, (k0 + kk) == NFC - 1) for kk in range(kn)],
                                reads=[Rw] + [R_act[k0 + kk] for kk in range(kn)], writes=[R_ps[bks[jt]]])
                for jt in range(4):
                    P.op("dve", lambda e, jt=jt, cgp=cgp, bk=bks[jt], cur=cur: e.tensor_tensor(
                        out=cur[jt][:, cgp * 512:(cgp + 1) * 512], in0=ps[bk][:], in1=cur[jt][:, cgp * 512:(cgp + 1) * 512],
                        op=ALU.add), reads=[R_ps[bks[jt]]], writes=R_cur)
                    pfree(bks[jt])
                if cgp == 0 and t + 1 < NT_2:
                    load_x1(t + 1, first_extra=([R_V, R_kr] if t == 0 else ()))
                    norm_T(XBUF[(t + 1) % 2], R_XBUF[(t + 1) % 2], 4, P_FFN, hT, R_hT, TT)
            for j in range(4):
                P.op("act", lambda e, j=j, cur=cur: e.activation(out=junk[:], in_=cur[j], func=AF.Square,
                                                         accum_out=small[:, j:j + 1]), reads=R_cur, writes=[R_junk, R_small])
            P.op("act", lambda e: e.activation(out=small[:, 5:9], in_=small[:, 0:4], func=AF.Ln, scale=1.0 / D, bias=eps_ap),
                 writes=[R_small])
            P.op("act", lambda e: e.activation(out=small[:, 5:9], in_=small[:, 5:9], func=AF.Exp, scale=-0.5), writes=[R_small])
            for j in range(4):
                P.op("dve", lambda e, j=j, cur=cur: e.scalar_tensor_tensor(out=cur[j], in0=cur[j], scalar=small[:, 5 + j:6 + j],
                                                                   in1=gfin, op0=ALU.mult, op1=ALU.mult),
                     reads=[R_small, R_gfin], writes=R_cur)
            for j in range(4):
                dst = y[t * TT + j * 128: t * TT + (j + 1) * 128, :]
                P.dma("sp", lambda e, dst=dst, j=j, cur=cur: e.dma_start(out=dst, in_=cur[j]), sem_st, reads=R_cur)
        P.wait_all("sp", [(("dma", i), P.dsem[i]) for i in range(len(P.dsem))])
        P.q["sp"].append(("op", lambda e: e.nop(), False))
        P.run(st)
    return nc


_NC_CACHE = {}


def kernel(**inputs):
    f32 = np.float32
    if "nc" not in _NC_CACHE:
        _NC_CACHE["nc"] = build_nc()
    nc = _NC_CACHE["nc"]

    def a(name):
        return np.ascontiguousarray(np.asarray(inputs[name]))

    x = a("x").astype(f32, copy=False)
    mem = a("mem").astype(f32, copy=False)
    pos = a("positions").astype(np.int32, copy=False)
    B = x.shape[0]
    prm = np.concatenate([a(n).astype(f32, copy=False).reshape(-1) for n in
                          ("mix_norm", "ffn_norm", "mem_norm", "q_norm", "kv_norm", "gate_bias", "conv_w", "ffn_conv_w")]
                         ).reshape(NPRM, 128)
    cst = np.zeros((128, 130), f32)
    cst[:, :128] = np.eye(128, dtype=f32)
    invf = np.power(f32(10000.0), -np.arange(0, 64, 2, dtype=f32) / f32(64)).astype(f32)
    cst[0:32, 128] = invf
    cst[32:64, 128] = invf
    shared = {
        "w_in": a("w_in")[0], "w_uq": a("w_uq")[0], "w_ukv": a("w_ukv")[0], "w_o_mla": a("w_o_mla")[0],
        "w_out_conv": a("w_out_conv")[0], "w_mem_kv": a("w_mem_kv")[0], "w_o_mem": a("w_o_mem")[0],
        "w_o": a("w_o")[0], "w_up": a("w_up")[0], "w_down": a("w_down")[0],
        "prm": prm, "final_norm": a("final_norm").reshape(1, D), "cst": cst,
    }
    in_maps = []
    for c in range(B):
        m = dict(shared)
        m["x"] = x[c]
        m["mem"] = mem[c]
        m["positions"] = pos[c].reshape(1, S)
        in_maps.append(m)
    res = run_bass_kernel_spmd(nc, in_maps, core_ids=list(range(B)))
    return np.stack([r["y"] for r in res.results], axis=0).astype(f32, copy=False)
```

```python
import math
from contextlib import ExitStack
import numpy as np
import concourse.bass as bass
import concourse.mybir as mybir
from concourse.bass_utils import run_bass_kernel_spmd

F32 = mybir.dt.float32
BF16 = mybir.dt.bfloat16
I32 = mybir.dt.int32
AF = mybir.ActivationFunctionType
ALU = mybir.AluOpType

S = 2048
D = 2048
TT = 512
NT = S // TT
MEM = 256
DFF = 5632
NFC = DFF // 128
IN_COLS = 11328
C_CQ, C_CKV, C_KR, C_CV, C_CB, C_CC, C_QX, C_G = 0, 512, 1024, 1088, 2112, 3136, 4160, 5184
EPS = 1e-6
P_MIX, P_FFN, P_MEMN, P_QN, P_KVN, P_GB, P_CW, P_FCW = 0, 16, 32, 48, 52, 56, 104, 128
NPRM = 392
NS = 4
SLOT = 4096


class Res:
    __slots__ = ("w", "r")

    def __init__(self):
        self.w = {}
        self.r = {}


def _mrg(d, tok):
    k, v = tok
    if d.get(k, 0) < v:
        d[k] = v


class Prog:
    ENG = ("pe", "act", "dve", "pool", "sp")

    def __init__(self, nc):
        self.nc = nc
        self.q = {e: [] for e in self.ENG}
        self.cnt = {e: 0 for e in self.ENG}
        self.dsem = []
        self.seen = {e: {} for e in self.ENG}

    def _wait(self, eng, key, val):
        if key == eng and eng == "pe":
            return
        if self.seen[eng].get(key, 0) >= val:
            return
        self.seen[eng][key] = val
        self.q[eng].append(("wait", key, val))

    def _deps(self, eng, reads, writes):
        for r in reads:
            for k, v in r.w.items():
                self._wait(eng, k, v)
        for w in writes:
            for k, v in w.w.items():
                self._wait(eng, k, v)
            for k, v in w.r.items():
                self._wait(eng, k, v)

    def _note(self, tok, reads, writes):
        for r in reads:
            _mrg(r.r, tok)
        for w in writes:
            w.w = {tok[0]: tok[1]}
            w.r = {}

    def op(self, eng, fn, reads=(), writes=()):
        self._deps(eng, reads, writes)
        self.cnt[eng] += 1
        tok = (eng, self.cnt[eng])
        self.q[eng].append(("op", fn, True))
        self._note(tok, reads, writes)
        return tok

    def group(self, eng, fns, reads=(), writes=()):
        self._deps(eng, reads, writes)
        n = len(fns)
        for i, fn in enumerate(fns):
            self.q[eng].append(("op", fn, i == n - 1))
        self.cnt[eng] += 1
        tok = (eng, self.cnt[eng])
        self._note(tok, reads, writes)
        return tok

    def new_dsem(self):
        self.dsem.append(0)
        return len(self.dsem) - 1

    def dma(self, eng, fn, sem, reads=(), writes=()):
        self._deps(eng, reads, writes)
        self.dsem[sem] += 16
        tok = (("dma", sem), self.dsem[sem])
        self.q[eng].append(("dma", fn, sem))
        self._note(tok, reads, writes)
        return tok

    def wait_all(self, eng, toks):
        for t in toks:
            self._wait(eng, t[0], t[1])

    def run(self, stack):
        nc = self.nc
        semh = {}
        for e in self.ENG:
            semh[e] = stack.enter_context(nc.semaphore("s_" + e))
        for i in range(len(self.dsem)):
            semh[("dma", i)] = stack.enter_context(nc.semaphore("d_%d" % i))
        block = stack.enter_context(nc.Block())

        def replay(name):
            def f(engine):
                for item in self.q[name]:
                    if item[0] == "wait":
                        engine.wait_ge(semh[item[1]], item[2])
                    elif item[0] == "op":
                        ins = item[1](engine)
                        if item[2]:
                            ins.then_inc(semh[name], 1)
                    else:
                        ins = item[1](engine)
                        ins.then_inc(semh[("dma", item[2])], 16)
            return f

        block.tensor(replay("pe"))
        block.scalar(replay("act"))
        block.vector(replay("dve"))
        block.gpsimd(replay("pool"))
        block.sync(replay("sp"))


def build_nc(dbg=None):
    import os
    dbg = dbg or {}
    NT_A = dbg.get('A', NT); NT_1 = dbg.get('S1', NT); NT_2 = dbg.get('S2', NT); DO_M = dbg.get('M', 1)
    nc = bass.Bass("TRN2", target_bir_lowering=False)

    def din(name, shape, dt=F32):
        return nc.dram_tensor(name, shape, dt, kind="ExternalInput").ap()

    x = din("x", [S, D])
    mem = din("mem", [MEM, D])
    pos = din("positions", [1, S], I32)
    w_in = din("w_in", [D, IN_COLS])
    w_uq = din("w_uq", [512, 1536])
    w_ukv = din("w_ukv", [512, 2048])
    w_o_mla = din("w_o_mla", [1024, D])
    w_out_conv = din("w_out_conv", [1024, D])
    w_mem_kv = din("w_mem_kv", [D, 2048])
    w_o_mem = din("w_o_mem", [1024, D])
    w_o = din("w_o", [D, D])
    w_up = din("w_up", [D, 2 * DFF])
    w_down = din("w_down", [DFF, D])
    prm = din("prm", [NPRM, 128])
    fin = din("final_norm", [1, D])
    cst = din("cst", [128, 130])
    y = nc.dram_tensor("y", [S, D], F32, kind="ExternalOutput").ap()
    x1s = nc.dram_tensor("x1s", [S, D], F32, kind="ExternalOutput").ap()

    with ExitStack() as st:
        def sb(name, shape, dt):
            return st.enter_context(nc.sbuf_tensor(name, shape, dt))

        P = Prog(nc)
        slots = [sb("slot%d" % i, [128, SLOT], BF16) for i in range(NS)]
        R_slot = [Res() for _ in range(NS)]
        slot_sem = [P.new_dsem() for _ in range(NS)]
        hT = sb("hT", [128, 16, TT], BF16); R_hT = Res()
        hTh = sb("hTh", [128, 16, 4], BF16); R_hTh = Res()
        big = sb("big", [128, 34816], BF16)
        KT = big[:, 0:16384].rearrange("p (h s) -> p h s", h=8)
        Vt = big[:, 16384:32768].rearrange("p (c n) -> p c n", c=16)
        krT = big[:, 32768:34816]
        actT = big[:, 0:NFC * TT].rearrange("p (c n) -> p c n", c=NFC)
        gfin = big[:, 22528:26624].bitcast(F32)
        R_b0 = Res(); R_b1 = Res()
        R_KT = Res(); R_V = Res(); R_kr = Res(); R_gfin = Res()
        R_act = [Res() for _ in range(NFC)]
        KV8 = sb("KV8", [128, 4096], BF16)
        KmT = KV8[:, 0:2048].rearrange("p (c n) -> p c n", c=8); R_Km = Res()
        Vm = KV8[:, 2048:4096].rearrange("p (c n) -> p c n", c=2); R_Vm = Res()
        U = sb("U", [128, 12288], F32)
        xt = U[:, 0:8192].rearrange("p (j d) -> p j d", j=4)
        attnT = U[:, 0:2048].bitcast(BF16).rearrange("p (c n) -> p c n", c=8)
        convT = U[:, 2048:4096].bitcast(BF16).rearrange("p (c n) -> p c n", c=8)
        memT = U[:, 4096:6144].bitcast(BF16).rearrange("p (c n) -> p c n", c=8)
        cqraw = U[:, 6144:8192].rearrange("p (c n) -> p c n", c=4)
        R_attn = Res(); R_conv = Res(); R_memo = Res(); R_cqraw = Res()
        R_xt = [R_attn, R_conv, R_memo, R_cqraw]
        mg = U[:, 8192:12288]
        mergedT = mg.bitcast(BF16).rearrange("p (c n) -> p c n", c=16)
        xn = mg.bitcast(BF16).rearrange("p (j d) -> p j d", j=4)
        cqn = mg[:, 0:1024].bitcast(BF16).rearrange("p (c n) -> p c n", c=4)
        cosT = mg[0:64, 1024:1536]
        sinT = mg[0:64, 1536:2048]
        rtA = mg[0:64, 2048:2560]
        rtB = mg[0:64, 2560:3072]
        posi = mg[0:64, 3072:3584].bitcast(I32)
        R_cqn = Res(); R_cs = Res(); R_rt = Res()
        R_mg = [R_cqn, R_cs, R_rt]
        qq = sb("qq", [128, 8, TT], BF16)
        qn = qq[:, 0:4, :]
        qr = qq[:, 4:8, :]
        R_qn = Res(); R_qr = Res()
        xh = qq[:, :, :].rearrange("p a b -> p (a b)").bitcast(F32).rearrange("p (j d) -> p j d", j=1)
        R_xh = [R_qn, R_qr]
        junk = sb("junk", [128, 2048], BF16); R_junk = Res()
        T = [sb("T%d" % i, [128, 514], F32) for i in range(6)]
        R_T = [Res() for _ in range(6)]
        prow = T[0][:, 0:512].rearrange("p (b c) -> p b c", b=4)
        R_prow = R_T[0]
        PT = [sb("PT%d" % i, [128, TT], BF16) for i in range(4)]
        R_PT = [Res() for _ in range(4)]
        wrot = sb("wrot", [128, 16, 64], BF16); R_wrot = Res()
        pT = sb("pT", [128, NPRM], F32); R_pT = Res()
        idf = sb("idf", [128, 130], F32); R_idf = Res()
        idb = sb("idb", [128, 128], BF16); R_idb = Res()
        ones = sb("ones", [128, 128], BF16); R_ones = Res()
        small = sb("small", [128, 16], F32); R_small = Res()
        ucar = sb("ucar", [128, 8], F32); R_ucar = Res()
        uh = sb("uh", [128, 8, 4], F32); R_uh = Res()
        fcar = sb("fcar", [128, 2 * NFC], F32); R_fcar = Res()
        fh = sb("fh", [128, 2 * NFC, 4], F32); R_fh = Res()
        zero1 = sb("zero1", [128, 4], F32)

        NB = 6
        ps = [st.enter_context(nc.psum_tensor("ps%d" % i, [128, 512], F32)) for i in range(NB)]
        psT = [st.enter_context(nc.psum_tensor("psT%d" % i, [128, 1024], BF16)) for i in range(2)]
        R_ps = [Res() for _ in range(NB)]
        R_psT = [Res(), Res()]
        held = [False] * NB
        rot = [0]

        def palloc():
            for _ in range(NB):
                i = rot[0] % NB
                rot[0] += 1
                if not held[i]:
                    held[i] = True
                    return i
            raise RuntimeError("psum exhausted")

        def pfree(i):
            held[i] = False

        sem_x = P.new_dsem(); sem_misc = P.new_dsem(); sem_st = P.new_dsem(); sem_pos = P.new_dsem()
        sem_xh = P.new_dsem(); sem_xr = P.new_dsem(); sem_idf = P.new_dsem(); sem_prm = P.new_dsem()
        slot_ctr = [0]

        def wload(w, r0, kcn, c0, ncols):
            i = slot_ctr[0] % NS
            slot_ctr[0] += 1
            view = slots[i][:, 0:kcn * ncols].rearrange("p (k n) -> p k n", k=kcn)
            src = w[r0:r0 + kcn * 128, c0:c0 + ncols].rearrange("(k p) n -> p k n", p=128)
            P.dma("pool", lambda e, view=view, src=src: e.dma_start(out=view, in_=src), slot_sem[i],
                  writes=[R_slot[i]])
            return view, R_slot[i]

        def MM(o, l, r, s, t):
            return lambda e: e.matmul(o, l, r, start=s, stop=t)

        P.dma("sp", lambda e: e.dma_start(out=idf[:], in_=cst), sem_idf, writes=[R_idf])
        for b in range(4):
            r0 = b * 128
            n = min(128, NPRM - r0)
            P.dma("sp", lambda e, b=b, r0=r0, n=n: e.dma_start(out=prow[0:n, b, :], in_=prm[r0:r0 + n, :]),
                  sem_prm, writes=[R_prow])
        P.op("dve", lambda e: e.memset(ones[:], 1.0), writes=[R_ones])
        P.op("dve", lambda e: e.memset(small[:], 0.0), writes=[R_small])
        P.op("dve", lambda e: e.memset(small[:, 10:11], EPS), writes=[R_small])
        P.op("dve", lambda e: e.memset(small[:, 11:12], math.pi / 2), writes=[R_small])
        P.op("dve", lambda e: e.memset(zero1[:], 0.0))
        P.op("dve", lambda e: e.memset(ucar[:], 0.0), writes=[R_ucar])
        P.op("dve", lambda e: e.memset(fcar[:], 0.0), writes=[R_fcar])
        P.op("dve", lambda e: e.memset(uh[:], 0.0), writes=[R_uh])
        P.op("dve", lambda e: e.memset(fh[:], 0.0), writes=[R_fh])
        P.op("dve", lambda e: e.tensor_copy(out=idb[:], in_=idf[:, 0:128]), reads=[R_idf], writes=[R_idb])
        for b in range(4):
            n = min(128, NPRM - b * 128)
            bk = palloc()
            P.group("pe", [lambda e, b=b, n=n, bk=bk: e.transpose(ps[bk][:, 0:n], prow[0:n, b, :], idf[0:n, 0:n])],
                    reads=[R_prow, R_idf], writes=[R_ps[bk]])
            P.op("dve", lambda e, b=b, n=n, bk=bk: e.tensor_copy(out=pT[:, b * 128:b * 128 + n], in_=ps[bk][:, 0:n]),
                 reads=[R_ps[bk]], writes=[R_pT])
            pfree(bk)
        eps_ap = small[:, 10:11]
        halfpi = small[:, 11:12]
        invf = idf[0:64, 128:129]

        def norm_T(src, R_src, nj, gbase, dst, R_dst, ncopy):
            for j in range(nj):
                P.op("act", lambda e, j=j: e.activation(out=junk[:], in_=src[j], func=AF.Square,
                                                         accum_out=small[:, j:j + 1]),
                     reads=R_src, writes=[R_junk, R_small])
            P.op("act", lambda e: e.activation(out=small[:, 5:5 + nj], in_=small[:, 0:nj], func=AF.Ln,
                                               scale=1.0 / D, bias=eps_ap), reads=[], writes=[R_small])
            P.op("act", lambda e: e.activation(out=small[:, 5:5 + nj], in_=small[:, 5:5 + nj], func=AF.Exp,
                                               scale=-0.5), writes=[R_small])
            for j in range(nj):
                P.op("dve", lambda e, j=j: e.tensor_scalar(out=xn[:, j, :], in0=src[j],
                                                            scalar1=small[:, 5 + j:6 + j], scalar2=None,
                                                            op0=ALU.mult),
                     reads=list(R_src) + [R_small], writes=R_mg)
            for kc in range(16):
                hb = kc % 2
                fns = [(lambda e, j=j, kc=kc, hb=hb: e.transpose(psT[hb][:, j * 128:(j + 1) * 128],
                                                                 xn[:, j, kc * 128:(kc + 1) * 128], idb[:]))
                       for j in range(nj)]
                P.group("pe", fns, reads=R_mg + [R_idb], writes=[R_psT[hb]])
                P.op("dve", lambda e, kc=kc, hb=hb: e.tensor_scalar(out=dst[:, kc, 0:ncopy],
                                                                    in0=psT[hb][:, 0:ncopy],
                                                                    scalar1=pT[:, gbase + kc: gbase + kc + 1],
                                                                    scalar2=None, op0=ALU.mult),
                     reads=[R_psT[hb], R_pT], writes=[R_dst])

        def load_xt(srcd, t, sem):
            src = srcd[t * TT:(t + 1) * TT, :].rearrange("(j p) d -> p j d", p=128)
            P.dma("sp", lambda e, src=src: e.dma_start(out=xt, in_=src), sem, writes=R_xt)

        def load_halo(srcd):
            P.op("dve", lambda e: e.memset(xh, 0.0), writes=R_xh)
            for i in range(3):
                P.dma("sp", lambda e, i=i: e.dma_start(out=xh[i:i + 1, 0, :], in_=srcd[(i + 1) * TT:(i + 1) * TT + 1, :]),
                      sem_xh, writes=R_xh)

        def rope_tables(t):
            RL = dbg.get('RL', 9)
            P.dma("sp", lambda e: e.dma_start(out=posi, in_=pos[0:1, t * TT:(t + 1) * TT].partition_broadcast(64)),
                  sem_pos, writes=[R_rt])
            if RL < 1:
                return
            P.op("dve", lambda e: e.tensor_copy(out=rtA, in_=posi), reads=[R_rt], writes=[R_rt])
            P.op("dve", lambda e: e.tensor_scalar(out=rtA, in0=rtA, scalar1=invf, scalar2=None, op0=ALU.mult),
                 reads=[R_rt, R_idf], writes=[R_rt])
            P.op("dve", lambda e: e.tensor_scalar(out=rtB, in0=rtA, scalar1=1.0 / (2 * math.pi), scalar2=None,
                                                  op0=ALU.mult), reads=[R_rt], writes=[R_rt])
            if RL < 2:
                return
            P.op("dve", lambda e: e.tensor_copy(out=posi, in_=rtB), reads=[R_rt], writes=[R_rt])
            P.op("dve", lambda e: e.tensor_copy(out=rtB, in_=posi), reads=[R_rt], writes=[R_rt])
            P.op("dve", lambda e: e.scalar_tensor_tensor(out=rtA, in0=rtB, scalar=-2 * math.pi, in1=rtA,
                                                         op0=ALU.mult, op1=ALU.add), reads=[R_rt], writes=[R_rt])
            P.op("dve", lambda e: e.tensor_scalar(out=rtA, in0=rtA, scalar1=-3.141592, scalar2=3.141592,
                                                  op0=ALU.max, op1=ALU.min), reads=[R_rt], writes=[R_rt])
            if RL < 3:
                return
            P.op("act", lambda e: e.activation(out=sinT, in_=rtA, func=AF.Sin), reads=[R_rt], writes=[R_cs])
            if RL < 4:
                return
            P.op("act", lambda e: e.activation(out=rtB, in_=rtA, func=AF.Abs), reads=[R_rt], writes=[R_rt])
            if RL < 5:
                return
            P.op("act", lambda e: e.activation(out=cosT, in_=rtB, func=AF.Sin, scale=-1.0, bias=halfpi[0:64, :]),
                 reads=[R_rt, R_small], writes=[R_cs])

        def rms_bcast(bk_ss, n, dstT, R_dstT):
            P.op("act", lambda e: e.activation(out=dstT[:, 0:TT], in_=ps[bk_ss][:], func=AF.Ln, scale=1.0 / n,
                                               bias=eps_ap), reads=[R_ps[bk_ss], R_small], writes=[R_dstT])
            P.op("act", lambda e: e.activation(out=dstT[:, 0:TT], in_=dstT[:, 0:TT], func=AF.Exp, scale=-0.5),
                 writes=[R_dstT])

        def latent_norm(wcols, gbase, dst, R_dst):
            LL = dbg.get('LL', 9)
            bss = palloc()
            for g in range(2):
                wv, Rw = wload(w_in, 0, 16, wcols + g * 256, 256)
                if LL < 1:
                    continue
                for cc in range(2):
                    c = g * 2 + cc
                    bk = palloc()
                    P.group("pe", [MM(ps[bk][:], wv[:, kc, cc * 128:(cc + 1) * 128], hT[:, kc, :], kc == 0, kc == 15)
                                   for kc in range(16)], reads=[Rw, R_hT], writes=[R_ps[bk]])
                    LV = dbg.get('LV', 3)
                    if LL >= 2 and (LV & 2):
                        P.op("dve", lambda e, bk=bk, c=c: e.tensor_copy(out=cqraw[:, c, :], in_=ps[bk][:]),
                             reads=[R_ps[bk]], writes=[R_cqraw])
                    if LL >= 2 and (LV & 1):
                        P.op("act", lambda e, c=c: e.activation(out=PT[0][:], in_=cqraw[:, c, :], func=AF.Square),
                             reads=[R_cqraw], writes=[R_PT[0]])
                    pfree(bk)
                    if LL >= 3:
                        P.group("pe", [MM(ps[bss][:], ones[:], PT[0][:], c == 0, c == 3)], reads=[R_PT[0], R_ones],
                                writes=[R_ps[bss]])
            if LL >= 4:
                rms_bcast(bss, 512, T[0], R_T[0])
            pfree(bss)
            if LL < 5:
                return
            for c in range(4):
                P.op("dve", lambda e, c=c: e.scalar_tensor_tensor(out=dst[:, c, :], in0=cqraw[:, c, :],
                                                                   scalar=pT[:, gbase + c:gbase + c + 1],
                                                                   in1=T[0][:, 0:TT], op0=ALU.mult, op1=ALU.mult),
                     reads=[R_cqraw, R_pT, R_T[0]], writes=[R_dst])

        def rope_evac(bk_raw, bk_rot, dst_ap, R_dst):
            P.op("dve", lambda e: e.tensor_tensor(out=T[1][0:64, 0:TT], in0=ps[bk_raw][0:64, :], in1=cosT, op=ALU.mult),
                 reads=[R_ps[bk_raw], R_cs], writes=[R_T[1]])
            P.op("dve", lambda e: e.tensor_tensor(out=T[2][0:64, 0:TT], in0=ps[bk_rot][0:64, :], in1=sinT, op=ALU.mult),
                 reads=[R_ps[bk_rot], R_cs], writes=[R_T[2]])
            P.op("dve", lambda e: e.tensor_tensor(out=dst_ap, in0=T[1][0:64, 0:TT], in1=T[2][0:64, 0:TT], op=ALU.add),
                 reads=[R_T[1], R_T[2]], writes=[R_dst])

        P.dma("sp", lambda e: e.dma_start(out=xt[:, 0:2, :], in_=mem.rearrange("(j p) d -> p j d", p=128)),
              sem_x, writes=R_xt)
        norm_T([xt[:, j, :] for j in range(2)], R_xt, 2, P_MEMN, hT, R_hT, MEM)
        for g in range(8):
            wv, Rw = wload(w_mem_kv, 0, 16, g * 256, 256)
            if g < 4:
                for cc in range(2):
                    c = g * 2 + cc
                    bk = palloc()
                    P.group("pe", [MM(ps[bk][:, 0:MEM], wv[:, kc, cc * 128:(cc + 1) * 128], hT[:, kc, 0:MEM],
                                      kc == 0, kc == 15) for kc in range(16)], reads=[Rw, R_hT], writes=[R_ps[bk]])
                    P.op("dve", lambda e, bk=bk, c=c: e.tensor_copy(out=KmT[:, c, :], in_=ps[bk][:, 0:MEM]),
                         reads=[R_ps[bk]], writes=[R_Km])
                    pfree(bk)
            else:
                c0 = (g - 4) * 256
                for mc in range(2):
                    bk = palloc()
                    P.group("pe", [MM(ps[bk][:, 0:256], hT[:, kc, mc * 128:(mc + 1) * 128], wv[:, kc, :],
                                      kc == 0, kc == 15) for kc in range(16)], reads=[Rw, R_hT], writes=[R_ps[bk]])
                    P.op("dve", lambda e, bk=bk, mc=mc, c0=c0: e.tensor_copy(out=Vm[:, mc, c0:c0 + 256],
                                                                             in_=ps[bk][:, 0:256]),
                         reads=[R_ps[bk]], writes=[R_Vm])
                    pfree(bk)

        for t in range(NT_A):
            AL = dbg.get('AL', 9)
            load_xt(x, t, sem_x)
            norm_T([xt[:, j, :] for j in range(4)], R_xt, 4, P_MIX, hT, R_hT, TT)
            if AL < 1:
                continue
            rope_tables(t)
            if AL < 2:
                continue
            latent_norm(C_CKV, P_KVN, cqn, R_cqn)
            if AL < 3:
                continue
            wv, Rw = wload(w_in, 0, 16, C_KR, 64)
            P.op("act", lambda e, wv=wv: e.activation(out=wrot[:, :, 0:32], in_=wv[:, :, 32:64], func=AF.Copy, scale=-1.0),
                 reads=[Rw], writes=[R_wrot])
            P.op("act", lambda e, wv=wv: e.activation(out=wrot[:, :, 32:64], in_=wv[:, :, 0:32], func=AF.Copy),
                 reads=[Rw], writes=[R_wrot])
            b1 = palloc(); b2 = palloc()
            P.group("pe", [MM(ps[b1][0:64, :], wv[:, kc, :], hT[:, kc, :], kc == 0, kc == 15) for kc in range(16)],
                    reads=[Rw, R_hT], writes=[R_ps[b1]])
            P.group("pe", [MM(ps[b2][0:64, :], wrot[:, kc, :], hT[:, kc, :], kc == 0, kc == 15) for kc in range(16)],
                    reads=[R_wrot, R_hT], writes=[R_ps[b2]])
            rope_evac(b1, b2, krT[0:64, t * TT:(t + 1) * TT], R_kr)
            pfree(b1); pfree(b2)
            if AL < 4:
                continue
            for half in range(2):
                wv, Rw = wload(w_ukv, 0, 4, half * 1024, 1024)
                for hh in range(4):
                    h = half * 4 + hh
                    bk = palloc()
                    P.group("pe", [MM(ps[bk][:], wv[:, kc, hh * 256: hh * 256 + 128], cqn[:, kc, :], kc == 0, kc == 3)
                                   for kc in range(4)], reads=[Rw, R_cqn], writes=[R_ps[bk]])
                    P.op("act", lambda e, bk=bk, h=h, t=t: e.activation(out=KT[:, h, t * TT:(t + 1) * TT], in_=ps[bk][:],
                                                                        func=AF.Copy),
                         reads=[R_ps[bk]], writes=[R_KT])
                    pfree(bk)
                wv4 = wv.rearrange("p k (h c) -> p k h c", h=4)
                for j in range(4):
                    bk = palloc()
                    P.group("pe", [MM(ps[bk][:].rearrange("p (h c) -> p h c", h=4), cqn[:, kc, j * 128:(j + 1) * 128],
                                      wv4[:, kc, :, 128:256], kc == 0, kc == 3) for kc in range(4)],
                            reads=[Rw, R_cqn], writes=[R_ps[bk]])
                    P.op("dve", lambda e, bk=bk, j=j, t=t, half=half: e.tensor_copy(
                        out=Vt[:, t * 4 + j, half * 512:(half + 1) * 512], in_=ps[bk][:]),
                         reads=[R_ps[bk]], writes=[R_V])
                    pfree(bk)

        SC_MLA = 192.0 ** -0.5
        SC_MEM = 256.0 ** -0.5
        for t in range(NT_1):
            load_xt(x, t, sem_x)
            if t == 0:
                load_halo(x)
            norm_T([xt[:, j, :] for j in range(4)], R_xt, 4, P_MIX, hT, R_hT, TT)
            if t == 0:
                norm_T([xh[:, 0, :]], R_xh, 1, P_MIX, hTh, R_hTh, 4)
            rope_tables(t)
            latent_norm(C_CQ, P_QN, cqn, R_cqn)
            SL = dbg.get('SL', 9)
            if SL < 1:
                continue
            for hg in range(2):
                wv, Rw = wload(w_uq, 0, 4, hg * 768, 768)
                wv4 = wv.rearrange("p k (h c) -> p k h c", h=4)
                wrv = wrot[:, :, :].rearrange("p k c -> p (k c)")[:, 0:1024].rearrange("p (k h c) -> p k h c", k=4, h=4)
                P.op("act", lambda e, wv4=wv4, wrv=wrv: e.activation(out=wrv[:, :, :, 0:32], in_=wv4[:, :, :, 160:192],
                                                                     func=AF.Copy, scale=-1.0),
                     reads=[Rw], writes=[R_wrot])
                P.op("act", lambda e, wv4=wv4, wrv=wrv: e.activation(out=wrv[:, :, :, 32:64], in_=wv4[:, :, :, 128:160],
                                                                     func=AF.Copy),
                     reads=[Rw], writes=[R_wrot])
                for hh in range(4):
                    bk = palloc()
                    P.group("pe", [MM(ps[bk][:], wv4[:, kc, hh, 0:128], cqn[:, kc, :], kc == 0, kc == 3)
                                   for kc in range(4)], reads=[Rw, R_cqn], writes=[R_ps[bk]])
                    P.op("act", lambda e, bk=bk, hh=hh: e.activation(out=qn[:, hh, :], in_=ps[bk][:], func=AF.Copy),
                         reads=[R_ps[bk]], writes=[R_qn])
                    pfree(bk)
                    b1 = palloc(); b2 = palloc()
                    P.group("pe", [MM(ps[b1][0:64, :], wv4[:, kc, hh, 128:192], cqn[:, kc, :], kc == 0, kc == 3)
                                   for kc in range(4)], reads=[Rw, R_cqn], writes=[R_ps[b1]])
                    P.group("pe", [MM(ps[b2][0:64, :], wrv[:, kc, hh, :], cqn[:, kc, :], kc == 0, kc == 3)
                                   for kc in range(4)], reads=[R_wrot, R_cqn], writes=[R_ps[b2]])
                    rope_evac(b1, b2, qr[0:64, hh, :], R_qr)
                    pfree(b1); pfree(b2)
                if SL < 2:
                    continue
                for hh in range(4):
                    h = hg * 4 + hh
                    bo = palloc(); bs = palloc()
                    LAG = 3
                    for kk in range(16 + LAG):
                        if kk < 16:
                            kc = kk
                            bk = palloc()
                            pb = kc % 4
                            P.group("pe", [MM(ps[bk][:], KT[:, h, kc * 128:(kc + 1) * 128], qn[:, hh, :], True, False),
                                           MM(ps[bk][:], krT[0:64, kc * 128:(kc + 1) * 128], qr[0:64, hh, :], False, True)],
                                    reads=[R_KT, R_kr, R_qn, R_qr], writes=[R_ps[bk]])
                            P.op("act", lambda e, bk=bk, pb=pb: e.activation(out=PT[pb][:], in_=ps[bk][:], func=AF.Exp,
                                                                             scale=SC_MLA),
                                 reads=[R_ps[bk]], writes=[R_PT[pb]])
                            pfree(bk)
                        if kk >= LAG:
                            kc = kk - LAG
                            pb = kc % 4
                            P.group("pe", [MM(ps[bo][:], Vt[:, kc, h * 128:(h + 1) * 128], PT[pb][:], kc == 0, kc == 15),
                                           MM(ps[bs][:], ones[:], PT[pb][:], kc == 0, kc == 15)],
                                    reads=[R_V, R_PT[pb], R_ones], writes=[R_ps[bo], R_ps[bs]])
                    P.op("dve", lambda e, bs=bs: e.reciprocal(out=T[3][:, 0:TT], in_=ps[bs][:]),
                         reads=[R_ps[bs]], writes=[R_T[3]])
                    P.op("dve", lambda e, bo=bo, h=h: e.tensor_tensor(out=attnT[:, h, :], in0=ps[bo][:], in1=T[3][:, 0:TT],
                                                                      op=ALU.mult),
                         reads=[R_ps[bo], R_T[3]], writes=[R_attn])
                    pfree(bo); pfree(bs)
            if SL < 3:
                continue
            for g in range(4):
                wcv, Rcv = wload(w_in, 0, 16, C_CV + g * 256, 256)
                wcc, Rcc = wload(w_in, 0, 16, C_CC + g * 256, 256)
                wcb, Rcb = wload(w_in, 0, 16, C_CB + g * 256, 256)
                for cc in range(2):
                    i = g * 2 + cc
                    ub = T[4 + (i % 2)]; R_ub = R_T[4 + (i % 2)]
                    bv = palloc(); bc = palloc()
                    P.group("pe", [MM(ps[bv][:], wcv[:, kc, cc * 128:(cc + 1) * 128], hT[:, kc, :], kc == 0, kc == 15)
                                   for kc in range(16)], reads=[Rcv, R_hT], writes=[R_ps[bv]])
                    P.group("pe", [MM(ps[bc][:], wcc[:, kc, cc * 128:(cc + 1) * 128], hT[:, kc, :], kc == 0, kc == 15)
                                   for kc in range(16)], reads=[Rcc, R_hT], writes=[R_ps[bc]])
                    P.op("act", lambda e, bv=bv: e.activation(out=T[0][:, 0:TT], in_=ps[bv][:], func=AF.Copy),
                         reads=[R_ps[bv]], writes=[R_T[0]])
                    P.op("dve", lambda e, bc=bc, ub=ub: e.tensor_tensor(out=ub[:, 1:TT + 1], in0=ps[bc][:], in1=T[0][:, 0:TT],
                                                                        op=ALU.mult),
                         reads=[R_ps[bc], R_T[0]], writes=[R_ub])
                    pfree(bv); pfree(bc)
                    if t == 0:
                        hv = palloc(); hc = palloc()
                        P.group("pe", [MM(ps[hv][:, 0:4], wcv[:, kc, cc * 128:(cc + 1) * 128], hTh[:, kc, :], kc == 0, kc == 15)
                                       for kc in range(16)], reads=[Rcv, R_hTh], writes=[R_ps[hv]])
                        P.group("pe", [MM(ps[hc][:, 0:4], wcc[:, kc, cc * 128:(cc + 1) * 128], hTh[:, kc, :], kc == 0, kc == 15)
                                       for kc in range(16)], reads=[Rcc, R_hTh], writes=[R_ps[hc]])
                        P.op("act", lambda e, hv=hv: e.activation(out=T[1][:, 0:4], in_=ps[hv][:, 0:4], func=AF.Copy),
                             reads=[R_ps[hv]], writes=[R_T[1]])
                        P.op("dve", lambda e, hc=hc, i=i: e.tensor_tensor(out=uh[:, i, :], in0=ps[hc][:, 0:4], in1=T[1][:, 0:4],
                                                                          op=ALU.mult),
                             reads=[R_ps[hc], R_T[1]], writes=[R_uh])
                        pfree(hv); pfree(hc)
                    P.op("act", lambda e, ub=ub, i=i: e.activation(out=ub[:, 0:1], in_=ucar[:, i:i + 1], func=AF.Copy),
                         reads=[R_ucar], writes=[R_ub])
                    rsrc = uh[:, i, t:t + 1] if t < NT - 1 else zero1[:, 0:1]
                    P.op("act", lambda e, ub=ub, rsrc=rsrc: e.activation(out=ub[:, TT + 1:TT + 2], in_=rsrc, func=AF.Copy),
                         reads=[R_uh], writes=[R_ub])
                    P.op("act", lambda e, ub=ub, i=i: e.activation(out=ucar[:, i:i + 1], in_=ub[:, TT:TT + 1], func=AF.Copy),
                         reads=[R_ub], writes=[R_ucar])
                    P.op("dve", lambda e, ub=ub, i=i: e.tensor_scalar(out=T[1][:, 0:TT], in0=ub[:, 0:TT],
                                                                      scalar1=pT[:, P_CW + i:P_CW + i + 1], scalar2=None,
                                                                      op0=ALU.mult), reads=[R_ub, R_pT], writes=[R_T[1]])
                    P.op("dve", lambda e, ub=ub, i=i: e.scalar_tensor_tensor(out=T[1][:, 0:TT], in0=ub[:, 1:TT + 1],
                                                                             scalar=pT[:, P_CW + 8 + i:P_CW + 9 + i],
                                                                             in1=T[1][:, 0:TT], op0=ALU.mult, op1=ALU.add),
                         reads=[R_ub, R_pT], writes=[R_T[1]])
                    P.op("dve", lambda e, ub=ub, i=i: e.scalar_tensor_tensor(out=T[1][:, 0:TT], in0=ub[:, 2:TT + 2],
                                                                             scalar=pT[:, P_CW + 16 + i:P_CW + 17 + i],
                                                                             in1=T[1][:, 0:TT], op0=ALU.mult, op1=ALU.add),
                         reads=[R_ub, R_pT], writes=[R_T[1]])
                    bb = palloc()
                    P.group("pe", [MM(ps[bb][:], wcb[:, kc, cc * 128:(cc + 1) * 128], hT[:, kc, :], kc == 0, kc == 15)
                                   for kc in range(16)], reads=[Rcb, R_hT], writes=[R_ps[bb]])
                    P.op("dve", lambda e, bb=bb, i=i: e.tensor_tensor(out=convT[:, i, :], in0=ps[bb][:], in1=T[1][:, 0:TT],
                                                                      op=ALU.mult),
                         reads=[R_ps[bb], R_T[1]], writes=[R_conv])
                    pfree(bb)
            if SL < 4:
                continue
            qx = [qn, qr]
            R_qx = [R_qn, R_qr]
            for hx in range(4):
                wv, Rw = wload(w_in, 0, 16, C_QX + hx * 256, 256)
                for dc in range(2):
                    bk = palloc()
                    P.group("pe", [MM(ps[bk][:], wv[:, kc, dc * 128:(dc + 1) * 128], hT[:, kc, :], kc == 0, kc == 15)
                                   for kc in range(16)], reads=[Rw, R_hT], writes=[R_ps[bk]])
                    P.op("act", lambda e, bk=bk, hx=hx, dc=dc: e.activation(out=qx[dc][:, hx, :], in_=ps[bk][:], func=AF.Copy),
                         reads=[R_ps[bk]], writes=[R_qx[dc]])
                    pfree(bk)
            for hx in range(4):
                for mc in range(2):
                    bk = palloc()
                    P.group("pe", [MM(ps[bk][:], KmT[:, hx * 2 + dc, mc * 128:(mc + 1) * 128], qx[dc][:, hx, :], dc == 0, dc == 1)
                                   for dc in range(2)], reads=[R_Km, R_qn, R_qr], writes=[R_ps[bk]])
                    P.op("act", lambda e, bk=bk, mc=mc: e.activation(out=PT[mc][:], in_=ps[bk][:], func=AF.Exp, scale=SC_MEM),
                         reads=[R_ps[bk]], writes=[R_PT[mc]])
                    pfree(bk)
                bs = palloc()
                P.group("pe", [MM(ps[bs][:], ones[:], PT[mc][:], mc == 0, mc == 1) for mc in range(2)],
                        reads=[R_PT[0], R_PT[1], R_ones], writes=[R_ps[bs]])
                P.op("dve", lambda e, bs=bs: e.reciprocal(out=T[3][:, 0:TT], in_=ps[bs][:]), reads=[R_ps[bs]], writes=[R_T[3]])
                pfree(bs)
                for dv in range(2):
                    bo = palloc()
                    P.group("pe", [MM(ps[bo][:], Vm[:, mc, hx * 256 + dv * 128: hx * 256 + (dv + 1) * 128], PT[mc][:],
                                      mc == 0, mc == 1) for mc in range(2)], reads=[R_Vm, R_PT[0], R_PT[1]],
                            writes=[R_ps[bo]])
                    P.op("dve", lambda e, bo=bo, hx=hx, dv=dv: e.tensor_tensor(out=memT[:, hx * 2 + dv, :], in0=ps[bo][:],
                                                                               in1=T[3][:, 0:TT], op=ALU.mult),
                         reads=[R_ps[bo], R_T[3]], writes=[R_memo])
                    pfree(bo)
            if SL < 5:
                continue
            branches = [(w_o_mla, attnT, R_attn), (w_out_conv, convT, R_conv), (w_o_mem, memT, R_memo)]
            macc = [T[4], T[5]]
            R_macc = [R_T[4], R_T[5]]
            for jp in range(8):
                for br in range(3):
                    wsrc, actv, R_actv = branches[br]
                    wp, Rp = wload(wsrc, 0, 8, jp * 256, 256)
                    wg, Rg = wload(w_in, 0, 16, C_G + br * 2048 + jp * 256, 256)
                    for jj in range(2):
                        j = jp * 2 + jj
                        by = palloc(); bg = palloc()
                        P.group("pe", [MM(ps[by][:], wp[:, kc, jj * 128:(jj + 1) * 128], actv[:, kc, :], kc == 0, kc == 7)
                                       for kc in range(8)], reads=[Rp, R_actv], writes=[R_ps[by]])
                        P.group("pe", [MM(ps[bg][:], wg[:, kc, jj * 128:(jj + 1) * 128], hT[:, kc, :], kc == 0, kc == 15)
                                       for kc in range(16)], reads=[Rg, R_hT], writes=[R_ps[bg]])
                        gcol = P_GB + br * 16 + j
                        P.op("act", lambda e, bg=bg, gcol=gcol: e.activation(out=T[0][:, 0:TT], in_=ps[bg][:], func=AF.Sigmoid,
                                                                             bias=pT[:, gcol:gcol + 1]),
                             reads=[R_ps[bg], R_pT], writes=[R_T[0]])
                        if br == 0:
                            P.op("dve", lambda e, by=by, jj=jj: e.tensor_tensor(out=macc[jj][:, 0:TT], in0=ps[by][:],
                                                                                in1=T[0][:, 0:TT], op=ALU.mult),
                                 reads=[R_ps[by], R_T[0]], writes=[R_macc[jj]])
                        else:
                            P.op("dve", lambda e, by=by: e.tensor_tensor(out=T[1][:, 0:TT], in0=ps[by][:], in1=T[0][:, 0:TT],
                                                                         op=ALU.mult),
                                 reads=[R_ps[by], R_T[0]], writes=[R_T[1]])
                            if br == 1:
                                P.op("dve", lambda e, jj=jj: e.tensor_tensor(out=macc[jj][:, 0:TT], in0=macc[jj][:, 0:TT],
                                                                             in1=T[1][:, 0:TT], op=ALU.add),
                                     reads=[R_T[1]], writes=[R_macc[jj]])
                            else:
                                P.op("dve", lambda e, jj=jj, j=j: e.tensor_tensor(out=mergedT[:, j, :], in0=macc[jj][:, 0:TT],
                                                                                  in1=T[1][:, 0:TT], op=ALU.add),
                                     reads=[R_T[1], R_macc[jj]], writes=R_mg)
                        pfree(by); pfree(bg)
            if SL < 6:
                continue
            load_xt(x, t, sem_xr)
            for cg in range(4):
                wA, RA = wload(w_o, 0, 8, cg * 512, 512)
                wB, RB = wload(w_o, 1024, 8, cg * 512, 512)
                for jt in range(4):
                    bk = palloc()
                    P.group("pe", [MM(ps[bk][:], mergedT[:, kc, jt * 128:(jt + 1) * 128],
                                      (wA[:, kc, :] if kc < 8 else wB[:, kc - 8, :]), kc == 0, kc == 15)
                                   for kc in range(16)], reads=[RA, RB] + R_mg, writes=[R_ps[bk]])
                    P.op("dve", lambda e, bk=bk, jt=jt, cg=cg: e.tensor_tensor(out=xt[:, jt, cg * 512:(cg + 1) * 512],
                                                                               in0=ps[bk][:],
                                                                               in1=xt[:, jt, cg * 512:(cg + 1) * 512], op=ALU.add),
                         reads=[R_ps[bk]], writes=R_xt)
                    pfree(bk)
            dst = x1s[t * TT:(t + 1) * TT, :].rearrange("(j p) d -> p j d", p=128)
            P.dma("sp", lambda e, dst=dst: e.dma_start(out=dst, in_=xt), sem_st, reads=R_xt)

        P.dma("sp", lambda e: e.dma_start(out=gfin, in_=fin.partition_broadcast(128)), sem_misc,
              writes=[R_gfin, R_KT, R_V, R_kr])
        XB = [big[:, 26624:30720].bitcast(F32), big[:, 30720:34816].bitcast(F32),
              qq[:, :, :].rearrange("p a b -> p (a b)").bitcast(F32), KV8[:, :].bitcast(F32)]
        R_XB = [R_b0, R_b1, R_qn, R_qr, R_Km, R_Vm]
        XBUF = [[xt[:, j, :] for j in range(4)], XB]
        R_XBUF = [R_xt, R_XB]
        sem_xb = P.new_dsem()
        SEM_L = [sem_x, sem_xb]

        def load_x1(t, first_extra=()):
            bsel = t % 2
            for j in range(4):
                src = x1s[t * TT + j * 128: t * TT + (j + 1) * 128, :]
                P.dma("sp", lambda e, src=src, dstp=XBUF[bsel][j]: e.dma_start(out=dstp, in_=src), SEM_L[bsel],
                      writes=list(R_XBUF[bsel]) + list(first_extra))

        P.wait_all("sp", [(("dma", sem_st), P.dsem[sem_st])])
        if NT_2 > 0:
            load_x1(0)
            load_halo(x1s)
            norm_T(XBUF[0], R_XBUF[0], 4, P_FFN, hT, R_hT, TT)
            norm_T([xh[:, 0, :]], R_xh, 1, P_FFN, hTh, R_hTh, 4)
        for t in range(NT_2):
            cur = XBUF[t % 2]
            R_cur = R_XBUF[t % 2]
            for g in range(NFC // 2):
                wa, Ra = wload(w_up, 0, 16, g * 256, 256)
                wb, Rb = wload(w_up, 0, 16, DFF + g * 256, 256)
                for cc in range(2):
                    i = g * 2 + cc
                    outs = []
                    for ab, (wv, Rw) in enumerate(((wa, Ra), (wb, Rb))):
                        idx = ab * NFC + i
                        ub = T[ab]; R_ub = R_T[ab]
                        cv = T[2 + ab]; R_cv = R_T[2 + ab]
                        bk = palloc()
                        P.group("pe", [MM(ps[bk][:], wv[:, kc, cc * 128:(cc + 1) * 128], hT[:, kc, :], kc == 0, kc == 15)
                                       for kc in range(16)], reads=[Rw, R_hT], writes=[R_ps[bk]])
                        P.op("act", lambda e, bk=bk, ub=ub: e.activation(out=ub[:, 1:TT + 1], in_=ps[bk][:], func=AF.Copy),
                             reads=[R_ps[bk]], writes=[R_ub])
                        pfree(bk)
                        if t == 0:
                            hb_ = palloc()
                            P.group("pe", [MM(ps[hb_][:, 0:4], wv[:, kc, cc * 128:(cc + 1) * 128], hTh[:, kc, :], kc == 0, kc == 15)
                                           for kc in range(16)], reads=[Rw, R_hTh], writes=[R_ps[hb_]])
                            P.op("act", lambda e, hb_=hb_, idx=idx: e.activation(out=fh[:, idx, :], in_=ps[hb_][:, 0:4], func=AF.Copy),
                                 reads=[R_ps[hb_]], writes=[R_fh])
                            pfree(hb_)
                        P.op("act", lambda e, ub=ub, idx=idx: e.activation(out=ub[:, 0:1], in_=fcar[:, idx:idx + 1], func=AF.Copy),
                             reads=[R_fcar], writes=[R_ub])
                        rsrc = fh[:, idx, t:t + 1] if t < NT - 1 else zero1[:, 0:1]
                        P.op("act", lambda e, ub=ub, rsrc=rsrc: e.activation(out=ub[:, TT + 1:TT + 2], in_=rsrc, func=AF.Copy),
                             reads=[R_fh], writes=[R_ub])
                        P.op("act", lambda e, ub=ub, idx=idx: e.activation(out=fcar[:, idx:idx + 1], in_=ub[:, TT:TT + 1], func=AF.Copy),
                             reads=[R_ub], writes=[R_fcar])
                        c0 = P_FCW + ab * NFC + i
                        P.op("dve", lambda e, ub=ub, cv=cv, c0=c0: e.tensor_scalar(out=cv[:, 0:TT], in0=ub[:, 0:TT],
                                                                                   scalar1=pT[:, c0:c0 + 1], scalar2=None,
                                                                                   op0=ALU.mult), reads=[R_ub, R_pT], writes=[R_cv])
                        P.op("dve", lambda e, ub=ub, cv=cv, c0=c0: e.scalar_tensor_tensor(out=cv[:, 0:TT], in0=ub[:, 1:TT + 1],
                                                                                          scalar=pT[:, c0 + 88:c0 + 89],
                                                                                          in1=cv[:, 0:TT], op0=ALU.mult, op1=ALU.add),
                             reads=[R_ub, R_pT], writes=[R_cv])
                        P.op("dve", lambda e, ub=ub, cv=cv, c0=c0: e.scalar_tensor_tensor(out=cv[:, 0:TT], in0=ub[:, 2:TT + 2],
                                                                                          scalar=pT[:, c0 + 176:c0 + 177],
                                                                                          in1=cv[:, 0:TT], op0=ALU.mult, op1=ALU.add),
                             reads=[R_ub, R_pT], writes=[R_cv])
                    P.op("act", lambda e: e.activation(out=T[4][:, 0:TT], in_=T[2][:, 0:TT], func=AF.Silu),
                         reads=[R_T[2]], writes=[R_T[4]])
                    P.op("dve", lambda e, i=i: e.tensor_tensor(out=actT[:, i, :], in0=T[4][:, 0:TT], in1=T[3][:, 0:TT], op=ALU.mult),
                         reads=[R_T[4], R_T[3]], writes=[R_act[i], R_KT, R_V])
            KG = [(0, 8), (8, 8), (16, 8), (24, 8), (32, 8), (40, 4)]
            for cgp in range(4):
                bks = [palloc() for _ in range(4)]
                for (k0, kn) in KG:
                    wv, Rw = wload(w_down, k0 * 128, kn, cgp * 512, 512)
                    for jt in range(4):
                        P.group("pe", [MM(ps[bks[jt]][:], actT[:, k0 + kk, jt * 128:(jt + 1) * 128], wv[:, kk, :],
                                          (k0 + kk) == 0, (k0 + kk) == NFC - 1) for kk in range(kn)],
                                reads=[Rw] + [R_act[k0 + kk] for kk in range(kn)], writes=[R_ps[bks[jt]]])
                for jt in range(4):
                    P.op("dve", lambda e, jt=jt, cgp=cgp, bk=bks[jt], cur=cur: e.tensor_tensor(
                        out=cur[jt][:, cgp * 512:(cgp + 1) * 512], in0=ps[bk][:], in1=cur[jt][:, cgp * 512:(cgp + 1) * 512],
                        op=ALU.add), reads=[R_ps[bks[jt]]], writes=R_cur)
                    pfree(bks[jt])
                if cgp == 0 and t + 1 < NT_2:
                    load_x1(t + 1, first_extra=([R_V, R_kr] if t == 0 else ()))
                    norm_T(XBUF[(t + 1) % 2], R_XBUF[(t + 1) % 2], 4, P_FFN, hT, R_hT, TT)
            for j in range(4):
                P.op("act", lambda e, j=j, cur=cur: e.activation(out=junk[:], in_=cur[j], func=AF.Square,
                                                         accum_out=small[:, j:j + 1]), reads=R_cur, writes=[R_junk, R_small])
            P.op("act", lambda e: e.activation(out=small[:, 5:9], in_=small[:, 0:4], func=AF.Ln, scale=1.0 / D, bias=eps_ap),
                 writes=[R_small])
            P.op("act", lambda e: e.activation(out=small[:, 5:9], in_=small[:, 5:9], func=AF.Exp, scale=-0.5), writes=[R_small])
            for j in range(4):
                P.op("dve", lambda e, j=j, cur=cur: e.scalar_tensor_tensor(out=cur[j], in0=cur[j], scalar=small[:, 5 + j:6 + j],
                                                                   in1=gfin, op0=ALU.mult, op1=ALU.mult),
                     reads=[R_small, R_gfin], writes=R_cur)
            for j in range(4):
                dst = y[t * TT + j * 128: t * TT + (j + 1) * 128, :]
                P.dma("sp", lambda e, dst=dst, j=j, cur=cur: e.dma_start(out=dst, in_=cur[j]), sem_st, reads=R_cur)
        P.wait_all("sp", [(("dma", i), P.dsem[i]) for i in range(len(P.dsem))])
        P.q["sp"].append(("op", lambda e: e.nop(), False))
        P.run(st)
    return nc


_NC_CACHE = {}


def kernel(**inputs):
    f32 = np.float32
    if "nc" not in _NC_CACHE:
        _NC_CACHE["nc"] = build_nc()
    nc = _NC_CACHE["nc"]

    def a(name):
        return np.ascontiguousarray(np.asarray(inputs[name]))

    x = a("x").astype(f32, copy=False)
    mem = a("mem").astype(f32, copy=False)
    pos = a("positions").astype(np.int32, copy=False)
    B = x.shape[0]
    prm = np.concatenate([a(n).astype(f32, copy=False).reshape(-1) for n in
                          ("mix_norm", "ffn_norm", "mem_norm", "q_norm", "kv_norm", "gate_bias", "conv_w", "ffn_conv_w")]
                         ).reshape(NPRM, 128)
    cst = np.zeros((128, 130), f32)
    cst[:, :128] = np.eye(128, dtype=f32)
    invf = np.power(f32(10000.0), -np.arange(0, 64, 2, dtype=f32) / f32(64)).astype(f32)
    cst[0:32, 128] = invf
    cst[32:64, 128] = invf
    shared = {
        "w_in": a("w_in")[0], "w_uq": a("w_uq")[0], "w_ukv": a("w_ukv")[0], "w_o_mla": a("w_o_mla")[0],
        "w_out_conv": a("w_out_conv")[0], "w_mem_kv": a("w_mem_kv")[0], "w_o_mem": a("w_o_mem")[0],
        "w_o": a("w_o")[0], "w_up": a("w_up")[0], "w_down": a("w_down")[0],
        "prm": prm, "final_norm": a("final_norm").reshape(1, D), "cst": cst,
    }
    in_maps = []
    for c in range(B):
        m = dict(shared)
        m["x"] = x[c]
        m["mem"] = mem[c]
        m["positions"] = pos[c].reshape(1, S)
        in_maps.append(m)
    res = run_bass_kernel_spmd(nc, in_maps, core_ids=list(range(B)))
    return np.stack([r["y"] for r in res.results], axis=0).astype(f32, copy=False)
```

```python
import math
from contextlib import ExitStack
import numpy as np
import concourse.bass as bass
import concourse.mybir as mybir
from concourse.bass_utils import run_bass_kernel_spmd

F32 = mybir.dt.float32
BF16 = mybir.dt.bfloat16
I32 = mybir.dt.int32
AF = mybir.ActivationFunctionType
ALU = mybir.AluOpType

S = 2048
D = 2048
TT = 512
NT = S // TT
MEM = 256
DFF = 5632
NFC = DFF // 128
IN_COLS = 11328
C_CQ, C_CKV, C_KR, C_CV, C_CB, C_CC, C_QX, C_G = 0, 512, 1024, 1088, 2112, 3136, 4160, 5184
EPS = 1e-6
P_MIX, P_FFN, P_MEMN, P_QN, P_KVN, P_GB, P_CW, P_FCW = 0, 16, 32, 48, 52, 56, 104, 128
NPRM = 392
NS = 4
SLOT = 4096


class Res:
    __slots__ = ("w", "r")

    def __init__(self):
        self.w = {}
        self.r = {}


def _mrg(d, tok):
    k, v = tok
    if d.get(k, 0) < v:
        d[k] = v


class Prog:
    ENG = ("pe", "act", "dve", "pool", "sp")

    def __init__(self, nc):
        self.nc = nc
        self.q = {e: [] for e in self.ENG}
        self.cnt = {e: 0 for e in self.ENG}
        self.dsem = []
        self.seen = {e: {} for e in self.ENG}

    def _wait(self, eng, key, val):
        if key == eng and eng == "pe":
            return
        if self.seen[eng].get(key, 0) >= val:
            return
        self.seen[eng][key] = val
        self.q[eng].append(("wait", key, val))

    def _deps(self, eng, reads, writes):
        for r in reads:
            for k, v in r.w.items():
                self._wait(eng, k, v)
        for w in writes:
            for k, v in w.w.items():
                self._wait(eng, k, v)
            for k, v in w.r.items():
                self._wait(eng, k, v)

    def _note(self, tok, reads, writes):
        for r in reads:
            _mrg(r.r, tok)
        for w in writes:
            w.w = {tok[0]: tok[1]}
            w.r = {}

    def op(self, eng, fn, reads=(), writes=()):
        self._deps(eng, reads, writes)
        self.cnt[eng] += 1
        tok = (eng, self.cnt[eng])
        self.q[eng].append(("op", fn, True))
        self._note(tok, reads, writes)
        return tok

    def group(self, eng, fns, reads=(), writes=()):
        self._deps(eng, reads, writes)
        n = len(fns)
        for i, fn in enumerate(fns):
            self.q[eng].append(("op", fn, i == n - 1))
        self.cnt[eng] += 1
        tok = (eng, self.cnt[eng])
        self._note(tok, reads, writes)
        return tok

    def new_dsem(self):
        self.dsem.append(0)
        return len(self.dsem) - 1

    def dma(self, eng, fn, sem, reads=(), writes=()):
        self._deps(eng, reads, writes)
        self.dsem[sem] += 16
        tok = (("dma", sem), self.dsem[sem])
        self.q[eng].append(("dma", fn, sem))
        self._note(tok, reads, writes)
        return tok

    def wait_all(self, eng, toks):
        for t in toks:
            self._wait(eng, t[0], t[1])

    def run(self, stack):
        nc = self.nc
        semh = {}
        for e in self.ENG:
            semh[e] = stack.enter_context(nc.semaphore("s_" + e))
        for i in range(len(self.dsem)):
            semh[("dma", i)] = stack.enter_context(nc.semaphore("d_%d" % i))
        block = stack.enter_context(nc.Block())

        def replay(name):
            def f(engine):
                for item in self.q[name]:
                    if item[0] == "wait":
                        engine.wait_ge(semh[item[1]], item[2])
                    elif item[0] == "op":
                        ins = item[1](engine)
                        if item[2]:
                            ins.then_inc(semh[name], 1)
                    else:
                        ins = item[1](engine)
                        ins.then_inc(semh[("dma", item[2])], 16)
            return f

        block.tensor(replay("pe"))
        block.scalar(replay("act"))
        block.vector(replay("dve"))
        block.gpsimd(replay("pool"))
        block.sync(replay("sp"))


def build_nc(dbg=None):
    import os
    dbg = dbg or {}
    NT_A = dbg.get('A', NT); NT_1 = dbg.get('S1', NT); NT_2 = dbg.get('S2', NT); DO_M = dbg.get('M', 1)
    nc = bass.Bass("TRN2", target_bir_lowering=False)

    def din(name, shape, dt=F32):
        return nc.dram_tensor(name, shape, dt, kind="ExternalInput").ap()

    x = din("x", [S, D])
    mem = din("mem", [MEM, D])
    pos = din("positions", [1, S], I32)
    w_in = din("w_in", [D, IN_COLS])
    w_uq = din("w_uq", [512, 1536])
    w_ukv = din("w_ukv", [512, 2048])
    w_o_mla = din("w_o_mla", [1024, D])
    w_out_conv = din("w_out_conv", [1024, D])
    w_mem_kv = din("w_mem_kv", [D, 2048])
    w_o_mem = din("w_o_mem", [1024, D])
    w_o = din("w_o", [D, D])
    w_up = din("w_up", [D, 2 * DFF])
    w_down = din("w_down", [DFF, D])
    prm = din("prm", [NPRM, 128])
    fin = din("final_norm", [1, D])
    cst = din("cst", [128, 130])
    y = nc.dram_tensor("y", [S, D], F32, kind="ExternalOutput").ap()
    x1s = nc.dram_tensor("x1s", [S, D], F32, kind="ExternalOutput").ap()

    with ExitStack() as st:
        def sb(name, shape, dt):
            return st.enter_context(nc.sbuf_tensor(name, shape, dt))

        P = Prog(nc)
        slots = [sb("slot%d" % i, [128, SLOT], BF16) for i in range(NS)]
        R_slot = [Res() for _ in range(NS)]
        slot_sem = [P.new_dsem() for _ in range(NS)]
        hT = sb("hT", [128, 16, TT], BF16); R_hT = Res()
        hTh = sb("hTh", [128, 16, 4], BF16); R_hTh = Res()
        big = sb("big", [128, 34816], BF16)
        KT = big[:, 0:16384].rearrange("p (h s) -> p h s", h=8)
        Vt = big[:, 16384:32768].rearrange("p (c n) -> p c n", c=16)
        krT = big[:, 32768:34816]
        actT = big[:, 0:NFC * TT].rearrange("p (c n) -> p c n", c=NFC)
        gfin = big[:, 22528:26624].bitcast(F32)
        R_b0 = Res(); R_b1 = Res()
        R_KT = Res(); R_V = Res(); R_kr = Res(); R_gfin = Res()
        R_act = [Res() for _ in range(NFC)]
        KV8 = sb("KV8", [128, 4096], BF16)
        KmT = KV8[:, 0:2048].rearrange("p (c n) -> p c n", c=8); R_Km = Res()
        Vm = KV8[:, 2048:4096].rearrange("p (c n) -> p c n", c=2); R_Vm = Res()
        U = sb("U", [128, 12288], F32)
        xt = U[:, 0:8192].rearrange("p (j d) -> p j d", j=4)
        attnT = U[:, 0:2048].bitcast(BF16).rearrange("p (c n) -> p c n", c=8)
        convT = U[:, 2048:4096].bitcast(BF16).rearrange("p (c n) -> p c n", c=8)
        memT = U[:, 4096:6144].bitcast(BF16).rearrange("p (c n) -> p c n", c=8)
        cqraw = U[:, 6144:8192].rearrange("p (c n) -> p c n", c=4)
        R_attn = Res(); R_conv = Res(); R_memo = Res(); R_cqraw = Res()
        R_xt = [R_attn, R_conv, R_memo, R_cqraw]
        mg = U[:, 8192:12288]
        mergedT = mg.bitcast(BF16).rearrange("p (c n) -> p c n", c=16)
        xn = mg.bitcast(BF16).rearrange("p (j d) -> p j d", j=4)
        cqn = mg[:, 0:1024].bitcast(BF16).rearrange("p (c n) -> p c n", c=4)
        cosT = mg[0:64, 1024:1536]
        sinT = mg[0:64, 1536:2048]
        rtA = mg[0:64, 2048:2560]
        rtB = mg[0:64, 2560:3072]
        posi = mg[0:64, 3072:3584].bitcast(I32)
        R_cqn = Res(); R_cs = Res(); R_rt = Res()
        R_mg = [R_cqn, R_cs, R_rt]
        qq = sb("qq", [128, 8, TT], BF16)
        qn = qq[:, 0:4, :]
        qr = qq[:, 4:8, :]
        R_qn = Res(); R_qr = Res()
        xh = qq[:, :, :].rearrange("p a b -> p (a b)").bitcast(F32).rearrange("p (j d) -> p j d", j=1)
        R_xh = [R_qn, R_qr]
        junk = sb("junk", [128, 2048], BF16); R_junk = Res()
        T = [sb("T%d" % i, [128, 514], F32) for i in range(6)]
        R_T = [Res() for _ in range(6)]
        prow = T[0][:, 0:512].rearrange("p (b c) -> p b c", b=4)
        R_prow = R_T[0]
        PT = [sb("PT%d" % i, [128, TT], BF16) for i in range(5)]
        R_PT = [Res() for _ in range(5)]
        wrot = sb("wrot", [128, 16, 64], BF16); R_wrot = Res()
        pT = sb("pT", [128, NPRM], F32); R_pT = Res()
        idf = sb("idf", [128, 130], F32); R_idf = Res()
        idb = sb("idb", [128, 128], BF16); R_idb = Res()
        ones = sb("ones", [128, 128], BF16); R_ones = Res()
        small = sb("small", [128, 16], F32); R_small = Res()
        ucar = sb("ucar", [128, 8], F32); R_ucar = Res()
        uh = sb("uh", [128, 8, 4], F32); R_uh = Res()
        fcar = sb("fcar", [128, 2 * NFC], F32); R_fcar = Res()
        fh = sb("fh", [128, 2 * NFC, 4], F32); R_fh = Res()
        zero1 = sb("zero1", [128, 4], F32)

        NB = 6
        ps = [st.enter_context(nc.psum_tensor("ps%d" % i, [128, 512], F32)) for i in range(NB)]
        psT = [st.enter_context(nc.psum_tensor("psT%d" % i, [128, 1024], BF16)) for i in range(2)]
        R_ps = [Res() for _ in range(NB)]
        R_psT = [Res(), Res()]
        held = [False] * NB
        rot = [0]

        def palloc():
            for _ in range(NB):
                i = rot[0] % NB
                rot[0] += 1
                if not held[i]:
                    held[i] = True
                    return i
            raise RuntimeError("psum exhausted")

        def pfree(i):
            held[i] = False

        sem_x = P.new_dsem(); sem_misc = P.new_dsem(); sem_st = P.new_dsem(); sem_pos = P.new_dsem()
        sem_xh = P.new_dsem(); sem_xr = P.new_dsem(); sem_idf = P.new_dsem(); sem_prm = P.new_dsem()
        slot_ctr = [0]

        def wload(w, r0, kcn, c0, ncols):
            i = slot_ctr[0] % NS
            slot_ctr[0] += 1
            view = slots[i][:, 0:kcn * ncols].rearrange("p (k n) -> p k n", k=kcn)
            src = w[r0:r0 + kcn * 128, c0:c0 + ncols].rearrange("(k p) n -> p k n", p=128)
            P.dma("pool", lambda e, view=view, src=src: e.dma_start(out=view, in_=src), slot_sem[i],
                  writes=[R_slot[i]])
            return view, R_slot[i]

        def MM(o, l, r, s, t):
            return lambda e: e.matmul(o, l, r, start=s, stop=t)

        P.dma("sp", lambda e: e.dma_start(out=idf[:], in_=cst), sem_idf, writes=[R_idf])
        for b in range(4):
            r0 = b * 128
            n = min(128, NPRM - r0)
            P.dma("sp", lambda e, b=b, r0=r0, n=n: e.dma_start(out=prow[0:n, b, :], in_=prm[r0:r0 + n, :]),
                  sem_prm, writes=[R_prow])
        P.op("dve", lambda e: e.memset(ones[:], 1.0), writes=[R_ones])
        P.op("dve", lambda e: e.memset(small[:], 0.0), writes=[R_small])
        P.op("dve", lambda e: e.memset(small[:, 10:11], EPS), writes=[R_small])
        P.op("dve", lambda e: e.memset(small[:, 11:12], math.pi / 2), writes=[R_small])
        P.op("dve", lambda e: e.memset(zero1[:], 0.0))
        P.op("dve", lambda e: e.memset(ucar[:], 0.0), writes=[R_ucar])
        P.op("dve", lambda e: e.memset(fcar[:], 0.0), writes=[R_fcar])
        P.op("dve", lambda e: e.memset(uh[:], 0.0), writes=[R_uh])
        P.op("dve", lambda e: e.memset(fh[:], 0.0), writes=[R_fh])
        P.op("dve", lambda e: e.tensor_copy(out=idb[:], in_=idf[:, 0:128]), reads=[R_idf], writes=[R_idb])
        for b in range(4):
            n = min(128, NPRM - b * 128)
            bk = palloc()
            P.group("pe", [lambda e, b=b, n=n, bk=bk: e.transpose(ps[bk][:, 0:n], prow[0:n, b, :], idf[0:n, 0:n])],
                    reads=[R_prow, R_idf], writes=[R_ps[bk]])
            P.op("dve", lambda e, b=b, n=n, bk=bk: e.tensor_copy(out=pT[:, b * 128:b * 128 + n], in_=ps[bk][:, 0:n]),
                 reads=[R_ps[bk]], writes=[R_pT])
            pfree(bk)
        eps_ap = small[:, 10:11]
        halfpi = small[:, 11:12]
        invf = idf[0:64, 128:129]

        def norm_T(src, R_src, nj, gbase, dst, R_dst, ncopy):
            for j in range(nj):
                P.op("act", lambda e, j=j: e.activation(out=junk[:], in_=src[j], func=AF.Square,
                                                         accum_out=small[:, j:j + 1]),
                     reads=R_src, writes=[R_junk, R_small])
            P.op("act", lambda e: e.activation(out=small[:, 5:5 + nj], in_=small[:, 0:nj], func=AF.Ln,
                                               scale=1.0 / D, bias=eps_ap), reads=[], writes=[R_small])
            P.op("act", lambda e: e.activation(out=small[:, 5:5 + nj], in_=small[:, 5:5 + nj], func=AF.Exp,
                                               scale=-0.5), writes=[R_small])
            for j in range(nj):
                P.op("dve", lambda e, j=j: e.tensor_scalar(out=xn[:, j, :], in0=src[j],
                                                            scalar1=small[:, 5 + j:6 + j], scalar2=None,
                                                            op0=ALU.mult),
                     reads=list(R_src) + [R_small], writes=R_mg)
            for kc in range(16):
                hb = kc % 2
                fns = [(lambda e, j=j, kc=kc, hb=hb: e.transpose(psT[hb][:, j * 128:(j + 1) * 128],
                                                                 xn[:, j, kc * 128:(kc + 1) * 128], idb[:]))
                       for j in range(nj)]
                P.group("pe", fns, reads=R_mg + [R_idb], writes=[R_psT[hb]])
                P.op("dve", lambda e, kc=kc, hb=hb: e.tensor_scalar(out=dst[:, kc, 0:ncopy],
                                                                    in0=psT[hb][:, 0:ncopy],
                                                                    scalar1=pT[:, gbase + kc: gbase + kc + 1],
                                                                    scalar2=None, op0=ALU.mult),
                     reads=[R_psT[hb], R_pT], writes=[R_dst])

        def load_xt(srcd, t, sem):
            src = srcd[t * TT:(t + 1) * TT, :].rearrange("(j p) d -> p j d", p=128)
            P.dma("sp", lambda e, src=src: e.dma_start(out=xt, in_=src), sem, writes=R_xt)

        def load_halo(srcd):
            P.op("dve", lambda e: e.memset(xh, 0.0), writes=R_xh)
            for i in range(3):
                P.dma("sp", lambda e, i=i: e.dma_start(out=xh[i:i + 1, 0, :], in_=srcd[(i + 1) * TT:(i + 1) * TT + 1, :]),
                      sem_xh, writes=R_xh)

        def rope_tables(t):
            RL = dbg.get('RL', 9)
            P.dma("sp", lambda e: e.dma_start(out=posi, in_=pos[0:1, t * TT:(t + 1) * TT].partition_broadcast(64)),
                  sem_pos, writes=[R_rt])
            if RL < 1:
                return
            P.op("dve", lambda e: e.tensor_copy(out=rtA, in_=posi), reads=[R_rt], writes=[R_rt])
            P.op("dve", lambda e: e.tensor_scalar(out=rtA, in0=rtA, scalar1=invf, scalar2=None, op0=ALU.mult),
                 reads=[R_rt, R_idf], writes=[R_rt])
            P.op("dve", lambda e: e.tensor_scalar(out=rtB, in0=rtA, scalar1=1.0 / (2 * math.pi), scalar2=None,
                                                  op0=ALU.mult), reads=[R_rt], writes=[R_rt])
            if RL < 2:
                return
            P.op("dve", lambda e: e.tensor_copy(out=posi, in_=rtB), reads=[R_rt], writes=[R_rt])
            P.op("dve", lambda e: e.tensor_copy(out=rtB, in_=posi), reads=[R_rt], writes=[R_rt])
            P.op("dve", lambda e: e.scalar_tensor_tensor(out=rtA, in0=rtB, scalar=-2 * math.pi, in1=rtA,
                                                         op0=ALU.mult, op1=ALU.add), reads=[R_rt], writes=[R_rt])
            P.op("dve", lambda e: e.tensor_scalar(out=rtA, in0=rtA, scalar1=-3.141592, scalar2=3.141592,
                                                  op0=ALU.max, op1=ALU.min), reads=[R_rt], writes=[R_rt])
            if RL < 3:
                return
            P.op("act", lambda e: e.activation(out=sinT, in_=rtA, func=AF.Sin), reads=[R_rt], writes=[R_cs])
            if RL < 4:
                return
            P.op("act", lambda e: e.activation(out=rtB, in_=rtA, func=AF.Abs), reads=[R_rt], writes=[R_rt])
            if RL < 5:
                return
            P.op("act", lambda e: e.activation(out=cosT, in_=rtB, func=AF.Sin, scale=-1.0, bias=halfpi[0:64, :]),
                 reads=[R_rt, R_small], writes=[R_cs])

        def rms_bcast(bk_ss, n, dstT, R_dstT):
            P.op("act", lambda e: e.activation(out=dstT[:, 0:TT], in_=ps[bk_ss][:], func=AF.Ln, scale=1.0 / n,
                                               bias=eps_ap), reads=[R_ps[bk_ss], R_small], writes=[R_dstT])
            P.op("act", lambda e: e.activation(out=dstT[:, 0:TT], in_=dstT[:, 0:TT], func=AF.Exp, scale=-0.5),
                 writes=[R_dstT])

        def latent_norm(wcols, gbase, dst, R_dst):
            LL = dbg.get('LL', 9)
            bss = palloc()
            for g in range(2):
                wv, Rw = wload(w_in, 0, 16, wcols + g * 256, 256)
                if LL < 1:
                    continue
                for cc in range(2):
                    c = g * 2 + cc
                    bk = palloc()
                    P.group("pe", [MM(ps[bk][:], wv[:, kc, cc * 128:(cc + 1) * 128], hT[:, kc, :], kc == 0, kc == 15)
                                   for kc in range(16)], reads=[Rw, R_hT], writes=[R_ps[bk]])
                    LV = dbg.get('LV', 3)
                    if LL >= 2 and (LV & 2):
                        P.op("dve", lambda e, bk=bk, c=c: e.tensor_copy(out=cqraw[:, c, :], in_=ps[bk][:]),
                             reads=[R_ps[bk]], writes=[R_cqraw])
                    if LL >= 2 and (LV & 1):
                        P.op("act", lambda e, c=c: e.activation(out=PT[0][:], in_=cqraw[:, c, :], func=AF.Square),
                             reads=[R_cqraw], writes=[R_PT[0]])
                    pfree(bk)
                    if LL >= 3:
                        P.group("pe", [MM(ps[bss][:], ones[:], PT[0][:], c == 0, c == 3)], reads=[R_PT[0], R_ones],
                                writes=[R_ps[bss]])
            if LL >= 4:
                rms_bcast(bss, 512, T[0], R_T[0])
            pfree(bss)
            if LL < 5:
                return
            for c in range(4):
                P.op("dve", lambda e, c=c: e.scalar_tensor_tensor(out=dst[:, c, :], in0=cqraw[:, c, :],
                                                                   scalar=pT[:, gbase + c:gbase + c + 1],
                                                                   in1=T[0][:, 0:TT], op0=ALU.mult, op1=ALU.mult),
                     reads=[R_cqraw, R_pT, R_T[0]], writes=[R_dst])

        def rope_evac(bk_raw, bk_rot, dst_ap, R_dst):
            P.op("dve", lambda e: e.tensor_tensor(out=T[1][0:64, 0:TT], in0=ps[bk_raw][0:64, :], in1=cosT, op=ALU.mult),
                 reads=[R_ps[bk_raw], R_cs], writes=[R_T[1]])
            P.op("dve", lambda e: e.tensor_tensor(out=T[2][0:64, 0:TT], in0=ps[bk_rot][0:64, :], in1=sinT, op=ALU.mult),
                 reads=[R_ps[bk_rot], R_cs], writes=[R_T[2]])
            P.op("dve", lambda e: e.tensor_tensor(out=dst_ap, in0=T[1][0:64, 0:TT], in1=T[2][0:64, 0:TT], op=ALU.add),
                 reads=[R_T[1], R_T[2]], writes=[R_dst])

        P.dma("sp", lambda e: e.dma_start(out=xt[:, 0:2, :], in_=mem.rearrange("(j p) d -> p j d", p=128)),
              sem_x, writes=R_xt)
        norm_T([xt[:, j, :] for j in range(2)], R_xt, 2, P_MEMN, hT, R_hT, MEM)
        for g in range(8):
            wv, Rw = wload(w_mem_kv, 0, 16, g * 256, 256)
            if g < 4:
                for cc in range(2):
                    c = g * 2 + cc
                    bk = palloc()
                    P.group("pe", [MM(ps[bk][:, 0:MEM], wv[:, kc, cc * 128:(cc + 1) * 128], hT[:, kc, 0:MEM],
                                      kc == 0, kc == 15) for kc in range(16)], reads=[Rw, R_hT], writes=[R_ps[bk]])
                    P.op("dve", lambda e, bk=bk, c=c: e.tensor_copy(out=KmT[:, c, :], in_=ps[bk][:, 0:MEM]),
                         reads=[R_ps[bk]], writes=[R_Km])
                    pfree(bk)
            else:
                c0 = (g - 4) * 256
                for mc in range(2):
                    bk = palloc()
                    P.group("pe", [MM(ps[bk][:, 0:256], hT[:, kc, mc * 128:(mc + 1) * 128], wv[:, kc, :],
                                      kc == 0, kc == 15) for kc in range(16)], reads=[Rw, R_hT], writes=[R_ps[bk]])
                    P.op("dve", lambda e, bk=bk, mc=mc, c0=c0: e.tensor_copy(out=Vm[:, mc, c0:c0 + 256],
                                                                             in_=ps[bk][:, 0:256]),
                         reads=[R_ps[bk]], writes=[R_Vm])
                    pfree(bk)

        for t in range(NT_A):
            AL = dbg.get('AL', 9)
            load_xt(x, t, sem_x)
            norm_T([xt[:, j, :] for j in range(4)], R_xt, 4, P_MIX, hT, R_hT, TT)
            if AL < 1:
                continue
            rope_tables(t)
            if AL < 2:
                continue
            latent_norm(C_CKV, P_KVN, cqn, R_cqn)
            if AL < 3:
                continue
            wv, Rw = wload(w_in, 0, 16, C_KR, 64)
            P.op("act", lambda e, wv=wv: e.activation(out=wrot[:, :, 0:32], in_=wv[:, :, 32:64], func=AF.Copy, scale=-1.0),
                 reads=[Rw], writes=[R_wrot])
            P.op("act", lambda e, wv=wv: e.activation(out=wrot[:, :, 32:64], in_=wv[:, :, 0:32], func=AF.Copy),
                 reads=[Rw], writes=[R_wrot])
            b1 = palloc(); b2 = palloc()
            P.group("pe", [MM(ps[b1][0:64, :], wv[:, kc, :], hT[:, kc, :], kc == 0, kc == 15) for kc in range(16)],
                    reads=[Rw, R_hT], writes=[R_ps[b1]])
            P.group("pe", [MM(ps[b2][0:64, :], wrot[:, kc, :], hT[:, kc, :], kc == 0, kc == 15) for kc in range(16)],
                    reads=[R_wrot, R_hT], writes=[R_ps[b2]])
            rope_evac(b1, b2, krT[0:64, t * TT:(t + 1) * TT], R_kr)
            pfree(b1); pfree(b2)
            if AL < 4:
                continue
            for half in range(2):
                wv, Rw = wload(w_ukv, 0, 4, half * 1024, 1024)
                for hh in range(4):
                    h = half * 4 + hh
                    bk = palloc()
                    P.group("pe", [MM(ps[bk][:], wv[:, kc, hh * 256: hh * 256 + 128], cqn[:, kc, :], kc == 0, kc == 3)
                                   for kc in range(4)], reads=[Rw, R_cqn], writes=[R_ps[bk]])
                    P.op("act", lambda e, bk=bk, h=h, t=t: e.activation(out=KT[:, h, t * TT:(t + 1) * TT], in_=ps[bk][:],
                                                                        func=AF.Copy),
                         reads=[R_ps[bk]], writes=[R_KT])
                    pfree(bk)
                wv4 = wv.rearrange("p k (h c) -> p k h c", h=4)
                for j in range(4):
                    bk = palloc()
                    P.group("pe", [MM(ps[bk][:].rearrange("p (h c) -> p h c", h=4), cqn[:, kc, j * 128:(j + 1) * 128],
                                      wv4[:, kc, :, 128:256], kc == 0, kc == 3) for kc in range(4)],
                            reads=[Rw, R_cqn], writes=[R_ps[bk]])
                    P.op("dve", lambda e, bk=bk, j=j, t=t, half=half: e.tensor_copy(
                        out=Vt[:, t * 4 + j, half * 512:(half + 1) * 512], in_=ps[bk][:]),
                         reads=[R_ps[bk]], writes=[R_V])
                    pfree(bk)

        SC_MLA = 192.0 ** -0.5
        SC_MEM = 256.0 ** -0.5
        for t in range(NT_1):
            load_xt(x, t, sem_x)
            if t == 0:
                load_halo(x)
            norm_T([xt[:, j, :] for j in range(4)], R_xt, 4, P_MIX, hT, R_hT, TT)
            if t == 0:
                norm_T([xh[:, 0, :]], R_xh, 1, P_MIX, hTh, R_hTh, 4)
            rope_tables(t)
            latent_norm(C_CQ, P_QN, cqn, R_cqn)
            SL = dbg.get('SL', 9)
            if SL < 1:
                continue
            for hg in range(2):
                wv, Rw = wload(w_uq, 0, 4, hg * 768, 768)
                wv4 = wv.rearrange("p k (h c) -> p k h c", h=4)
                wrv = wrot[:, :, :].rearrange("p k c -> p (k c)")[:, 0:1024].rearrange("p (k h c) -> p k h c", k=4, h=4)
                P.op("act", lambda e, wv4=wv4, wrv=wrv: e.activation(out=wrv[:, :, :, 0:32], in_=wv4[:, :, :, 160:192],
                                                                     func=AF.Copy, scale=-1.0),
                     reads=[Rw], writes=[R_wrot])
                P.op("act", lambda e, wv4=wv4, wrv=wrv: e.activation(out=wrv[:, :, :, 32:64], in_=wv4[:, :, :, 128:160],
                                                                     func=AF.Copy),
                     reads=[Rw], writes=[R_wrot])
                for hh in range(4):
                    bk = palloc()
                    P.group("pe", [MM(ps[bk][:], wv4[:, kc, hh, 0:128], cqn[:, kc, :], kc == 0, kc == 3)
                                   for kc in range(4)], reads=[Rw, R_cqn], writes=[R_ps[bk]])
                    P.op("act", lambda e, bk=bk, hh=hh: e.activation(out=qn[:, hh, :], in_=ps[bk][:], func=AF.Copy),
                         reads=[R_ps[bk]], writes=[R_qn])
                    pfree(bk)
                    b1 = palloc(); b2 = palloc()
                    P.group("pe", [MM(ps[b1][0:64, :], wv4[:, kc, hh, 128:192], cqn[:, kc, :], kc == 0, kc == 3)
                                   for kc in range(4)], reads=[Rw, R_cqn], writes=[R_ps[b1]])
                    P.group("pe", [MM(ps[b2][0:64, :], wrv[:, kc, hh, :], cqn[:, kc, :], kc == 0, kc == 3)
                                   for kc in range(4)], reads=[R_wrot, R_cqn], writes=[R_ps[b2]])
                    rope_evac(b1, b2, qr[0:64, hh, :], R_qr)
                    pfree(b1); pfree(b2)
                if SL < 2:
                    continue
                for hh in range(4):
                    h = hg * 4 + hh
                    bo = palloc(); bs = palloc()
                    LAG = 4
                    for kk in range(16 + LAG):
                        if kk < 16:
                            kc = kk
                            bk = palloc()
                            pb = kc % 5
                            P.group("pe", [MM(ps[bk][:], KT[:, h, kc * 128:(kc + 1) * 128], qn[:, hh, :], True, False),
                                           MM(ps[bk][:], krT[0:64, kc * 128:(kc + 1) * 128], qr[0:64, hh, :], False, True)],
                                    reads=[R_KT, R_kr, R_qn, R_qr], writes=[R_ps[bk]])
                            P.op("act", lambda e, bk=bk, pb=pb: e.activation(out=PT[pb][:], in_=ps[bk][:], func=AF.Exp,
                                                                             scale=SC_MLA),
                                 reads=[R_ps[bk]], writes=[R_PT[pb]])
                            pfree(bk)
                        if kk >= LAG:
                            kc = kk - LAG
                            pb = kc % 5
                            P.group("pe", [MM(ps[bo][:], Vt[:, kc, h * 128:(h + 1) * 128], PT[pb][:], kc == 0, kc == 15),
                                           MM(ps[bs][:], ones[:], PT[pb][:], kc == 0, kc == 15)],
                                    reads=[R_V, R_PT[pb], R_ones], writes=[R_ps[bo], R_ps[bs]])
                    P.op("dve", lambda e, bs=bs: e.reciprocal(out=T[3][:, 0:TT], in_=ps[bs][:]),
                         reads=[R_ps[bs]], writes=[R_T[3]])
                    P.op("dve", lambda e, bo=bo, h=h: e.tensor_tensor(out=attnT[:, h, :], in0=ps[bo][:], in1=T[3][:, 0:TT],
                                                                      op=ALU.mult),
                         reads=[R_ps[bo], R_T[3]], writes=[R_attn])
                    pfree(bo); pfree(bs)
            if SL < 3:
                continue
            for g in range(4):
                wcv, Rcv = wload(w_in, 0, 16, C_CV + g * 256, 256)
                wcc, Rcc = wload(w_in, 0, 16, C_CC + g * 256, 256)
                wcb, Rcb = wload(w_in, 0, 16, C_CB + g * 256, 256)
                for cc in range(2):
                    i = g * 2 + cc
                    ub = T[4 + (i % 2)]; R_ub = R_T[4 + (i % 2)]
                    bv = palloc(); bc = palloc()
                    P.group("pe", [MM(ps[bv][:], wcv[:, kc, cc * 128:(cc + 1) * 128], hT[:, kc, :], kc == 0, kc == 15)
                                   for kc in range(16)], reads=[Rcv, R_hT], writes=[R_ps[bv]])
                    P.group("pe", [MM(ps[bc][:], wcc[:, kc, cc * 128:(cc + 1) * 128], hT[:, kc, :], kc == 0, kc == 15)
                                   for kc in range(16)], reads=[Rcc, R_hT], writes=[R_ps[bc]])
                    P.op("act", lambda e, bv=bv: e.activation(out=T[0][:, 0:TT], in_=ps[bv][:], func=AF.Copy),
                         reads=[R_ps[bv]], writes=[R_T[0]])
                    P.op("dve", lambda e, bc=bc, ub=ub: e.tensor_tensor(out=ub[:, 1:TT + 1], in0=ps[bc][:], in1=T[0][:, 0:TT],
                                                                        op=ALU.mult),
                         reads=[R_ps[bc], R_T[0]], writes=[R_ub])
                    pfree(bv); pfree(bc)
                    if t == 0:
                        hv = palloc(); hc = palloc()
                        P.group("pe", [MM(ps[hv][:, 0:4], wcv[:, kc, cc * 128:(cc + 1) * 128], hTh[:, kc, :], kc == 0, kc == 15)
                                       for kc in range(16)], reads=[Rcv, R_hTh], writes=[R_ps[hv]])
                        P.group("pe", [MM(ps[hc][:, 0:4], wcc[:, kc, cc * 128:(cc + 1) * 128], hTh[:, kc, :], kc == 0, kc == 15)
                                       for kc in range(16)], reads=[Rcc, R_hTh], writes=[R_ps[hc]])
                        P.op("act", lambda e, hv=hv: e.activation(out=T[1][:, 0:4], in_=ps[hv][:, 0:4], func=AF.Copy),
                             reads=[R_ps[hv]], writes=[R_T[1]])
                        P.op("dve", lambda e, hc=hc, i=i: e.tensor_tensor(out=uh[:, i, :], in0=ps[hc][:, 0:4], in1=T[1][:, 0:4],
                                                                          op=ALU.mult),
                             reads=[R_ps[hc], R_T[1]], writes=[R_uh])
                        pfree(hv); pfree(hc)
                    P.op("act", lambda e, ub=ub, i=i: e.activation(out=ub[:, 0:1], in_=ucar[:, i:i + 1], func=AF.Copy),
                         reads=[R_ucar], writes=[R_ub])
                    rsrc = uh[:, i, t:t + 1] if t < NT - 1 else zero1[:, 0:1]
                    P.op("act", lambda e, ub=ub, rsrc=rsrc: e.activation(out=ub[:, TT + 1:TT + 2], in_=rsrc, func=AF.Copy),
                         reads=[R_uh], writes=[R_ub])
                    P.op("act", lambda e, ub=ub, i=i: e.activation(out=ucar[:, i:i + 1], in_=ub[:, TT:TT + 1], func=AF.Copy),
                         reads=[R_ub], writes=[R_ucar])
                    P.op("dve", lambda e, ub=ub, i=i: e.tensor_scalar(out=T[1][:, 0:TT], in0=ub[:, 0:TT],
                                                                      scalar1=pT[:, P_CW + i:P_CW + i + 1], scalar2=None,
                                                                      op0=ALU.mult), reads=[R_ub, R_pT], writes=[R_T[1]])
                    P.op("dve", lambda e, ub=ub, i=i: e.scalar_tensor_tensor(out=T[1][:, 0:TT], in0=ub[:, 1:TT + 1],
                                                                             scalar=pT[:, P_CW + 8 + i:P_CW + 9 + i],
                                                                             in1=T[1][:, 0:TT], op0=ALU.mult, op1=ALU.add),
                         reads=[R_ub, R_pT], writes=[R_T[1]])
                    P.op("dve", lambda e, ub=ub, i=i: e.scalar_tensor_tensor(out=T[1][:, 0:TT], in0=ub[:, 2:TT + 2],
                                                                             scalar=pT[:, P_CW + 16 + i:P_CW + 17 + i],
                                                                             in1=T[1][:, 0:TT], op0=ALU.mult, op1=ALU.add),
                         reads=[R_ub, R_pT], writes=[R_T[1]])
                    bb = palloc()
                    P.group("pe", [MM(ps[bb][:], wcb[:, kc, cc * 128:(cc + 1) * 128], hT[:, kc, :], kc == 0, kc == 15)
                                   for kc in range(16)], reads=[Rcb, R_hT], writes=[R_ps[bb]])
                    P.op("dve", lambda e, bb=bb, i=i: e.tensor_tensor(out=convT[:, i, :], in0=ps[bb][:], in1=T[1][:, 0:TT],
                                                                      op=ALU.mult),
                         reads=[R_ps[bb], R_T[1]], writes=[R_conv])
                    pfree(bb)
            if SL < 4:
                continue
            qx = [qn, qr]
            R_qx = [R_qn, R_qr]
            for hx in range(4):
                wv, Rw = wload(w_in, 0, 16, C_QX + hx * 256, 256)
                for dc in range(2):
                    bk = palloc()
                    P.group("pe", [MM(ps[bk][:], wv[:, kc, dc * 128:(dc + 1) * 128], hT[:, kc, :], kc == 0, kc == 15)
                                   for kc in range(16)], reads=[Rw, R_hT], writes=[R_ps[bk]])
                    P.op("act", lambda e, bk=bk, hx=hx, dc=dc: e.activation(out=qx[dc][:, hx, :], in_=ps[bk][:], func=AF.Copy),
                         reads=[R_ps[bk]], writes=[R_qx[dc]])
                    pfree(bk)
            for hx in range(4):
                for mc in range(2):
                    bk = palloc()
                    P.group("pe", [MM(ps[bk][:], KmT[:, hx * 2 + dc, mc * 128:(mc + 1) * 128], qx[dc][:, hx, :], dc == 0, dc == 1)
                                   for dc in range(2)], reads=[R_Km, R_qn, R_qr], writes=[R_ps[bk]])
                    P.op("act", lambda e, bk=bk, mc=mc: e.activation(out=PT[mc][:], in_=ps[bk][:], func=AF.Exp, scale=SC_MEM),
                         reads=[R_ps[bk]], writes=[R_PT[mc]])
                    pfree(bk)
                bs = palloc()
                P.group("pe", [MM(ps[bs][:], ones[:], PT[mc][:], mc == 0, mc == 1) for mc in range(2)],
                        reads=[R_PT[0], R_PT[1], R_ones], writes=[R_ps[bs]])
                P.op("dve", lambda e, bs=bs: e.reciprocal(out=T[3][:, 0:TT], in_=ps[bs][:]), reads=[R_ps[bs]], writes=[R_T[3]])
                pfree(bs)
                for dv in range(2):
                    bo = palloc()
                    P.group("pe", [MM(ps[bo][:], Vm[:, mc, hx * 256 + dv * 128: hx * 256 + (dv + 1) * 128], PT[mc][:],
                                      mc == 0, mc == 1) for mc in range(2)], reads=[R_Vm, R_PT[0], R_PT[1]],
                            writes=[R_ps[bo]])
                    P.op("dve", lambda e, bo=bo, hx=hx, dv=dv: e.tensor_tensor(out=memT[:, hx * 2 + dv, :], in0=ps[bo][:],
                                                                               in1=T[3][:, 0:TT], op=ALU.mult),
                         reads=[R_ps[bo], R_T[3]], writes=[R_memo])
                    pfree(bo)
            if SL < 5:
                continue
            branches = [(w_o_mla, attnT, R_attn), (w_out_conv, convT, R_conv), (w_o_mem, memT, R_memo)]
            macc = [T[4], T[5]]
            R_macc = [R_T[4], R_T[5]]
            for jp in range(8):
                for br in range(3):
                    wsrc, actv, R_actv = branches[br]
                    wp, Rp = wload(wsrc, 0, 8, jp * 256, 256)
                    wg, Rg = wload(w_in, 0, 16, C_G + br * 2048 + jp * 256, 256)
                    for jj in range(2):
                        j = jp * 2 + jj
                        by = palloc(); bg = palloc()
                        P.group("pe", [MM(ps[by][:], wp[:, kc, jj * 128:(jj + 1) * 128], actv[:, kc, :], kc == 0, kc == 7)
                                       for kc in range(8)], reads=[Rp, R_actv], writes=[R_ps[by]])
                        P.group("pe", [MM(ps[bg][:], wg[:, kc, jj * 128:(jj + 1) * 128], hT[:, kc, :], kc == 0, kc == 15)
                                       for kc in range(16)], reads=[Rg, R_hT], writes=[R_ps[bg]])
                        gcol = P_GB + br * 16 + j
                        P.op("act", lambda e, bg=bg, gcol=gcol: e.activation(out=T[0][:, 0:TT], in_=ps[bg][:], func=AF.Sigmoid,
                                                                             bias=pT[:, gcol:gcol + 1]),
                             reads=[R_ps[bg], R_pT], writes=[R_T[0]])
                        if br == 0:
                            P.op("dve", lambda e, by=by, jj=jj: e.tensor_tensor(out=macc[jj][:, 0:TT], in0=ps[by][:],
                                                                                in1=T[0][:, 0:TT], op=ALU.mult),
                                 reads=[R_ps[by], R_T[0]], writes=[R_macc[jj]])
                        else:
                            P.op("dve", lambda e, by=by: e.tensor_tensor(out=T[1][:, 0:TT], in0=ps[by][:], in1=T[0][:, 0:TT],
                                                                         op=ALU.mult),
                                 reads=[R_ps[by], R_T[0]], writes=[R_T[1]])
                            if br == 1:
                                P.op("dve", lambda e, jj=jj: e.tensor_tensor(out=macc[jj][:, 0:TT], in0=macc[jj][:, 0:TT],
                                                                             in1=T[1][:, 0:TT], op=ALU.add),
                                     reads=[R_T[1]], writes=[R_macc[jj]])
                            else:
                                P.op("dve", lambda e, jj=jj, j=j: e.tensor_tensor(out=mergedT[:, j, :], in0=macc[jj][:, 0:TT],
                                                                                  in1=T[1][:, 0:TT], op=ALU.add),
                                     reads=[R_T[1], R_macc[jj]], writes=R_mg)
                        pfree(by); pfree(bg)
            if SL < 6:
                continue
            load_xt(x, t, sem_xr)
            for cg in range(4):
                wA, RA = wload(w_o, 0, 8, cg * 512, 512)
                wB, RB = wload(w_o, 1024, 8, cg * 512, 512)
                for jt in range(4):
                    bk = palloc()
                    P.group("pe", [MM(ps[bk][:], mergedT[:, kc, jt * 128:(jt + 1) * 128],
                                      (wA[:, kc, :] if kc < 8 else wB[:, kc - 8, :]), kc == 0, kc == 15)
                                   for kc in range(16)], reads=[RA, RB] + R_mg, writes=[R_ps[bk]])
                    P.op("dve", lambda e, bk=bk, jt=jt, cg=cg: e.tensor_tensor(out=xt[:, jt, cg * 512:(cg + 1) * 512],
                                                                               in0=ps[bk][:],
                                                                               in1=xt[:, jt, cg * 512:(cg + 1) * 512], op=ALU.add),
                         reads=[R_ps[bk]], writes=R_xt)
                    pfree(bk)
            dst = x1s[t * TT:(t + 1) * TT, :].rearrange("(j p) d -> p j d", p=128)
            P.dma("sp", lambda e, dst=dst: e.dma_start(out=dst, in_=xt), sem_st, reads=R_xt)

        P.dma("sp", lambda e: e.dma_start(out=gfin, in_=fin.partition_broadcast(128)), sem_misc,
              writes=[R_gfin, R_KT, R_V, R_kr])
        XB = [big[:, 26624:30720].bitcast(F32), big[:, 30720:34816].bitcast(F32),
              qq[:, :, :].rearrange("p a b -> p (a b)").bitcast(F32), KV8[:, :].bitcast(F32)]
        R_XB = [R_b0, R_b1, R_qn, R_qr, R_Km, R_Vm]
        XBUF = [[xt[:, j, :] for j in range(4)], XB]
        R_XBUF = [R_xt, R_XB]
        sem_xb = P.new_dsem()
        SEM_L = [sem_x, sem_xb]

        def load_x1(t, first_extra=()):
            bsel = t % 2
            for j in range(4):
                src = x1s[t * TT + j * 128: t * TT + (j + 1) * 128, :]
                P.dma("sp", lambda e, src=src, dstp=XBUF[bsel][j]: e.dma_start(out=dstp, in_=src), SEM_L[bsel],
                      writes=list(R_XBUF[bsel]) + list(first_extra))

        P.wait_all("sp", [(("dma", sem_st), P.dsem[sem_st])])
        if NT_2 > 0:
            load_x1(0)
            load_halo(x1s)
            norm_T(XBUF[0], R_XBUF[0], 4, P_FFN, hT, R_hT, TT)
            norm_T([xh[:, 0, :]], R_xh, 1, P_FFN, hTh, R_hTh, 4)
        for t in range(NT_2):
            cur = XBUF[t % 2]
            R_cur = R_XBUF[t % 2]
            for g in range(NFC // 2):
                wa, Ra = wload(w_up, 0, 16, g * 256, 256)
                wb, Rb = wload(w_up, 0, 16, DFF + g * 256, 256)
                for cc in range(2):
                    i = g * 2 + cc
                    outs = []
                    for ab, (wv, Rw) in enumerate(((wa, Ra), (wb, Rb))):
                        idx = ab * NFC + i
                        ub = T[ab]; R_ub = R_T[ab]
                        cv = T[2 + ab]; R_cv = R_T[2 + ab]
                        bk = palloc()
                        P.group("pe", [MM(ps[bk][:], wv[:, kc, cc * 128:(cc + 1) * 128], hT[:, kc, :], kc == 0, kc == 15)
                                       for kc in range(16)], reads=[Rw, R_hT], writes=[R_ps[bk]])
                        P.op("act", lambda e, bk=bk, ub=ub: e.activation(out=ub[:, 1:TT + 1], in_=ps[bk][:], func=AF.Copy),
                             reads=[R_ps[bk]], writes=[R_ub])
                        pfree(bk)
                        if t == 0:
                            hb_ = palloc()
                            P.group("pe", [MM(ps[hb_][:, 0:4], wv[:, kc, cc * 128:(cc + 1) * 128], hTh[:, kc, :], kc == 0, kc == 15)
                                           for kc in range(16)], reads=[Rw, R_hTh], writes=[R_ps[hb_]])
                            P.op("act", lambda e, hb_=hb_, idx=idx: e.activation(out=fh[:, idx, :], in_=ps[hb_][:, 0:4], func=AF.Copy),
                                 reads=[R_ps[hb_]], writes=[R_fh])
                            pfree(hb_)
                        P.op("act", lambda e, ub=ub, idx=idx: e.activation(out=ub[:, 0:1], in_=fcar[:, idx:idx + 1], func=AF.Copy),
                             reads=[R_fcar], writes=[R_ub])
                        rsrc = fh[:, idx, t:t + 1] if t < NT - 1 else zero1[:, 0:1]
                        P.op("act", lambda e, ub=ub, rsrc=rsrc: e.activation(out=ub[:, TT + 1:TT + 2], in_=rsrc, func=AF.Copy),
                             reads=[R_fh], writes=[R_ub])
                        P.op("act", lambda e, ub=ub, idx=idx: e.activation(out=fcar[:, idx:idx + 1], in_=ub[:, TT:TT + 1], func=AF.Copy),
                             reads=[R_ub], writes=[R_fcar])
                        c0 = P_FCW + ab * NFC + i
                        P.op("dve", lambda e, ub=ub, cv=cv, c0=c0: e.tensor_scalar(out=cv[:, 0:TT], in0=ub[:, 0:TT],
                                                                                   scalar1=pT[:, c0:c0 + 1], scalar2=None,
                                                                                   op0=ALU.mult), reads=[R_ub, R_pT], writes=[R_cv])
                        P.op("dve", lambda e, ub=ub, cv=cv, c0=c0: e.scalar_tensor_tensor(out=cv[:, 0:TT], in0=ub[:, 1:TT + 1],
                                                                                          scalar=pT[:, c0 + 88:c0 + 89],
                                                                                          in1=cv[:, 0:TT], op0=ALU.mult, op1=ALU.add),
                             reads=[R_ub, R_pT], writes=[R_cv])
                        P.op("dve", lambda e, ub=ub, cv=cv, c0=c0: e.scalar_tensor_tensor(out=cv[:, 0:TT], in0=ub[:, 2:TT + 2],
                                                                                          scalar=pT[:, c0 + 176:c0 + 177],
                                                                                          in1=cv[:, 0:TT], op0=ALU.mult, op1=ALU.add),
                             reads=[R_ub, R_pT], writes=[R_cv])
                    P.op("act", lambda e: e.activation(out=T[4][:, 0:TT], in_=T[2][:, 0:TT], func=AF.Silu),
                         reads=[R_T[2]], writes=[R_T[4]])
                    P.op("dve", lambda e, i=i: e.tensor_tensor(out=actT[:, i, :], in0=T[4][:, 0:TT], in1=T[3][:, 0:TT], op=ALU.mult),
                         reads=[R_T[4], R_T[3]], writes=[R_act[i], R_KT, R_V])
            KG = [(0, 8), (8, 8), (16, 8), (24, 8), (32, 8), (40, 4)]
            for cgp in range(4):
                bks = [palloc() for _ in range(4)]
                for (k0, kn) in KG:
                    wv, Rw = wload(w_down, k0 * 128, kn, cgp * 512, 512)
                    for jt in range(4):
                        P.group("pe", [MM(ps[bks[jt]][:], actT[:, k0 + kk, jt * 128:(jt + 1) * 128], wv[:, kk, :],
                                          (k0 + kk) == 0, (k0 + kk) == NFC - 1) for kk in range(kn)],
                                reads=[Rw] + [R_act[k0 + kk] for kk in range(kn)], writes=[R_ps[bks[jt]]])
                for jt in range(4):
                    P.op("dve", lambda e, jt=jt, cgp=cgp, bk=bks[jt], cur=cur: e.tensor_tensor(
                        out=cur[jt][:, cgp * 512:(cgp + 1) * 512], in0=ps[bk][:], in1=cur[jt][:, cgp * 512:(cgp + 1) * 512],
                        op=ALU.add), reads=[R_ps[bks[jt]]], writes=R_cur)
                    pfree(bks[jt])
                if cgp == 0 and t + 1 < NT_2:
                    load_x1(t + 1, first_extra=([R_V, R_kr] if t == 0 else ()))
                    norm_T(XBUF[(t + 1) % 2], R_XBUF[(t + 1) % 2], 4, P_FFN, hT, R_hT, TT)
            for j in range(4):
                P.op("act", lambda e, j=j, cur=cur: e.activation(out=junk[:], in_=cur[j], func=AF.Square,
                                                         accum_out=small[:, j:j + 1]), reads=R_cur, writes=[R_junk, R_small])
            P.op("act", lambda e: e.activation(out=small[:, 5:9], in_=small[:, 0:4], func=AF.Ln, scale=1.0 / D, bias=eps_ap),
                 writes=[R_small])
            P.op("act", lambda e: e.activation(out=small[:, 5:9], in_=small[:, 5:9], func=AF.Exp, scale=-0.5), writes=[R_small])
            for j in range(4):
                P.op("dve", lambda e, j=j, cur=cur: e.scalar_tensor_tensor(out=cur[j], in0=cur[j], scalar=small[:, 5 + j:6 + j],
                                                                   in1=gfin, op0=ALU.mult, op1=ALU.mult),
                     reads=[R_small, R_gfin], writes=R_cur)
            for j in range(4):
                dst = y[t * TT + j * 128: t * TT + (j + 1) * 128, :]
                P.dma("sp", lambda e, dst=dst, j=j, cur=cur: e.dma_start(out=dst, in_=cur[j]), sem_st, reads=R_cur)
        P.wait_all("sp", [(("dma", i), P.dsem[i]) for i in range(len(P.dsem))])
        P.q["sp"].append(("op", lambda e: e.nop(), False))
        P.run(st)
    return nc


_NC_CACHE = {}


def kernel(**inputs):
    f32 = np.float32
    if "nc" not in _NC_CACHE:
        _NC_CACHE["nc"] = build_nc()
    nc = _NC_CACHE["nc"]

    def a(name):
        return np.ascontiguousarray(np.asarray(inputs[name]))

    x = a("x").astype(f32, copy=False)
    mem = a("mem").astype(f32, copy=False)
    pos = a("positions").astype(np.int32, copy=False)
    B = x.shape[0]
    prm = np.concatenate([a(n).astype(f32, copy=False).reshape(-1) for n in
                          ("mix_norm", "ffn_norm", "mem_norm", "q_norm", "kv_norm", "gate_bias", "conv_w", "ffn_conv_w")]
                         ).reshape(NPRM, 128)
    cst = np.zeros((128, 130), f32)
    cst[:, :128] = np.eye(128, dtype=f32)
    invf = np.power(f32(10000.0), -np.arange(0, 64, 2, dtype=f32) / f32(64)).astype(f32)
    cst[0:32, 128] = invf
    cst[32:64, 128] = invf
    shared = {
        "w_in": a("w_in")[0], "w_uq": a("w_uq")[0], "w_ukv": a("w_ukv")[0], "w_o_mla": a("w_o_mla")[0],
        "w_out_conv": a("w_out_conv")[0], "w_mem_kv": a("w_mem_kv")[0], "w_o_mem": a("w_o_mem")[0],
        "w_o": a("w_o")[0], "w_up": a("w_up")[0], "w_down": a("w_down")[0],
        "prm": prm, "final_norm": a("final_norm").reshape(1, D), "cst": cst,
    }
    in_maps = []
    for c in range(B):
        m = dict(shared)
        m["x"] = x[c]
        m["mem"] = mem[c]
        m["positions"] = pos[c].reshape(1, S)
        in_maps.append(m)
    res = run_bass_kernel_spmd(nc, in_maps, core_ids=list(range(B)))
    return np.stack([r["y"] for r in res.results], axis=0).astype(f32, copy=False)
```
